# Optimizing a Trainium2 kernel written in Bass

```python
import jax
import jax.numpy as jnp
from jax import lax
import numpy as np

D_MODEL = 1024
BATCH = 8
SEQ = 2048
DEPTH = 2
DEC_BATCH = 128
DEC_SEQ = 1
PAST_LEN = 16384
PAGE_SIZE = 128

N_AB_LAYERS = (DEPTH + 1) // 2
N_C_LAYERS = DEPTH // 2
POOL_WINDOWS = (2, 4, 8, 16)
POOL_GROUPS = len(POOL_WINDOWS)
POOL_GROUP_WIDTH = D_MODEL // 8
POOL_WIDTH = POOL_GROUPS * POOL_GROUP_WIDTH
POOL_BUF = max(POOL_WINDOWS) - 1
RWKV_HEAD = 64
RWKV_WIDTH = D_MODEL // 2
RWKV_HEADS = RWKV_WIDTH // RWKV_HEAD
W_LORA = 64
A_LORA = 64
G_LORA = 128
RWKV_IN = 3 * RWKV_WIDTH + W_LORA + A_LORA + G_LORA
AB_IN = POOL_WIDTH + RWKV_IN
AB_OUT = POOL_WIDTH + RWKV_WIDTH
CHUNK = 128
SGU_WIDTH = 2 * D_MODEL
SGU_GROUPS = 4
MEM_LEN = 256
XA_HEADS = 4
XA_HEAD_DIM = D_MODEL // XA_HEADS
D_FF = 4 * D_MODEL
RMS_EPS = 1e-5
LN_EPS = 1e-5
GN_EPS = RWKV_HEAD * 1e-5
L2_EPS = 1e-12

kernel_name = 'hybrid_pool_rwkv7_gmlp_memxattn_step'


def rmsnorm(x, g):
    xf = x.astype(jnp.float32)
    y = xf * lax.rsqrt(jnp.mean(xf * xf, axis=-1, keepdims=True) + RMS_EPS)
    return (y * g).astype(x.dtype)


def layernorm(x, g, b):
    xf = x.astype(jnp.float32)
    m = jnp.mean(xf, axis=-1, keepdims=True)
    var = jnp.mean(jnp.square(xf - m), axis=-1, keepdims=True)
    return ((xf - m) * lax.rsqrt(var + LN_EPS) * g + b).astype(x.dtype)


def pool_mix(z, buf, pos0, w_group, scale):
    b, l, _ = z.shape
    full = jnp.concatenate([buf.astype(z.dtype), z], axis=1)
    cs = jnp.cumsum(full.astype(jnp.float32), axis=1)
    cs = jnp.concatenate([jnp.zeros((b, 1, POOL_WIDTH), jnp.float32), cs], axis=1)
    pos = pos0 + jnp.arange(l, dtype=jnp.int32)
    hi = cs[:, POOL_BUF + 1:POOL_BUF + 1 + l]
    pooled = []
    for gi, win in enumerate(POOL_WINDOWS):
        cols = slice(gi * POOL_GROUP_WIDTH, (gi + 1) * POOL_GROUP_WIDTH)
        lo = cs[:, POOL_BUF + 1 - win:POOL_BUF + 1 - win + l, cols]
        count = jnp.minimum(pos + 1, win).astype(jnp.float32)[None, :, None]
        pooled.append((hi[..., cols] - lo) / count)
    d = jnp.concatenate(pooled, axis=-1) - z.astype(jnp.float32)
    d = d.reshape(b, l, POOL_GROUPS, POOL_GROUP_WIDTH)
    y = jnp.einsum('blgc,gcd->blgd', d, w_group.astype(jnp.float32)).reshape(b, l, POOL_WIDTH) * scale
    return y.astype(z.dtype), full[:, -POOL_BUF:]


def rwkv7_mix(z, prev, s0, mu, w0, w2, a0, a2, g2, k_k, k_a, r_k, gn_g, gn_b):
    b, l, _ = z.shape
    zf = z.astype(jnp.float32)
    shifted = jnp.concatenate([prev.astype(jnp.float32)[:, None], zf[:, :-1]], axis=1)
    zs = zf + (shifted - zf) * mu
    w = RWKV_WIDTH
    r, k, v, wl, al, gl = jnp.split(zs, [w, 2 * w, 3 * w, 3 * w + W_LORA, 3 * w + W_LORA + A_LORA], axis=-1)
    wlog = -jax.nn.softplus(-(w0 + jnp.tanh(wl) @ w2)) - 0.5
    decay = jnp.exp(-jnp.exp(wlog))
    a = jax.nn.sigmoid(a0 + al @ a2)
    g = jax.nn.sigmoid(gl) @ g2
    kk = k * k_k
    k = k * (1.0 + (a - 1.0) * k_a)

    def heads(t):
        return t.reshape(b, l, RWKV_HEADS, RWKV_HEAD)

    rh, kh, vh, dh, ah, kkh = heads(r), heads(k), heads(v), heads(decay), heads(a), heads(kk)
    kkh = kkh / jnp.maximum(jnp.sqrt(jnp.sum(kkh * kkh, axis=-1, keepdims=True)), L2_EPS)

    def step(s, inp):
        r_t, k_t, v_t, d_t, kk_t, a_t = inp
        sa = jnp.einsum('bhvk,bhk->bhv', s, -kk_t)
        s = (s * d_t[:, :, None, :] + sa[..., None] * (kk_t * a_t)[:, :, None, :]
             + v_t[..., None] * k_t[:, :, None, :])
        return s, jnp.einsum('bhvk,bhk->bhv', s, r_t)

    xs = tuple(jnp.moveaxis(t, 1, 0) for t in (rh, kh, vh, dh, kkh, ah))
    s_last, ys = lax.scan(step, s0.astype(jnp.float32), xs)
    y = jnp.moveaxis(ys, 0, 1)
    m = jnp.mean(y, axis=-1, keepdims=True)
    var = jnp.mean(jnp.square(y - m), axis=-1, keepdims=True)
    yn = ((y - m) * lax.rsqrt(var + GN_EPS)).reshape(b, l, RWKV_WIDTH) * gn_g + gn_b
    bonus = (jnp.sum(rh * kh * r_k, axis=-1, keepdims=True) * vh).reshape(b, l, RWKV_WIDTH)
    out = (yn + bonus) * g
    return out.astype(z.dtype), z[:, -1], s_last.astype(s0.dtype)


def chunk_spatial(v, w_s, b_s):
    b, l, e = v.shape
    blk = min(l, CHUNK)
    n = -(-l // blk)
    lp = n * blk
    vp = jnp.pad(v, ((0, 0), (0, lp - l), (0, 0))).reshape(b, n, blk, SGU_GROUPS, e // SGU_GROUPS)
    wm = jnp.tril(w_s[:, :blk, :blk])
    out = jnp.einsum('gij,bcjgd->bcigd', wm, vp) + jnp.transpose(b_s[:, :blk])[None, None, :, :, None]
    return out.reshape(b, lp, e)[:, :l]


def sgu_mix(h, w_in, ln_g, ln_b, w_s, b_s, w_out):
    zc = jax.nn.gelu(h @ w_in, approximate=False)
    u, v = jnp.split(zc, 2, axis=-1)
    v = layernorm(v, ln_g, ln_b)
    return (u * chunk_spatial(v, w_s, b_s)) @ w_out, v


def memory_kv(mem, g, w_k, w_v):
    b, m, _ = mem.shape
    mn = rmsnorm(mem, g)
    return ((mn @ w_k).reshape(b, m, XA_HEADS, XA_HEAD_DIM),
            (mn @ w_v).reshape(b, m, XA_HEADS, XA_HEAD_DIM))


def cross_attn(h, k, v, w_q, w_o):
    b, l, _ = h.shape
    q = (h @ w_q).reshape(b, l, XA_HEADS, XA_HEAD_DIM)
    s = jnp.einsum('blhd,bmhd->bhlm', q, k.astype(q.dtype)).astype(jnp.float32) * (XA_HEAD_DIM ** -0.5)
    p = jax.nn.softmax(s, axis=-1).astype(q.dtype)
    o = jnp.einsum('bhlm,bmhd->blhd', p, v.astype(q.dtype)).reshape(b, l, D_MODEL)
    return o @ w_o


def sq_relu_mlp(h, w_up, w_down):
    return jnp.square(jax.nn.relu(h @ w_up)) @ w_down


def setup_inputs(seed: int = 0) -> dict:
    key = jax.random.key(seed)
    ks = iter(jax.random.split(key, 48))
    f32 = jnp.float32

    def nrm(shape, scale=1.0):
        return jax.random.normal(next(ks), shape, f32) * scale

    def gain(shape):
        return 1.0 + 0.02 * jax.random.normal(next(ks), shape, f32)

    d = D_MODEL
    return {
        'x_prompt': nrm((BATCH, SEQ, d)),
        'x_sample': nrm((DEC_BATCH, DEC_SEQ, d)),
        'mem_prompt': nrm((BATCH, MEM_LEN, d)),
        'cache_mem_k': nrm((DEPTH, DEC_BATCH, MEM_LEN, XA_HEADS, XA_HEAD_DIM)),
        'cache_mem_v': nrm((DEPTH, DEC_BATCH, MEM_LEN, XA_HEADS, XA_HEAD_DIM)),
        'state_pool': nrm((N_AB_LAYERS, DEC_BATCH, POOL_BUF, POOL_WIDTH)),
        'state_shift': nrm((N_AB_LAYERS, DEC_BATCH, RWKV_IN)),
        'state_wkv': nrm((N_AB_LAYERS, DEC_BATCH, RWKV_HEADS, RWKV_HEAD, RWKV_HEAD)),
        'norm_mix_g': gain((DEPTH, d)),
        'norm_xa_g': gain((DEPTH, d)),
        'norm_mem_g': gain((DEPTH, d)),
        'norm_ffn_g': gain((DEPTH, d)),
        'norm_final_g': gain((d,)),
        'w_in_ab': nrm((N_AB_LAYERS, d, AB_IN), d ** -0.5),
        'w_out_ab': nrm((N_AB_LAYERS, AB_OUT, d), AB_OUT ** -0.5),
        'pool_w': nrm((N_AB_LAYERS, POOL_GROUPS, POOL_GROUP_WIDTH, POOL_GROUP_WIDTH), POOL_GROUP_WIDTH ** -0.5),
        'pool_scale': gain((N_AB_LAYERS, POOL_WIDTH)),
        'rwkv_mu': jax.random.uniform(next(ks), (N_AB_LAYERS, RWKV_IN), f32),
        'rwkv_w0': nrm((N_AB_LAYERS, RWKV_WIDTH), 0.5),
        'rwkv_w2': nrm((N_AB_LAYERS, W_LORA, RWKV_WIDTH), 0.5 * W_LORA ** -0.5),
        'rwkv_a0': nrm((N_AB_LAYERS, RWKV_WIDTH), 0.1),
        'rwkv_a2': nrm((N_AB_LAYERS, A_LORA, RWKV_WIDTH), 0.5 * A_LORA ** -0.5),
        'rwkv_g2': nrm((N_AB_LAYERS, G_LORA, RWKV_WIDTH), G_LORA ** -0.5),
        'rwkv_k_k': 0.85 + nrm((N_AB_LAYERS, RWKV_WIDTH), 0.02),
        'rwkv_k_a': gain((N_AB_LAYERS, RWKV_WIDTH)),
        'rwkv_r_k': nrm((N_AB_LAYERS, RWKV_HEADS, RWKV_HEAD), 0.1),
        'rwkv_gn_g': gain((N_AB_LAYERS, RWKV_WIDTH)),
        'rwkv_gn_b': nrm((N_AB_LAYERS, RWKV_WIDTH), 0.02),
        'w_in_c': nrm((N_C_LAYERS, d, 2 * SGU_WIDTH), d ** -0.5),
        'sgu_ln_g': gain((N_C_LAYERS, SGU_WIDTH)),
        'sgu_ln_b': nrm((N_C_LAYERS, SGU_WIDTH), 0.02),
        'sgu_w_s': nrm((N_C_LAYERS, SGU_GROUPS, CHUNK, CHUNK), CHUNK ** -0.5),
        'sgu_b_s': gain((N_C_LAYERS, SGU_GROUPS, CHUNK)),
        'w_out_c': nrm((N_C_LAYERS, SGU_WIDTH, d), SGU_WIDTH ** -0.5),
        'w_xq': nrm((DEPTH, d, d), d ** -0.5),
        'w_xk': nrm((DEPTH, d, d), d ** -0.5),
        'w_xv': nrm((DEPTH, d, d), d ** -0.5),
        'w_xo': nrm((DEPTH, d, d), d ** -0.5),
        'w_ff_up': nrm((DEPTH, d, D_FF), d ** -0.5),
        'w_ff_down': nrm((DEPTH, D_FF, d), D_FF ** -0.5),
    }


def reference(x_prompt, x_sample, mem_prompt, cache_mem_k, cache_mem_v, state_pool, state_shift, state_wkv,
              norm_mix_g, norm_xa_g, norm_mem_g, norm_ffn_g, norm_final_g,
              w_in_ab, w_out_ab, pool_w, pool_scale,
              rwkv_mu, rwkv_w0, rwkv_w2, rwkv_a0, rwkv_a2, rwkv_g2, rwkv_k_k, rwkv_k_a, rwkv_r_k,
              rwkv_gn_g, rwkv_gn_b,
              w_in_c, sgu_ln_g, sgu_ln_b, sgu_w_s, sgu_b_s, w_out_c,
              w_xq, w_xk, w_xv, w_xo, w_ff_up, w_ff_down):

    def trunk(x, pos0, mem_k, mem_v, pool_in, shift_in, wkv_in):
        pool_out, shift_out, wkv_out, sgu_v_out = [], [], [], []
        for l in range(DEPTH):
            j = l // 2
            h = rmsnorm(x, norm_mix_g[l])
            if l % 2 == 0:
                z = h @ w_in_ab[j]
                y_pool, pool_new = pool_mix(z[..., :POOL_WIDTH], pool_in[j], pos0, pool_w[j], pool_scale[j])
                y_rwkv, shift_new, wkv_new = rwkv7_mix(
                    z[..., POOL_WIDTH:], shift_in[j], wkv_in[j], rwkv_mu[j], rwkv_w0[j], rwkv_w2[j],
                    rwkv_a0[j], rwkv_a2[j], rwkv_g2[j], rwkv_k_k[j], rwkv_k_a[j], rwkv_r_k[j],
                    rwkv_gn_g[j], rwkv_gn_b[j])
                mix = jnp.concatenate([y_pool, y_rwkv], axis=-1) @ w_out_ab[j]
                pool_out.append(pool_new)
                shift_out.append(shift_new)
                wkv_out.append(wkv_new)
            else:
                mix, v_rows = sgu_mix(h, w_in_c[j], sgu_ln_g[j], sgu_ln_b[j], sgu_w_s[j], sgu_b_s[j], w_out_c[j])
                sgu_v_out.append(v_rows)
            x = x + mix
            x = x + cross_attn(rmsnorm(x, norm_xa_g[l]), mem_k[l], mem_v[l], w_xq[l], w_xo[l])
            x = x + sq_relu_mlp(rmsnorm(x, norm_ffn_g[l]), w_ff_up[l], w_ff_down[l])
        return (rmsnorm(x, norm_final_g), jnp.stack(pool_out), jnp.stack(shift_out),
                jnp.stack(wkv_out), sgu_v_out)

    kv = [memory_kv(mem_prompt, norm_mem_g[l], w_xk[l], w_xv[l]) for l in range(DEPTH)]
    mem_k_prompt = jnp.stack([kv_l[0] for kv_l in kv])
    mem_v_prompt = jnp.stack([kv_l[1] for kv_l in kv])
    bp = x_prompt.shape[0]
    dt = x_prompt.dtype
    pool0 = jnp.zeros((N_AB_LAYERS, bp, POOL_BUF, POOL_WIDTH), dt)
    shift0 = jnp.zeros((N_AB_LAYERS, bp, RWKV_IN), dt)
    wkv0 = jnp.zeros((N_AB_LAYERS, bp, RWKV_HEADS, RWKV_HEAD, RWKV_HEAD), dt)
    y_prompt, pool_prompt, shift_prompt, wkv_prompt, _ = trunk(
        x_prompt, 0, mem_k_prompt, mem_v_prompt, pool0, shift0, wkv0)

    y_sample, pool_sample, shift_sample, wkv_sample, sgu_v = trunk(
        x_sample, PAST_LEN, cache_mem_k, cache_mem_v, state_pool, state_shift, state_wkv)
    sgu_v_sample = jnp.stack(sgu_v)

    return (y_prompt, y_sample, mem_k_prompt, mem_v_prompt, pool_prompt, pool_sample,
            shift_prompt, shift_sample, wkv_prompt, wkv_sample, sgu_v_sample)
```

```python
from contextlib import ExitStack
import numpy as np
import concourse.bass as bass
import concourse.mybir as mybir
from concourse.bass_utils import run_bass_kernel_spmd

F32 = mybir.dt.float32
F32R = mybir.dt.float32r
BF16 = mybir.dt.bfloat16
AF = mybir.ActivationFunctionType
ALU = mybir.AluOpType
AX = mybir.AxisListType

NCORES = 8
D = 1024
SEQ = 2048
NS = 16
LAM = 0.6065306597126334
WARM_LV = 2
WARM_PREP = 24
RMS_EPS = 1e-5
LN_EPS = 1e-5
GN_EPS = 64 * 1e-5
SAME_ENGINE_SYNC = True

PC = {}
_o = 0
for _n, _c in [("g_mix0", 8), ("g_xa0", 8), ("g_ffn0", 8), ("g_mix1", 8), ("g_xa1", 8), ("g_ffn1", 8),
               ("g_mem0", 8), ("g_mem1", 8), ("g_fin", 8), ("pscale", 4), ("mu", 14), ("w0", 4), ("a0", 4),
               ("k_k", 4), ("k_a", 4), ("r_k", 4), ("gn_g", 4), ("gn_b", 4), ("ln_g", 16), ("ln_b", 16)]:
    PC[_n] = _o
    _o += _c
NPRM = _o


class Res:
    __slots__ = ("name", "w", "r", "dsem", "dcnt", "psum")

    def __init__(self, name):
        self.name = name
        self.psum = False
        self.w = None
        self.r = {}
        self.dsem = None
        self.dcnt = 0


class Prog:
    ENG = ("pe", "act", "dve", "pool", "sp")

    def __init__(self, nc, stack):
        self.nc = nc
        self.stack = stack
        self.q = {e: [] for e in self.ENG}
        self.sems = {}
        self.cnt = {e: 0 for e in self.ENG}
        self.seen = {e: {} for e in self.ENG}
        self.nsem = 0
        for e in self.ENG:
            self.sems[e] = stack.enter_context(nc.semaphore("s_" + e))
            self.nsem += 1
        self.final = {}
        self.alld = {}

    def res(self, name):
        return Res(name)

    def _dsem(self, r):
        if r.dsem is None:
            key = "d%d" % self.nsem
            self.sems[key] = self.stack.enter_context(self.nc.semaphore(key))
            self.nsem += 1
            r.dsem = key
        return r.dsem

    def _need(self, e, reads, writes):
        need = {}

        def add(tok):
            if tok is None:
                return
            k, v = tok
            if need.get(k, 0) < v:
                need[k] = v
        for r in reads:
            add(r.w)
            if r.psum:
                for k, v in r.r.items():
                    if k != e:
                        add((k, v))
        for w in writes:
            add(w.w)
            for k, v in w.r.items():
                add((k, v))
        out = []
        for k, v in need.items():
            if k == e and (e == "pe" or not SAME_ENGINE_SYNC):
                continue
            if self.seen[e].get(k, 0) < v:
                self.seen[e][k] = v
                out.append((k, v))
        return out

    def _emit_waits(self, e, waits):
        for k, v in waits:
            sem = self.sems[k]
            self.q[e].append(lambda eng, sem=sem, v=v: eng.wait_ge(sem, v))

    def _mark(self, tok, reads, writes):
        k, v = tok
        for r in reads:
            if r.r.get(k, 0) < v:
                r.r[k] = v
        for w in writes:
            w.w = tok
            w.r = {}

    def op(self, e, fn, reads=(), writes=()):
        self._emit_waits(e, self._need(e, reads, writes))
        self.cnt[e] += 1
        sem = self.sems[e]
        self.q[e].append(lambda eng, fn=fn, sem=sem: fn(eng).then_inc(sem, 1))
        self._mark((e, self.cnt[e]), reads, writes)

    def dma(self, e, out, in_, reads=(), writes=(), owner=None, final=False, track=True, **kw):
        self._emit_waits(e, self._need(e, reads, writes))
        key = self._dsem(owner)
        owner.dcnt += 16
        sem = self.sems[key]
        self.q[e].append(lambda eng, out=out, in_=in_, sem=sem, kw=kw:
                         eng.dma_start(out=out, in_=in_, **kw).then_inc(sem, 16))
        self._mark((key, owner.dcnt), reads, writes)
        if final:
            self.final[key] = owner.dcnt
        if track:
            self.alld[key] = owner.dcnt

    def barrier_real(self, pool=True):
        engs = ("pe", "act", "dve", "pool", "sp") if pool else ("pe", "act", "dve", "sp")
        e0 = "sp"
        w = []
        for k, v in self.alld.items():
            if self.seen[e0].get(k, 0) < v:
                self.seen[e0][k] = v
                w.append((k, v))
        for o in engs:
            if o != e0 and self.seen[e0].get(o, 0) < self.cnt[o]:
                self.seen[e0][o] = self.cnt[o]
                w.append((o, self.cnt[o]))
        self._emit_waits(e0, w)
        self.cnt[e0] += 1
        sem = self.sems[e0]
        self.q[e0].append(lambda eng, sem=sem: eng.nop().then_inc(sem, 1))
        for o in engs:
            if o == e0:
                continue
            self.seen[o][e0] = self.cnt[e0]
            self._emit_waits(o, [(e0, self.cnt[e0])])
            for o2 in engs:
                self.seen[o][o2] = max(self.seen[o].get(o2, 0), self.cnt[o2])
            for k, v in self.alld.items():
                self.seen[o][k] = max(self.seen[o].get(k, 0), v)

    def barrier(self, pool=True):
        return

    def finish(self, e="sp"):
        for k, v in self.final.items():
            sem = self.sems[k]
            self.q[e].append(lambda eng, sem=sem, v=v: eng.wait_ge(sem, v))

    def emit(self):
        with self.nc.Block() as blk:
            def run(name):
                def f(eng):
                    for g in self.q[name]:
                        g(eng)
                return f
            blk.tensor(run("pe"))
            blk.scalar(run("act"))
            blk.vector(run("dve"))
            blk.gpsimd(run("pool"))
            blk.sync(run("sp"))


class T:
    def __init__(self, t, res):
        self.t = t
        self.res = res

    def r(self, i=0):
        return self.res[i if len(self.res) > 1 else 0]


def build_nc():
    nc = bass.Bass("TRN2", target_bir_lowering=False)

    def din(name, shape):
        return nc.dram_tensor(name, list(shape), F32, kind="ExternalInput").ap()

    def dout(name, shape):
        return nc.dram_tensor(name, list(shape), F32, kind="ExternalOutput").ap()

    I = dict(
        xp=din("xp", [SEQ, D]), xs=din("xs", [NS, D]), memp=din("memp", [256, D]),
        ck=din("ck", [2, NS, 256, D]), cv=din("cv", [2, NS, 256, D]),
        spool=din("spool", [NS, 15, 512]), sshift=din("sshift", [NS, 1792]), swkv=din("swkv", [128, 4096]),
        prm=din("prm", [128, NPRM]), bsb=din("bsb", [128, 512]), smp=din("smp", [128, 8]),
        lngb=din("lngb", [NS, 2048]), lnbb=din("lnbb", [NS, 2048]),
        w_in_ab=din("w_in_ab", [D, 2304]), w_out_ab=din("w_out_ab", [D, D]),
        pool_w=din("pool_w", [128, 4, 128]), w2=din("w2", [64, 512]), a2=din("a2", [64, 512]), g2=din("g2", [128, 512]),
        w_in_c=din("w_in_c", [D, 4096]), wsT=din("wsT", [128, 4, 128]), w_out_c=din("w_out_c", [2048, D]),
        w_xq=din("w_xq", [2, D, D]), w_xk=din("w_xk", [2, D, D]), w_xv=din("w_xv", [2, D, D]), w_xo=din("w_xo", [2, D, D]),
        w_ff_up=din("w_ff_up", [2, D, 4096]), w_ff_down=din("w_ff_down", [2, 4096, D]),
    )
    O = dict(
        yp=dout("yp", [SEQ, D]), ys=dout("ys", [NS, D]), mk=dout("mk", [2, 256, D]), mv=dout("mv", [2, 256, D]),
        poolp=dout("poolp", [15, 512]), pools=dout("pools", [NS, 15, 512]),
        shiftp=dout("shiftp", [14, 128]), shifts=dout("shifts", [NS, 1792]),
        wkvp=dout("wkvp", [8, 64, 64]), wkvs=dout("wkvs", [128, 4096]), sguv=dout("sguv", [NS, 2048]),
    )
    scr1 = nc.dram_tensor("scr1", [6, NS, 512], F32, kind="Internal").ap()
    scr2 = nc.dram_tensor("scr2", [NS, 512], F32, kind="Internal").ap()

    with ExitStack() as st:
        P = Prog(nc, st)

        def sb(name, shape, dt=F32, nres=1):
            t = st.enter_context(nc.sbuf_tensor("s_" + name, list(shape), dt))
            return T(t, [P.res("%s%d" % (name, i)) for i in range(nres)])

        PS = [st.enter_context(nc.psum_tensor("ps%d" % i, [128, 512], F32)) for i in range(8)]
        RPS = [P.res("ps%d" % i) for i in range(8)]
        for r_ in RPS:
            r_.psum = True
        pctr = [0]

        NBANK = [8]

        def nb():
            i = pctr[0] % NBANK[0]
            pctr[0] += 1
            return PS[i], RPS[i]

        def warm(n):
            for _ in range(n):
                P.op("pe", lambda e: e.matmul(PS[7][:, :], WARML.t[:], WARMR.t[:], start=True, stop=True),
                     reads=[WARML.r(), WARMR.r()], writes=[RPS[7]])

        NSLOT = 3
        WS = [sb("ws%d" % i, [128, 8, 512], BF16) for i in range(NSLOT)]
        wctr = [0]
        XT = sb("xT", [128, 8, 512], F32, 8)
        HT = sb("hT", [128, 8, 512], BF16, 8)
        XT_S = sb("xTs", [128, 8, NS], F32, 8)
        HT_S = sb("hTs", [128, 8, NS], BF16, 8)
        CUR = {"XT": XT, "HT": HT}

        def set_stream(sample):
            CUR["XT"] = XT_S if sample else XT
            CUR["HT"] = HT_S if sample else HT
        KT = [sb("kT%d" % l, [128, 8, 256], BF16) for l in range(2)]
        VM = [sb("vM%d" % l, [128, 2, 1024], BF16) for l in range(2)]
        PRM = sb("prm", [128, NPRM])
        IDN = sb("ident", [128, 128])
        MU2 = sb("mU2", [128, 512])
        MSL = sb("mSL", [128, 256])
        BO1 = sb("bo1", [128, 128], F32R)
        BO64 = sb("bo64", [128, 128], F32R)
        ON1K = sb("on1k", [128, 128], F32R)
        ONES = sb("ones", [128, 128])
        W2X = sb("w2x", [128, 512], F32R)
        A2X = sb("a2x", [128, 512], F32R)
        G2 = sb("g2", [128, 512], F32R)
        PW = sb("pw", [128, 4, 128], F32R)
        WMT = sb("wmT", [128, 4, 128], F32R)
        SMP = sb("smp", [128, 8])
        CS16 = sb("cs16", [128, 16])
        W00I = sb("w00i", [16, 4, 16], F32R)
        ICNT = sb("icnt", [128, 4, 16])
        SBLK = sb("sblk", [128, 4, 128], F32R, 4)
        SQ = [sb("sq%d" % i, [128, 512], F32R) for i in range(2)]
        RSTD = sb("rstd", [128, 512])
        ZHIS = sb("zhist", [128, 18, 16])
        WARML = sb("warml", [128, 128], BF16)
        WARMR = sb("warmr", [128, 512], BF16)
        sqc = [0]

        ARENA_R_W = 9216
        ARENA_F_W = 29520 - ARENA_R_W
        ARENAF = st.enter_context(nc.sbuf_tensor("arenaf", [128, ARENA_F_W], F32))
        ARENAR = st.enter_context(nc.sbuf_tensor("arenar", [128, ARENA_R_W], F32R))
        aoff = {"f": 0, "r": 0}
        LIVE = []
        REGION = {"f": [], "r": []}

        def _tokens(t):
            d = {}
            for r_ in t.res:
                if r_.w is not None and d.get(r_.w[0], 0) < r_.w[1]:
                    d[r_.w[0]] = r_.w[1]
                for k_, v_ in r_.r.items():
                    if d.get(k_, 0) < v_:
                        d[k_] = v_
            return d

        def _retire(pred):
            keep = []
            for ent in LIVE:
                key, o, end, t = ent
                if pred(ent):
                    reg = REGION[key]
                    reg[:] = [e_ for e_ in reg if not (e_[0] >= o and e_[1] <= end)]
                    reg.append((o, end, _tokens(t)))
                else:
                    keep.append(ent)
            LIVE[:] = keep

        def arena_reset():
            _retire(lambda ent: True)
            aoff["f"] = 0
            aoff["r"] = 0

        def amark():
            return dict(aoff)

        def arelease(m):
            _retire(lambda ent: ent[1] >= m[ent[0]])
            aoff.update(m)

        def av(name, shape, dt=F32, nres=1):
            nparts = shape[0]
            n = 1
            for s_ in shape[1:]:
                n *= s_
            words = n if dt in (F32, F32R) else (n + 1) // 2
            key = "r" if dt == F32R else "f"
            o = aoff[key]
            aoff[key] += (words + 7) // 8 * 8
            assert aoff[key] <= (ARENA_R_W if key == "r" else ARENA_F_W), (name, key, aoff[key])
            if key == "r":
                base = ARENAR[0:nparts, o:o + words]
            else:
                base = ARENAF[0:nparts, o:o + words]
                if dt == BF16:
                    base = base.bitcast(BF16)
            if len(shape) == 3:
                base = base.rearrange("p (a b) -> p a b", a=shape[1])
            elif len(shape) == 4:
                base = base.rearrange("p (a b c) -> p a b c", a=shape[1], b=shape[2])
            elif len(shape) == 5:
                base = base.rearrange("p (a b c d) -> p a b c d", a=shape[1], b=shape[2], c=shape[3])
            t_ = T(base, [P.res("%s%d" % (name, i)) for i in range(nres)])
            o_end = o + (words + 7) // 8 * 8
            inh = {}
            for (ro, rend, tok) in REGION[key]:
                if ro < o_end and rend > o:
                    for k_, v_ in tok.items():
                        if inh.get(k_, 0) < v_:
                            inh[k_] = v_
            for r_ in t_.res:
                r_.r = dict(inh)
            LIVE.append((key, o, o_end, t_))
            return t_

        def A(x):
            return x.t if isinstance(x, T) else x

        def mm(out, lhsT, rhs, start, stop, reads, wres):
            P.op("pe", lambda e: e.matmul(out, lhsT, rhs, start=start, stop=stop), reads=reads, writes=[wres])

        def tr(out, in_, n_in_part, reads, wres):
            P.op("pe", lambda e: e.transpose(out, in_, IDN.t[0:n_in_part, 0:n_in_part]), reads=list(reads) + [IDN.r()], writes=[wres])

        def act(out, in_, func, reads, writes, bias=None, scale=None, accum=None):
            kw = {}
            if bias is not None:
                kw["bias"] = bias
            if scale is not None:
                kw["scale"] = scale
            if accum is not None:
                kw["accum_out"] = accum
            P.op("act", lambda e: e.activation(out=out, in_=in_, func=func, **kw), reads=reads, writes=writes)

        def tt(out, in0, in1, op, reads, writes, eng="dve"):
            P.op(eng, lambda e: e.tensor_tensor(out=out, in0=in0, in1=in1, op=op), reads=reads, writes=writes)

        def ts(out, in0, s1, s2, op0, op1, reads, writes, eng="dve"):
            if s2 is None:
                P.op(eng, lambda e: e.tensor_scalar(out=out, in0=in0, scalar1=s1, scalar2=None, op0=op0), reads=reads, writes=writes)
            else:
                P.op(eng, lambda e: e.tensor_scalar(out=out, in0=in0, scalar1=s1, scalar2=s2, op0=op0, op1=op1), reads=reads, writes=writes)

        def stt(out, in0, scalar, in1, op0, op1, reads, writes):
            P.op("dve", lambda e: e.scalar_tensor_tensor(out=out, in0=in0, scalar=scalar, in1=in1, op0=op0, op1=op1), reads=reads, writes=writes)

        def cp(out, in_, reads, writes, eng="dve"):
            if eng == "act":
                P.op("act", lambda e: e.activation(out=out, in_=in_, func=AF.Copy), reads=reads, writes=writes)
            else:
                P.op(eng, lambda e: e.tensor_copy(out=out, in_=in_), reads=reads, writes=writes)

        def memset(ap, val, writes, eng="dve"):
            P.op(eng, lambda e: e.memset(ap, val), writes=writes)

        def fill_r(ap2d, val, writes, p0=0):
            p, n = ap2d.shape
            ts(ap2d, ONES.t[p0:p0 + p, 0:1].to_broadcast([p, n]), float(val), None, ALU.mult, None, [ONES.r()], writes)

        def prm(name, c, n=1):
            return PRM.t[:, PC[name] + c:PC[name] + c + n]

        def wload(src2d, ncols):
            s = WS[wctr[0] % NSLOT]
            wctr[0] += 1
            P.dma("pool", s.t[:, :, 0:ncols], src2d.rearrange("(kc p) n -> p kc n", p=128), writes=[s.r()], owner=s.r(), track=False)
            return s

        evq = [0]

        def ev_eng():
            evq[0] += 1
            return "act" if evq[0] % 2 else "dve"

        memset(ONES.t[:], 1.0, [ONES.r()], "dve")
        memset(WARML.t[:], 1.0, [WARML.r()], "dve")
        memset(WARMR.t[:], 1.0, [WARMR.r()], "dve")
        P.dma("sp", PRM.t[:], I["prm"], writes=[PRM.r()], owner=PRM.r())
        P.dma("sp", SMP.t[:], I["smp"], writes=[SMP.r()], owner=SMP.r())
        P.dma("pool", G2.t[:], I["g2"], writes=[G2.r()], owner=G2.r())
        P.dma("pool", PW.t[:], I["pool_w"], writes=[PW.r()], owner=PW.r())
        fill_r(W2X.t[:], 0.0, [W2X.r()])
        fill_r(A2X.t[:], 0.0, [A2X.r()])
        P.dma("pool", W2X.t[0:64, :], I["w2"], writes=[W2X.r()], owner=W2X.r())
        P.dma("pool", A2X.t[64:128, :], I["a2"], writes=[A2X.r()], owner=A2X.r())
        memset(IDN.t[:], 0.0, [IDN.r()], "pool")
        P.op("pool", lambda e: e.affine_select(out=IDN.t[:], in_=IDN.t[:], pattern=[[-1, 128]], compare_op=ALU.not_equal,
                                                fill=1.0, base=0, channel_multiplier=1), reads=[IDN.r()], writes=[IDN.r()])
        fill_r(ON1K.t[:], 1.0 / 1024.0, [ON1K.r()])
        for h in range(2):
            o = h * 256
            memset(MU2.t[:, o:o + 256], 1.0, [MU2.r()], "pool")
            P.op("pool", lambda e, o=o: e.affine_select(out=MU2.t[:, o:o + 128], in_=MU2.t[:, o:o + 128], pattern=[[1, 128]],
                                                        compare_op=ALU.is_gt, fill=0.0, base=0, channel_multiplier=-1),
                 reads=[MU2.r()], writes=[MU2.r()])
            P.op("pool", lambda e, o=o: e.affine_select(out=MU2.t[:, o + 128:o + 256], in_=MU2.t[:, o + 128:o + 256], pattern=[[1, 128]],
                                                        compare_op=ALU.is_ge, fill=0.0, base=0, channel_multiplier=-1),
                 reads=[MU2.r()], writes=[MU2.r()])
            memset(MSL.t[:, h * 128:(h + 1) * 128], 1.0, [MSL.r()], "pool")
            P.op("pool", lambda e, h=h: e.affine_select(out=MSL.t[:, h * 128:(h + 1) * 128], in_=MSL.t[:, h * 128:(h + 1) * 128],
                                                        pattern=[[-1, 128]], compare_op=ALU.is_gt, fill=0.0, base=0, channel_multiplier=1),
                 reads=[MSL.r()], writes=[MSL.r()])
        fill_r(BO1.t[:], 0.0, [BO1.r()])
        fill_r(BO1.t[0:64, 0:64], 1.0, [BO1.r()])
        fill_r(BO1.t[64:128, 64:128], 1.0, [BO1.r()], 64)
        fill_r(BO64.t[:], 0.0, [BO64.r()])
        fill_r(BO64.t[0:64, 0:64], 1.0 / 64.0, [BO64.r()])
        fill_r(BO64.t[64:128, 64:128], 1.0 / 64.0, [BO64.r()], 64)
        if STOP_AT == "c1":
            P.finish("sp"); P.emit(); return nc
        P.op("pool", lambda e: e.iota(ICNT.t[:, 0, :], pattern=[[1, 16]], base=1, channel_multiplier=0, allow_small_or_imprecise_dtypes=True),
             writes=[ICNT.r()])
        for g, win in enumerate((2, 4, 8, 16)):
            if g > 0:
                cp(ICNT.t[:, g, :], ICNT.t[:, 0, :], [ICNT.r()], [ICNT.r()])
        for g, win in reversed(list(enumerate((2, 4, 8, 16)))):
            ts(ICNT.t[:, g, :], ICNT.t[:, g, :], float(win), None, ALU.min, None, [ICNT.r()], [ICNT.r()])
        P.op("dve", lambda e: e.reciprocal(out=ICNT.t[:], in_=ICNT.t[:]), reads=[ICNT.r()], writes=[ICNT.r()])
        fill_r(SBLK.t[:].rearrange("p a b -> p (a b)"), 0.0, SBLK.res)
        if STOP_AT == "c2":
            P.finish("sp"); P.emit(); return nc
        arena_reset()
        WST = av("wst", [128, 4, 128])
        P.dma("sp", WST.t[:], I["wsT"], writes=[WST.r()], owner=WST.r())
        tt(WMT.t[:], WST.t[:], MU2.t[:, 128:256].unsqueeze(1).to_broadcast([128, 4, 128]), ALU.mult, [WST.r(), MU2.r()], [WMT.r()])
        for g in range(4):
            ts(CS16.t[:, g * 4:g * 4 + 4], prm("ln_b", g * 4, 4), SMP.t[:, g:g + 1], SMP.t[:, 4 + g:5 + g], ALU.mult, ALU.add,
               [PRM.r(), SMP.r()], [CS16.r()])
            ts(W00I.t[:, g, :], IDN.t[0:16, 0:16], SMP.t[0:16, g:g + 1], None, ALU.mult, None, [IDN.r(), SMP.r()], [W00I.r()])
        if STOP_AT in ("c3", "c3x1", "c3x2", "c3x3"):
            P.finish("sp"); P.emit(); return nc
        def rmsnorm(Tn, gname, out_fn):
            bk, rb = nb()
            for c in range(8):
                s = SQ[sqc[0] % 2]
                sqc[0] += 1
                if c % 2 == 0:
                    act(s.t[:, 0:Tn], CUR["XT"].t[:, c, 0:Tn], AF.Square, [CUR["XT"].r(c)], [s.r()])
                else:
                    tt(s.t[:, 0:Tn], CUR["XT"].t[:, c, 0:Tn], CUR["XT"].t[:, c, 0:Tn], ALU.mult, [CUR["XT"].r(c)], [s.r()])
                mm(bk[:, 0:Tn], ON1K.t[:], s.t[:, 0:Tn], c == 0, c == 7, [ON1K.r(), s.r()], rb)
            act(RSTD.t[:, 0:Tn], bk[:, 0:Tn], AF.Ln, [rb], [RSTD.r()], bias=RMS_EPS)
            act(RSTD.t[:, 0:Tn], RSTD.t[:, 0:Tn], AF.Exp, [RSTD.r()], [RSTD.r()], scale=-0.5)
            for c in range(8):
                out_fn(c, CUR["XT"].t[:, c, 0:Tn], prm(gname, c), RSTD.t[:, 0:Tn])

        def norm_to_h(Tn, gname):
            def f(c, x, g, rs):
                stt(CUR["HT"].t[:, c, 0:Tn], x, g, rs, ALU.mult, ALU.mult, [CUR["XT"].r(c), PRM.r(), RSTD.r()], [CUR["HT"].r(c)])
            rmsnorm(Tn, gname, f)

        def proj_fm_multi(wsrc, kin, nout, targets):
            for c0 in range(0, nout, 512):
                ncols = min(512, nout - c0)
                s_ = wload(wsrc[:, c0:c0 + ncols], ncols)
                for mq in range(ncols // 128):
                    for (Tn, rhs_fn, evac_fn) in targets:
                        bk, rb = nb()
                        for kc in range(kin):
                            rap, rr = rhs_fn(kc)
                            mm(bk[:, 0:Tn], s_.t[:, kc, mq * 128:(mq + 1) * 128], rap, kc == 0, kc == kin - 1, [s_.r(), rr], rb)
                        evac_fn(c0 // 128 + mq, bk[:, 0:Tn], rb)

        def proj_fm(Tn, wsrc, kin, nout, rhs_fn, evac_fn):
            proj_fm_multi(wsrc, kin, nout, [(Tn, rhs_fn, evac_fn)])

        def proj_down_multi(wsrc, kchunks, targets):
            nkq = kchunks // 8
            for cg in range(2):
                banks = [[nb() for _ in range(4)] for _t in targets]
                for kq in range(nkq):
                    s_ = wload(wsrc[kq * 1024:(kq + 1) * 1024, cg * 512:(cg + 1) * 512], 512)
                    for mq in range(4):
                        for ti, (Tn, rhs_fn, xt) in enumerate(targets):
                            bk, rb = banks[ti][mq]
                            for kc in range(8):
                                rap, rr = rhs_fn(kq * 8 + kc)
                                mm(bk[:, 0:Tn], s_.t[:, kc, mq * 128:(mq + 1) * 128], rap, kq == 0 and kc == 0,
                                   kq == nkq - 1 and kc == 7, [s_.r(), rr], rb)
                for ti, (Tn, rhs_fn, xt) in enumerate(targets):
                    for mq in range(4):
                        m = cg * 4 + mq
                        bk, rb = banks[ti][mq]
                        tt(xt.t[:, m, 0:Tn], xt.t[:, m, 0:Tn], bk[:, 0:Tn], ALU.add, [xt.r(m), rb], [xt.r(m)])

        def proj_down(Tn, wsrc, kchunks, rhs_fn):
            proj_down_multi(wsrc, kchunks, [(Tn, rhs_fn, CUR["XT"])])

        def hrhs(Tn):
            return lambda kc: (CUR["HT"].t[:, kc, 0:Tn], CUR["HT"].r(kc))

        P.barrier(pool=True)
        arena_reset()
        MTOK = av("mtok", [128, 2, 1024])
        MT = av("memT", [128, 8, 256], F32, 8)
        MN = [av("mn%d" % l, [128, 8, 256], BF16, 8) for l in range(2)]
        KVS = [av("kvs%d" % i, [128, 2, 1024]) for i in range(2)]
        P.dma("sp", MTOK.t[:], I["memp"].rearrange("(a p) f -> p a f", p=128), writes=[MTOK.r()], owner=MTOK.r())
        for c in range(8):
            bk, rb = nb()
            for a in range(2):
                tr(bk[:, a * 128:(a + 1) * 128], MTOK.t[:, a, c * 128:(c + 1) * 128], 128, [MTOK.r()], rb)
            cp(MT.t[:, c, :], bk[:, 0:256], [rb], [MT.r(c)], ev_eng())
        bk, rb = nb()
        for c in range(8):
            s = SQ[sqc[0] % 2]
            sqc[0] += 1
            act(s.t[:, 0:256], MT.t[:, c, :], AF.Square, [MT.r(c)], [s.r()])
            mm(bk[:, 0:256], ON1K.t[:], s.t[:, 0:256], c == 0, c == 7, [ON1K.r(), s.r()], rb)
        act(RSTD.t[:, 0:256], bk[:, 0:256], AF.Ln, [rb], [RSTD.r()], bias=RMS_EPS)
        act(RSTD.t[:, 0:256], RSTD.t[:, 0:256], AF.Exp, [RSTD.r()], [RSTD.r()], scale=-0.5)
        for l in range(2):
            for c in range(8):
                stt(MN[l].t[:, c, :], MT.t[:, c, :], prm("g_mem%d" % l, c), RSTD.t[:, 0:256], ALU.mult, ALU.mult,
                    [MT.r(c), PRM.r(), RSTD.r()], [MN[l].r(c)])
        if STOP_AT == "mk1":
            P.finish("sp"); P.emit(); return nc
        kvi = 0
        for l in range(2):
            for which, wname, oname in (("k", "w_xk", "mk"), ("v", "w_xv", "mv")):
                for cg in range(2):
                    s = wload(I[wname][l][:, cg * 512:(cg + 1) * 512], 512)
                    if which == "k":
                        for mq in range(4):
                            bk, rb = nb()
                            for kc in range(8):
                                mm(bk[:, 0:256], s.t[:, kc, mq * 128:(mq + 1) * 128], MN[l].t[:, kc, :], kc == 0, kc == 7,
                                   [s.r(), MN[l].r(kc)], rb)
                            cp(KT[l].t[:, cg * 4 + mq, :], bk[:, 0:256], [rb], [KT[l].r()], ev_eng())
                    for a in range(2):
                        bk, rb = nb()
                        for kc in range(8):
                            mm(bk[:, :], MN[l].t[:, kc, a * 128:(a + 1) * 128], s.t[:, kc, :], kc == 0, kc == 7, [s.r(), MN[l].r(kc)], rb)
                        stg = KVS[kvi % 2]
                        cp(stg.t[:, a, cg * 512:(cg + 1) * 512], bk[:, :], [rb], [stg.r()], "act")
                        if which == "v":
                            cp(VM[l].t[:, a, cg * 512:(cg + 1) * 512], bk[:, :], [rb], [VM[l].r()], "dve")
                    if STOP_AT == "mk2" or (STOP_AT == "mk3c" and which == "v" and cg == 0) or (STOP_AT == "mk3d" and which == "v" and cg == 1):
                        P.finish("sp"); P.emit(); return nc
                stg = KVS[kvi % 2]
                kvi += 1
                P.dma("sp", O[oname][l].rearrange("(a p) f -> p a f", p=128), stg.t[:], reads=[stg.r()], owner=stg.r(), final=True)
                if STOP_AT == "mk3":
                    P.finish("sp"); P.emit(); return nc
        if STOP_AT == "mk4":
            P.finish("sp"); P.emit(); return nc
        P.barrier()
        try:
            stage("setup")
        except StopBuild:
            P.finish("sp")
            P.emit()
            return nc

        def dims(sample):
            return (NS, 1, NS) if sample else (512, 4, 128)

        def load_x(blk, sample, pool):
            Tn, nch, CH = dims(sample)
            set_stream(sample)
            arena_reset()
            XTOK = av("xtok", [128, nch, 1024])
            if sample:
                P.dma("sp", XTOK.t[0:NS, 0, :], I["xs"], writes=[XTOK.r()], owner=XTOK.r())
            else:
                P.dma("sp", XTOK.t[:], I["xp"][blk * 512:(blk + 1) * 512, :].rearrange("(a p) f -> p a f", p=128),
                      writes=[XTOK.r()], owner=XTOK.r())
            for c in range(8):
                bk, rb = nb()
                for a in range(nch):
                    tr(bk[:, a * CH:(a + 1) * CH], XTOK.t[0:CH, a, c * 128:(c + 1) * 128], CH, [XTOK.r()], rb)
                cp(CUR["XT"].t[:, c, 0:Tn], bk[:, 0:Tn], [rb], [CUR["XT"].r(c)], ev_eng())
            P.barrier(pool=pool)

        def phase_mixer(blk, sample, layer, pool):
            Tn, nch, CH = dims(sample)
            set_stream(sample)
            arena_reset()
            norm_to_h(Tn, "g_mix%d" % layer)
            if layer == 0:
                mixer_ab(blk, sample, Tn, nch, CH)
            else:
                mixer_c(blk, sample, Tn, nch, CH)
            P.barrier(pool=pool)

        def phase_attn(blk, sample, layer, pool):
            Tn, nch, CH = dims(sample)
            set_stream(sample)
            arena_reset()
            norm_to_h(Tn, "g_xa%d" % layer)
            if sample:
                xattn_sample(layer)
            else:
                xattn_prompt(layer)
            P.barrier(pool=pool)

        def phase_mlp(streams, layer, pool):
            arena_reset()
            up_t, dn_t = [], []
            for sample in streams:
                Tn, nch, CH = dims(sample)
                set_stream(sample)
                norm_to_h(Tn, "g_ffn%d" % layer)
                tagn = "s" if sample else "p"
                HID = av("hid" + tagn, [128, 32, Tn], BF16, 32)
                TMP = [av("mlptmp%s%d" % (tagn, i_), [128, Tn]) for i_ in range(2)]

                def ev_up(m, bk, rb, HID=HID, TMP=TMP, tctr=[0]):
                    t = TMP[tctr[0] % 2]
                    tctr[0] += 1
                    act(t.t[:], bk, AF.Square, [rb], [t.r()])
                    stt(HID.t[:, m, :], bk, 0.0, t.t[:], ALU.is_gt, ALU.mult, [rb, t.r()], [HID.r(m)])
                ht = CUR["HT"]
                up_t.append((Tn, (lambda kc, ht=ht, Tn=Tn: (ht.t[:, kc, 0:Tn], ht.r(kc))), ev_up))
                dn_t.append((Tn, (lambda k, HID=HID: (HID.t[:, k, :], HID.r(k))), CUR["XT"]))
            proj_fm_multi(I["w_ff_up"][layer], 8, 4096, up_t)
            proj_down_multi(I["w_ff_down"][layer], 32, dn_t)
            P.barrier(pool=pool)

        def final_out(blk, sample, pool):
            Tn, nch, CH = dims(sample)
            set_stream(sample)
            arena_reset()
            YF = av("yf", [128, 8, Tn], F32, 8)
            YTOK = av("ytok", [128, nch, 1024])
            xt = CUR["XT"]

            def fo(c, x, g, rs):
                stt(YF.t[:, c, :], x, g, rs, ALU.mult, ALU.mult, [xt.r(c), PRM.r(), RSTD.r()], [YF.r(c)])
            rmsnorm(Tn, "g_fin", fo)
            for a in range(nch):
                for cg in range(2):
                    bk, rb = nb()
                    for q in range(4):
                        c = cg * 4 + q
                        tr(bk[0:CH, q * 128:(q + 1) * 128], YF.t[:, c, a * CH:(a + 1) * CH], 128, [YF.r(c)], rb)
                    cp(YTOK.t[0:CH, a, cg * 512:(cg + 1) * 512], bk[0:CH, :], [rb], [YTOK.r()], ev_eng())
            if sample:
                P.dma("sp", O["ys"], YTOK.t[0:NS, 0, :], reads=[YTOK.r()], owner=YTOK.r(), final=True)
            else:
                P.dma("sp", O["yp"][blk * 512:(blk + 1) * 512, :].rearrange("(a p) f -> p a f", p=128), YTOK.t[:],
                      reads=[YTOK.r()], owner=YTOK.r(), final=True)
            P.barrier(pool=pool)

        def run_block(blk, sample):
            load_x(blk, sample, True)
            for layer in range(2):
                phase_mixer(blk, sample, layer, True)
                phase_attn(blk, sample, layer, True)
                phase_mlp([sample], layer, True)
            final_out(blk, sample, True)

        def run_joint():
            load_x(3, False, True)
            load_x(0, True, True)
            for layer in range(2):
                phase_mixer(3, False, layer, True)
                phase_attn(3, False, layer, True)
                phase_mixer(0, True, layer, True)
                phase_attn(0, True, layer, True)
                phase_mlp([False, True], layer, True)
            final_out(3, False, True)
            final_out(0, True, True)

        HOLD = {}

        def mixer_ab(blk, sample, Tn, nch, CH):
            NBANK[0] = 8 if sample else 7
            mixer_ab_(blk, sample, Tn, nch, CH)
            NBANK[0] = 8

        def mixer_ab_(blk, sample, Tn, nch, CH):
            H0 = 16
            ZB = av("zb", [128, 18, H0 + Tn], F32, 18)
            if not sample:
                HIS = ZHIS
                if "hist" not in HOLD:
                    HOLD["hist"] = True
                    memset(HIS.t[:], 0.0, [HIS.r()])
                cp(ZB.t[:, :, 0:H0], HIS.t[:], [HIS.r()], ZB.res, "dve")
            YC = av("ycat", [128, 8, Tn], BF16, 8)

            def ev_z(m, bk, rb):
                cp(ZB.t[:, m, H0:H0 + Tn], bk, [rb], [ZB.r(m)], ev_eng())
            proj_fm(Tn, I["w_in_ab"], 8, 2304, hrhs(Tn), ev_z)

            if sample:
                Q6 = av("q6", [128, 6, 512])
                QR = av("qr", [128, 6, 64])
                SAV = av("sav", [128, 64]); YV = av("yv", [128, 64]); YTS = av("yts", [128, 512])
                ZPS = av("zps", [128, 14, NS])
            PA = av("pa", [128, H0 + Tn])
            PB = av("pb", [128, H0 + Tn])
            P16 = av("p16", [128, 16])
            LOR = av("lor", [128, Tn], F32R)
            SG = av("sg", [128, Tn], F32R)
            TMPL = T(PA.t[:, 0:Tn], PA.res)
            ZSR = av("zsr", [128, Tn]); ZSK = av("zsk", [128, Tn]); ZSV = av("zsv", [128, Tn])
            SIG = av("sig", [128, Tn]); CSUM = av("csum", [128, Tn]); E1 = T(PA.t[:, 0:Tn], PA.res)
            AA = av("aa", [128, Tn]); KKN = av("kkn", [128, Tn]); XX = av("xx", [128, Tn], F32R)
            DP = XX
            XF = T(PB.t[:, 0:Tn], PB.res)
            YTP = av("ytp", [128, Tn], F32R)
            NLC = av("nlc", [128, 4]); DCT = av("dct", [128, 4])
            if not sample:
                EB = av("eb", [128, Tn]); ED = av("ed", [128, Tn])
                AR = av("ar", [128, 4, 2, 128], F32R)
                BT = av("bt", [128, 512], F32R); KTt = av("ktt", [128, 512], F32R)
                SETS = []
                for k_ in range(2):
                    S_ = dict(
                        TOK=av("tok%d" % k_, [128, 4, 128], F32R), M1=av("m1%d" % k_, [128, 2, 128], F32R), M3=av("m3%d" % k_, [128, 512], F32R),
                        ZQ=[av("zq%d%d" % (k_, q_), [128, 2, 2, 128], BF16) for q_ in range(2)],
                        PP=[av("pp%d%d" % (k_, q_), [128, 2, 128], BF16) for q_ in range(2)],
                        ARX=av("arx%d" % k_, [128, 4, 128], F32R), BTX=av("btx%d" % k_, [128, 2, 128], F32R),
                        APT=av("apt%d" % k_, [128, 128], F32R), UT=av("ut%d" % k_, [128, 128], F32R),
                        APC=av("apc%d" % k_, [128, 128]), YTK=av("ytk%d" % k_, [128, 128]),
                        BH=av("bh%d" % k_, [128, 128]), KH=av("kh%d" % k_, [128, 128]))
                    fill_r(S_["ARX"].t[:].rearrange("p a b -> p (a b)"), 0.0, [S_["ARX"].r()])
                    fill_r(S_["BTX"].t[:].rearrange("p a b -> p (a b)"), 0.0, [S_["BTX"].r()])
                    SETS.append(S_)
            else:
                pass

            AMARK = amark()
            if sample:
                SPT = av("spt", [128, 2, 512])
                PH = av("ph", [128, 4, 240])
                spf = I["spool"].rearrange("b p f -> (b p) f")
                P.dma("sp", SPT.t[:, 0, :], spf[0:128, :], writes=[SPT.r()], owner=SPT.r())
                P.dma("sp", SPT.t[0:112, 1, :], spf[128:240, :], writes=[SPT.r()], owner=SPT.r())
                for g in range(4):
                    bk, rb = nb()
                    tr(bk[:, 0:128], SPT.t[:, 0, g * 128:(g + 1) * 128], 128, [SPT.r()], rb)
                    tr(bk[:, 128:240], SPT.t[0:112, 1, g * 128:(g + 1) * 128], 112, [SPT.r()], rb)
                    cp(PH.t[:, g, :], bk[:, 0:240], [rb], [PH.r()], ev_eng())
                PCP = P.res("pcopy")
                P.dma("sp", O["pools"][:, 0:14, :], I["spool"][:, 1:15, :], owner=PCP, final=True)
            for g, win in enumerate((2, 4, 8, 16)):
                zg = ZB.t[:, g, :]
                zn = ZB.t[:, g, H0:H0 + Tn]
                if sample:
                    P.op("dve", lambda e, g=g, win=win: e.tensor_reduce(
                        out=PA.t[:, 0:NS], in_=PH.t[:, g, :].rearrange("p (b q) -> p b q", q=15)[:, :, 16 - win:15], axis=AX.X, op=ALU.add),
                        reads=[PH.r()], writes=[PA.r()])
                    tt(PA.t[:, 0:NS], PA.t[:, 0:NS], zn, ALU.add, [PA.r(), ZB.r(g)], [PA.r()])
                    S_ap = PA.t[:, 0:NS]
                    S_r = PA.r()
                else:
                    lo = {2: 16, 4: 14, 8: 10, 16: 2}[win]
                    tt(PA.t[:, lo:H0 + Tn], zg[:, lo:H0 + Tn], zg[:, lo - 1:H0 + Tn - 1], ALU.add, [ZB.r(g)], [PA.r()])
                    cur, oth = PA, PB
                    sh = 2
                    lo2 = lo
                    while sh < win:
                        lo2 = lo2 + sh
                        tt(oth.t[:, lo2:H0 + Tn], cur.t[:, lo2:H0 + Tn], cur.t[:, lo2 - sh:H0 + Tn - sh], ALU.add, [cur.r()], [oth.r()])
                        cur, oth = oth, cur
                        sh *= 2
                    S_ap = cur.t[:, H0:H0 + Tn]
                    S_r = cur.r()
                stt(DP.t[:], S_ap, 1.0 / win, zn, ALU.mult, ALU.subtract, [S_r, ZB.r(g)], [DP.r()])
                if (not sample) and blk == 0:
                    tt(P16.t[:], S_ap[:, 0:16], ICNT.t[:, g, :], ALU.mult, [S_r, ICNT.r()], [P16.r()])
                    tt(DP.t[:, 0:16], P16.t[:], zn[:, 0:16], ALU.subtract, [P16.r(), ZB.r(g)], [DP.r()])
                bk, rb = nb()
                mm(bk[:, 0:Tn], PW.t[:, g, :], DP.t[:], True, True, [PW.r(), DP.r()], rb)
                ts(YC.t[:, g, :], bk[:, 0:Tn], prm("pscale", g), None, ALU.mult, None, [rb, PRM.r()], [YC.r(g)])

            if sample:
                SHT = av("sht", [128, 1792])
                P.dma("sp", SHT.t[0:NS, :], I["sshift"], writes=[SHT.r()], owner=SHT.r())
                for q in range(0, 14, 7):
                    bk, rb = nb()
                    for j in range(q, q + 7):
                        tr(bk[:, (j - q) * NS:(j - q + 1) * NS], SHT.t[0:NS, j * 128:(j + 1) * 128], NS, [SHT.r()], rb)
                    cp(ZPS.t[:, q:q + 7, :], bk[:, 0:7 * NS].rearrange("p (a b) -> p a b", a=7), [rb], [ZPS.r()], ev_eng())

            def zshift(j, out, out_r):
                m = 4 + j
                zc = ZB.t[:, m, H0:H0 + Tn]
                zp = ZPS.t[:, j, :] if sample else ZB.t[:, m, H0 - 1:H0 + Tn - 1]
                rd = [ZB.r(m)] + ([ZPS.r()] if sample else [])
                tt(out, zp, zc, ALU.subtract, rd, [out_r])
                stt(out, out, prm("mu", j), zc, ALU.mult, ALU.add, [out_r, PRM.r(), ZB.r(m)], [out_r])

            zshift(12, TMPL.t[:], TMPL.r())
            act(LOR.t[0:64, :], TMPL.t[0:64, :], AF.Tanh, [TMPL.r()], [LOR.r()])
            cp(LOR.t[64:128, :], TMPL.t[64:128, :], [TMPL.r()], [LOR.r()], "dve")
            zshift(13, TMPL.t[:], TMPL.r())
            act(SG.t[:], TMPL.t[:], AF.Sigmoid, [TMPL.r()], [SG.r()])

            def prep_pair(i, bonus_only):
                zshift(i, ZSR.t[:], ZSR.r())
                zshift(4 + i, ZSK.t[:], ZSK.r())
                zshift(8 + i, ZSV.t[:], ZSV.r())
                bk, rb = nb()
                mm(bk[:, 0:Tn], A2X.t[:, i * 128:(i + 1) * 128], LOR.t[:], True, True, [A2X.r(), LOR.r()], rb)
                act(AA.t[:], bk[:, 0:Tn], AF.Sigmoid, [rb, PRM.r()], [AA.r()], bias=prm("a0", i))
                if not sample:
                    warm(WARM_PREP)
                if not bonus_only:
                    bk, rb = nb()
                    mm(bk[:, 0:Tn], W2X.t[:, i * 128:(i + 1) * 128], LOR.t[:], True, True, [W2X.r(), LOR.r()], rb)
                    if not sample:
                        warm(WARM_PREP)
                    act(SIG.t[:], bk[:, 0:Tn], AF.Sigmoid, [rb, PRM.r()], [SIG.r()], bias=prm("w0", i))
                    ts(KKN.t[:], ZSK.t[:], prm("k_k", i), None, ALU.mult, None, [ZSK.r(), PRM.r()], [KKN.r()])
                    act(XX.t[:], KKN.t[:], AF.Square, [KKN.r()], [XX.r()])
                    bk, rb = nb()
                    mm(bk[:, 0:Tn], BO1.t[:], XX.t[:], True, True, [BO1.r(), XX.r()], rb)
                    if not sample:
                        warm(WARM_PREP)
                    ts(XF.t[:], bk[:, 0:Tn], 1e-24, None, ALU.max, None, [rb], [XF.r()])
                    act(XF.t[:], XF.t[:], AF.Ln, [XF.r()], [XF.r()])
                    act(XF.t[:], XF.t[:], AF.Exp, [XF.r()], [XF.r()], scale=-0.5)
                    tt(KKN.t[:], KKN.t[:], XF.t[:], ALU.mult, [KKN.r(), XF.r()], [KKN.r()])
                ts(XF.t[:], AA.t[:], 1.0, prm("k_a", i), ALU.subtract, ALU.mult, [AA.r(), PRM.r()], [XF.r()])
                stt(ZSK.t[:], XF.t[:], 1.0, ZSK.t[:], ALU.add, ALU.mult, [XF.r(), ZSK.r()], [ZSK.r()])
                if not bonus_only:
                    tt(AA.t[:], KKN.t[:], AA.t[:], ALU.mult, [KKN.r(), AA.r()], [AA.r()])

            def gn_pair(i):
                bk, rb = nb()
                mm(bk[:, 0:Tn], BO64.t[:], YTP.t[:], True, True, [BO64.r(), YTP.r()], rb)
                tt(SIG.t[:], YTP.t[:], bk[:, 0:Tn], ALU.subtract, [YTP.r(), rb], [SIG.r()])
                act(XX.t[:], SIG.t[:], AF.Square, [SIG.r()], [XX.r()])
                bk, rb = nb()
                mm(bk[:, 0:Tn], BO64.t[:], XX.t[:], True, True, [BO64.r(), XX.r()], rb)
                act(CSUM.t[:], bk[:, 0:Tn], AF.Ln, [rb], [CSUM.r()], bias=GN_EPS)
                act(CSUM.t[:], CSUM.t[:], AF.Exp, [CSUM.r()], [CSUM.r()], scale=-0.5)
                tt(SIG.t[:], SIG.t[:], CSUM.t[:], ALU.mult, [SIG.r(), CSUM.r()], [SIG.r()])
                ts(SIG.t[:], SIG.t[:], prm("gn_g", i), prm("gn_b", i), ALU.mult, ALU.add, [SIG.r(), PRM.r()], [SIG.r()])
                stt(XX.t[:], ZSR.t[:], prm("r_k", i), ZSK.t[:], ALU.mult, ALU.mult, [ZSR.r(), PRM.r(), ZSK.r()], [XX.r()])
                bk, rb = nb()
                mm(bk[:, 0:Tn], BO1.t[:], XX.t[:], True, True, [BO1.r(), XX.r()], rb)
                tt(E1.t[:], bk[:, 0:Tn], ZSV.t[:], ALU.mult, [rb, ZSV.r()], [E1.r()])
                tt(SIG.t[:], SIG.t[:], E1.t[:], ALU.add, [SIG.r(), E1.r()], [SIG.r()])
                bk, rb = nb()
                mm(bk[:, 0:Tn], G2.t[:, i * 128:(i + 1) * 128], SG.t[:], True, True, [G2.r(), SG.r()], rb)
                tt(YC.t[:, 4 + i, :], SIG.t[:], bk[:, 0:Tn], ALU.mult, [SIG.r(), rb], [YC.r(4 + i)])

            c3 = lambda ap: ap.rearrange("p (c t) -> p c t", c=4)
            for i in range(4):
                prep_pair(i, False)
                if not sample:
                    for c in range(4):
                        P.op("dve", lambda e, c=c: e.tensor_tensor_scan(
                            out=CSUM.t[:, c * 128:(c + 1) * 128], data0=ONES.t[:, 0:128], data1=SIG.t[:, c * 128:(c + 1) * 128],
                            initial=0.0, op0=ALU.mult, op1=ALU.add), reads=[ONES.r(), SIG.r()], writes=[CSUM.r()])
                    ts(NLC.t[:], c3(CSUM.t[:])[:, :, 127], -LAM, None, ALU.mult, None, [CSUM.r()], [NLC.r()])
                    act(DCT.t[:], NLC.t[:], AF.Exp, [NLC.r()], [DCT.r()])
                    act(E1.t[:], CSUM.t[:], AF.Exp, [CSUM.r()], [E1.r()], scale=-LAM)
                    tt(AR.t[:, :, 1, :], c3(ZSR.t[:]), c3(E1.t[:]), ALU.mult, [ZSR.r(), E1.r()], [AR.r()])
                    act(EB.t[:], CSUM.t[:], AF.Exp, [CSUM.r()], [EB.r()], scale=LAM)
                    tt(BT.t[:], AA.t[:], EB.t[:], ALU.mult, [AA.r(), EB.r()], [BT.r()])
                    tt(KTt.t[:], ZSK.t[:], EB.t[:], ALU.mult, [ZSK.r(), EB.r()], [KTt.r()])
                    tt(SIG.t[:], CSUM.t[:], SIG.t[:], ALU.subtract, [CSUM.r(), SIG.r()], [SIG.r()])
                    act(E1.t[:], SIG.t[:], AF.Exp, [SIG.r()], [E1.r()], scale=-LAM)
                    stt(AR.t[:, :, 0, :], c3(KKN.t[:]), -1.0, c3(E1.t[:]), ALU.mult, ALU.mult, [KKN.r(), E1.r()], [AR.r()])
                    for c in range(4):
                        act(ED.t[:, c * 128:(c + 1) * 128], CSUM.t[:, c * 128:(c + 1) * 128], AF.Exp, [CSUM.r(), NLC.r()], [ED.r()],
                            bias=NLC.t[:, c:c + 1], scale=LAM)

                    def front_a(c, S):
                        csl = slice(c * 128, (c + 1) * 128)
                        ARX, BTX = S["ARX"], S["BTX"]
                        tt(S["BH"].t[:], AA.t[:, csl], ED.t[:, csl], ALU.mult, [AA.r(), ED.r()], [S["BH"].r()])
                        tt(S["KH"].t[:], ZSK.t[:, csl], ED.t[:, csl], ALU.mult, [ZSK.r(), ED.r()], [S["KH"].r()])
                        cp(ARX.t[0:64, 0:2, :], AR.t[0:64, c, :, :], [AR.r()], [ARX.r()], "pool")
                        cp(ARX.t[64:128, 2:4, :], AR.t[64:128, c, :, :], [AR.r()], [ARX.r()], "pool")
                        cp(BTX.t[0:64, 0, :], BT.t[0:64, csl], [BT.r()], [BTX.r()], "pool")
                        cp(BTX.t[64:128, 1, :], BT.t[64:128, csl], [BT.r()], [BTX.r()], "pool")

                    def front_b(c, S):
                        csl = slice(c * 128, (c + 1) * 128)
                        TOK, M1, M3, ZQ, ARX, BTX = S["TOK"], S["M1"], S["M3"], S["ZQ"], S["ARX"], S["BTX"]
                        bk, rb = nb()
                        tr(bk[:, 0:128], AR.t[:, c, 0, :].bitcast(F32), 128, [AR.r()], rb)
                        tr(bk[:, 128:256], ZSV.t[:, csl], 128, [ZSV.r()], rb)
                        tr(bk[:, 256:384], S["BH"].t[:], 128, [S["BH"].r()], rb)
                        tr(bk[:, 384:512], S["KH"].t[:], 128, [S["KH"].r()], rb)
                        cp(TOK.t[:].rearrange("p a b -> p (a b)"), bk[:, :], [rb], [TOK.r()], "act")
                        arx_c = ARX.t[:, :, :].rearrange("p a b -> p (a b)")
                        mu4 = MU2.t[:].rearrange("p (j q t) -> p j q t", j=2, q=2)
                        bk1, rb1 = nb()
                        mm(bk1[:, :], BT.t[:, csl], arx_c, True, True, [BT.r(), ARX.r()], rb1)
                        b14 = bk1[:, :].rearrange("p (j q t) -> p j q t", j=2, q=2)
                        P0 = S["PP"][1]
                        tt(P0.t[:], b14[:, :, 0, :], mu4[:, :, 0, :], ALU.mult, [rb1, MU2.r()], [P0.r()])
                        tt(M1.t[:], b14[:, :, 1, :], mu4[:, :, 1, :], ALU.mult, [rb1, MU2.r()], [M1.r()])
                        bk3, rb3 = nb()
                        mm(bk3[:, :], KTt.t[:, csl], arx_c, True, True, [KTt.r(), ARX.r()], rb3)
                        tt(M3.t[:], bk3[:, :], MU2.t[:], ALU.mult, [rb3, MU2.r()], [M3.r()])
                        bk2, rb2 = nb()
                        mm(bk2[:, 0:256], AR.t[:, c, 0, :], BTX.t[:, :, :].rearrange("p a b -> p (a b)"), True, True, [AR.r(), BTX.r()], rb2)
                        z0 = ZQ[0]
                        tt(z0.t[:, :, 1, :], bk2[:, 0:256].rearrange("p (a b) -> p a b", a=2), MSL.t[:].rearrange("p (a b) -> p a b", a=2),
                           ALU.mult, [rb2, MSL.r()], [z0.r()])
                        S["m3v"] = M3.t[:].rearrange("p (j q t) -> p j q t", j=2, q=2)
                        bkw, rbw = nb()
                        for j in range(2):
                            mm(bkw[:, j * 64:(j + 1) * 64], S["m3v"][:, j, 0, :], TOK.t[:, 1, j * 64:(j + 1) * 64], True, True, [M3.r(), TOK.r()], rbw)
                        cp(z0.t[:, :, 0, 0:64], TOK.t[:, 0, :].rearrange("p (j k) -> p j k", j=2), [TOK.r()], [z0.r()], "pool")
                        cp(z0.t[:, :, 0, 64:128], bkw[:, 0:128].rearrange("p (j k) -> p j k", j=2), [rbw], [z0.r()], "act")
                        S["zc"], S["zn"], S["pcur"] = ZQ[0], ZQ[1], P0

                    def neumann_mm(lv, S):
                        zc_ = S["zc"]
                        bkz, rbz = nb()
                        bkp, rbp = (None, None)
                        if lv < 6:
                            bkp, rbp = nb()
                        nz = 256 if lv < 6 else 128
                        for j in range(2):
                            lhs = S["pcur"].t[:, j, :]
                            lr = S["pcur"].r()
                            mm(bkz[:, j * 256:j * 256 + nz], lhs, zc_.t[:, j, :, :].rearrange("p a b -> p (a b)")[:, 0:nz], True, True,
                               [lr, zc_.r()], rbz)
                            if lv < 6:
                                mm(bkp[:, j * 128:(j + 1) * 128], zc_.t[:, j, 1, :], lhs, True, True, [lr, zc_.r()], rbp)
                        S["banks"] = (bkz, rbz, bkp, rbp)

                    def neumann_ev(lv, S, par):
                        zc_, zn_ = S["zc"], S["zn"]
                        bkz, rbz, bkp, rbp = S["banks"]
                        bz4 = bkz[:, :].rearrange("p (j q t) -> p j q t", j=2, q=2)
                        tt(zn_.t[:, :, 0, :], zc_.t[:, :, 0, :], bz4[:, :, 0, :], ALU.add, [zc_.r(), rbz], [zn_.r()])
                        if lv < 6:
                            cp(zn_.t[:, :, 1, :], bz4[:, :, 1, :], [rbz], [zn_.r()], "act")
                            pn = S["PP"][lv % 2]
                            cp(pn.t[:], bkp[:, 0:256].rearrange("p (j t) -> p j t", j=2), [rbp], [pn.r()], "act" if (lv + par) % 2 else "dve")
                            S["pcur"] = pn
                        S["zc"], S["zn"] = zn_, zc_

                    def chunk_back(c, S):
                        csl = slice(c * 128, (c + 1) * 128)
                        zf = S["zc"]
                        TOK, APC, APT, UT, YTK = S["TOK"], S["APC"], S["APT"], S["UT"], S["YTK"]
                        m3v = S["m3v"]
                        cp(APC.t[:].rearrange("p (j k) -> p j k", j=2), zf.t[:, :, 0, 0:64], [zf.r()], [APC.r()], "dve")
                        bka, rba = nb()
                        tr(bka[:, 0:128], APC.t[:], 128, [APC.r()], rba)
                        cp(APT.t[:], bka[:, 0:128], [rba], [APT.r()], "act")
                        bku, rbu = nb()
                        mm(bku[:, 0:128], APT.t[:], SBLK.t[:, i, :], True, True, [APT.r(), SBLK.r(i)], rbu)
                        tt(UT.t[:].rearrange("p (j k) -> p j k", j=2), bku[:, 0:128].rearrange("p (j k) -> p j k", j=2),
                           zf.t[:, :, 0, 64:128], ALU.add, [rbu, zf.r()], [UT.r()])
                        bky, rby = nb()
                        mm(bky[:, 0:128], AR.t[:, c, 1, :], SBLK.t[:, i, :], True, False, [AR.r(), SBLK.r(i)], rby)
                        for j in range(2):
                            mm(bky[:, j * 64:(j + 1) * 64], S["M1"].t[:, j, :], UT.t[:, j * 64:(j + 1) * 64], False, False, [S["M1"].r(), UT.r()], rby)
                            mm(bky[:, j * 64:(j + 1) * 64], m3v[:, j, 1, :], TOK.t[:, 1, j * 64:(j + 1) * 64], False, j == 1, [S["M3"].r(), TOK.r()], rby)
                        bks, rbs = nb()
                        mm(bks[:, 0:128], TOK.t[:, 2, :], UT.t[:], True, False, [TOK.r(), UT.r()], rbs)
                        mm(bks[:, 0:128], TOK.t[:, 3, :], TOK.t[:, 1, :], False, True, [TOK.r()], rbs)
                        cp(YTK.t[:], bky[:, 0:128], [rby], [YTK.r()], "act")
                        for j in range(2):
                            js = slice(j * 64, (j + 1) * 64)
                            stt(SBLK.t[js, i, js], SBLK.t[js, i, js], DCT.t[js, c:c + 1], bks[js, js], ALU.mult, ALU.add,
                                [SBLK.r(i), DCT.r(), rbs], [SBLK.r(i)])
                        bkt, rbt = nb()
                        tr(bkt[:, 0:128], YTK.t[:], 128, [YTK.r()], rbt)
                        cp(YTP.t[:, csl], bkt[:, 0:128], [rbt], [YTP.r()], "dve")

                    front_a(0, SETS[0])
                    front_a(1, SETS[1])
                    for c0 in (0, 2):
                        front_b(c0, SETS[0])
                        front_b(c0 + 1, SETS[1])
                        if c0 == 0:
                            front_a(2, SETS[0])
                            front_a(3, SETS[1])
                        for lv in range(7):
                            neumann_mm(lv, SETS[0])
                            neumann_mm(lv, SETS[1])
                            warm(WARM_LV)
                            neumann_ev(lv, SETS[0], 0)
                            neumann_ev(lv, SETS[1], 1)
                        chunk_back(c0, SETS[0])
                        chunk_back(c0 + 1, SETS[1])
                    gn_pair(i)
                else:
                    act(E1.t[:], SIG.t[:], AF.Exp, [SIG.r()], [E1.r()], scale=-LAM)
                    bk, rb = nb()
                    for q, src in enumerate((ZSR, ZSK, ZSV, E1)):
                        tr(bk[0:NS, q * 128:(q + 1) * 128], src.t[:], 128, [src.r()], rb)
                    cp(Q6.t[0:NS, 0:4, i * 128:(i + 1) * 128], bk[0:NS, :].rearrange("p (a b) -> p a b", a=4), [rb], [Q6.r()], "act")
                    bk, rb = nb()
                    for q, src in enumerate((KKN, AA)):
                        tr(bk[0:NS, q * 128:(q + 1) * 128], src.t[:], 128, [src.r()], rb)
                    cp(Q6.t[0:NS, 4:6, i * 128:(i + 1) * 128], bk[0:NS, 0:256].rearrange("p (a b) -> p a b", a=2), [rb], [Q6.r()], "act")

            if sample:
                STG = av("stg", [128, 2304])
                for q0 in range(0, 18, 4):
                    bk, rb = nb()
                    n = min(4, 18 - q0)
                    for q in range(n):
                        tr(bk[0:NS, q * 128:(q + 1) * 128], ZB.t[:, q0 + q, H0:H0 + NS], 128, [ZB.r(q0 + q)], rb)
                    cp(STG.t[0:NS, q0 * 128:(q0 + n) * 128], bk[0:NS, 0:n * 128], [rb], [STG.r()], ev_eng())
                P.dma("sp", O["pools"][:, 14, :], STG.t[0:NS, 0:512], reads=[STG.r()], owner=STG.r(), final=True)
                P.dma("sp", O["shifts"], STG.t[0:NS, 512:2304], reads=[STG.r()], owner=STG.r(), final=True)
                P.barrier(pool=True)
                arelease(AMARK)
                SST = av("sst", [128, 64, 64]); ST1 = av("st1", [128, 64, 64]); ST2 = av("st2", [128, 64, 64])
                S1 = P.res("scr1")
                P.dma("sp", scr1.rearrange("q b f -> b q f"), Q6.t[0:NS, :, :], reads=[Q6.r()], writes=[S1], owner=S1)
                P.dma("sp", QR.t[:], scr1.rearrange("q b (h k) -> (b h) q k", k=64), reads=[S1], writes=[QR.r()], owner=QR.r())
                P.dma("sp", SST.t[:].rearrange("p a b -> p (a b)"), I["swkv"], writes=[SST.r()], owner=SST.r())
                bc_k = lambda q: QR.t[:, q, :].unsqueeze(1).to_broadcast([128, 64, 64])
                bc_v = lambda ap: ap.unsqueeze(2).to_broadcast([128, 64, 64])
                tt(ST1.t[:], SST.t[:], bc_k(4), ALU.mult, [SST.r(), QR.r()], [ST1.r()])
                P.op("dve", lambda e: e.tensor_reduce(out=SAV.t[:], in_=ST1.t[:], axis=AX.X, op=ALU.add, negate=True), reads=[ST1.r()], writes=[SAV.r()])
                tt(ST2.t[:], SST.t[:], bc_k(3), ALU.mult, [SST.r(), QR.r()], [ST2.r()])
                tt(ST1.t[:], bc_v(SAV.t[:]), bc_k(5), ALU.mult, [SAV.r(), QR.r()], [ST1.r()])
                tt(ST2.t[:], ST2.t[:], ST1.t[:], ALU.add, [ST2.r(), ST1.r()], [ST2.r()])
                tt(ST1.t[:], bc_v(QR.t[:, 2, :]), bc_k(1), ALU.mult, [QR.r()], [ST1.r()])
                tt(ST2.t[:], ST2.t[:], ST1.t[:], ALU.add, [ST2.r(), ST1.r()], [ST2.r()])
                P.dma("sp", O["wkvs"], ST2.t[:].rearrange("p a b -> p (a b)"), reads=[ST2.r()], owner=ST2.r(), final=True)
                tt(ST1.t[:], ST2.t[:], bc_k(0), ALU.mult, [ST2.r(), QR.r()], [ST1.r()])
                P.op("dve", lambda e: e.tensor_reduce(out=YV.t[:], in_=ST1.t[:], axis=AX.X, op=ALU.add), reads=[ST1.r()], writes=[YV.r()])
                S2 = P.res("scr2")
                P.dma("sp", scr2.rearrange("b (h k) -> (b h) k", k=64), YV.t[:], reads=[YV.r()], writes=[S2], owner=S2)
                P.dma("sp", YTS.t[0:NS, :], scr2, reads=[S2], writes=[YTS.r()], owner=YTS.r())
                for i in range(4):
                    bk, rb = nb()
                    tr(bk[:, 0:NS], YTS.t[0:NS, i * 128:(i + 1) * 128], NS, [YTS.r()], rb)
                    cp(YTP.t[:], bk[:, 0:NS], [rb], [YTP.r()], "dve")
                    prep_pair(i, True)
                    gn_pair(i)

            proj_down(Tn, I["w_out_ab"], 8, lambda k: (YC.t[:, k, :], YC.r(k)))

            if not sample:
                cp(HIS.t[:], ZB.t[:, :, H0 + Tn - 16:H0 + Tn], ZB.res, [HIS.r()], "dve")
                if blk == 3:
                    STP = T(SIG.t[:], SIG.res); STS = T(CSUM.t[:, 0:128], CSUM.res); WK = T(AA.t[:].rearrange("p (a b) -> p a b", a=4), AA.res)
                    bk, rb = nb()
                    for g in range(4):
                        tr(bk[0:15, g * 128:(g + 1) * 128], ZB.t[:, g, H0 + Tn - 15:H0 + Tn], 128, [ZB.r(g)], rb)
                    cp(STP.t[0:15, :], bk[0:15, :], [rb], [STP.r()], "dve")
                    P.dma("sp", O["poolp"], STP.t[0:15, :], reads=[STP.r()], owner=STP.r(), final=True)
                    bk, rb = nb()
                    tr(bk[0:14, 0:128], ZB.t[:, 4:18, H0 + Tn - 1], 128, ZB.res, rb)
                    cp(STS.t[0:14, :], bk[0:14, 0:128], [rb], [STS.r()], "dve")
                    P.dma("sp", O["shiftp"], STS.t[0:14, :], reads=[STS.r()], owner=STS.r(), final=True)
                    for i in range(4):
                        bk, rb = nb()
                        tr(bk[:, 0:128], SBLK.t[:, i, :].bitcast(F32), 128, [SBLK.r(i)], rb)
                        cp(WK.t[:, i, :], bk[:, 0:128], [rb], [WK.r()], ev_eng())
                    wv = O["wkvp"].rearrange("(i j) v k -> j v i k", j=2)
                    for j in range(2):
                        js = slice(j * 64, (j + 1) * 64)
                        P.dma("sp", wv[j], WK.t[js, :, js], reads=[WK.r()], owner=WK.r(), final=True)

        def mixer_c(blk, sample, Tn, nch, CH):
            UU = av("uT", [128, 16, Tn], BF16, 16)
            VT = [av("vtok%d" % a, [128, 2048], F32R) for a in range(nch)]
            TMPS = [av("sptmp%d" % k, [128, Tn]) for k in range(2)]
            if not sample:
                CC = av("cc", [128, 16, 128]); BSB = av("bsb", [128, 512]); ONF = av("onf", [128, 128], F32R)
                P.dma("sp", BSB.t[:], I["bsb"], writes=[BSB.r()], owner=BSB.r())
                fill_r(ONF.t[:], 1.0, [ONF.r()])
                for g in range(4):
                    bk, rb = nb()
                    mm(bk[:, 0:128], ONF.t[:], WMT.t[:, g, :], True, True, [ONF.r(), WMT.r()], rb)
                    RS = av("rs%d" % g, [128, 128])
                    cp(RS.t[:], bk[:, 0:128], [rb], [RS.r()])
                    for q in range(4):
                        dcq = g * 4 + q
                        stt(CC.t[:, dcq, :], RS.t[:], prm("ln_b", dcq), BSB.t[:, g * 128:(g + 1) * 128], ALU.mult, ALU.add,
                            [RS.r(), PRM.r(), BSB.r()], [CC.r()])
            for cgp in range(4):
                s = wload(I["w_in_c"][:, 2048 + cgp * 512:2048 + (cgp + 1) * 512], 512)
                for a in range(nch):
                    bk, rb = nb()
                    for kc in range(8):
                        mm(bk[0:CH, :], CUR["HT"].t[:, kc, a * CH:(a + 1) * CH], s.t[:, kc, :], kc == 0, kc == 7, [s.r(), CUR["HT"].r(kc)], rb)
                    act(VT[a].t[0:CH, cgp * 512:(cgp + 1) * 512], bk[0:CH, :], AF.Gelu, [rb], [VT[a].r()])
            for a in range(nch):
                ST6 = av("st6_%d" % a, [128, 24]); MV = av("mvv%d" % a, [128, 2]); RSD = av("rsd%d" % a, [128, 1])
                vf = VT[a].t[0:CH, :].bitcast(F32)
                for q in range(4):
                    P.op("dve", lambda e, q=q, vf=vf, ST6=ST6: e.bn_stats(out=ST6.t[0:CH, q * 6:(q + 1) * 6], in_=vf[:, q * 512:(q + 1) * 512]),
                         reads=[VT[a].r()], writes=[ST6.r()])
                P.op("dve", lambda e, ST6=ST6, MV=MV: e.bn_aggr(out=MV.t[0:CH, :], in_=ST6.t[0:CH, :]), reads=[ST6.r()], writes=[MV.r()])
                act(RSD.t[0:CH, :], MV.t[0:CH, 1:2], AF.Ln, [MV.r()], [RSD.r()], bias=LN_EPS)
                act(RSD.t[0:CH, :], RSD.t[0:CH, :], AF.Exp, [RSD.r()], [RSD.r()], scale=-0.5)
                ts(VT[a].t[0:CH, :], vf, MV.t[0:CH, 0:1], RSD.t[0:CH, 0:1], ALU.subtract, ALU.mult, [VT[a].r(), MV.r(), RSD.r()], [VT[a].r()])
            if sample:
                LGB = av("lgb", [NS, 2048]); LBB = av("lbb", [NS, 2048]); SV = av("sv", [NS, 2048])
                P.dma("sp", LGB.t[:], I["lngb"], writes=[LGB.r()], owner=LGB.r())
                P.dma("sp", LBB.t[:], I["lnbb"], writes=[LBB.r()], owner=LBB.r())
                tt(SV.t[:], VT[0].t[0:NS, :].bitcast(F32), LGB.t[:], ALU.mult, [VT[0].r(), LGB.r()], [SV.r()])
                tt(SV.t[:], SV.t[:], LBB.t[:], ALU.add, [SV.r(), LBB.r()], [SV.r()])
                P.dma("sp", O["sguv"], SV.t[:], reads=[SV.r()], owner=SV.r(), final=True)

            def ev_u(m, bk, rb):
                act(UU.t[:, m, :], bk, AF.Gelu, [rb], [UU.r(m)])
            proj_fm(Tn, I["w_in_c"][:, 0:2048], 8, 2048, hrhs(Tn), ev_u)
            for dc in range(16):
                g = dc // 4
                bk, rb = nb()
                if sample:
                    mm(bk[:, 0:NS], VT[0].t[0:NS, dc * 128:(dc + 1) * 128], W00I.t[:, g, :], True, True, [VT[0].r(), W00I.r()], rb)
                else:
                    for a in range(nch):
                        mm(bk[:, a * 128:(a + 1) * 128], VT[a].t[:, dc * 128:(dc + 1) * 128], WMT.t[:, g, :], True, True, [VT[a].r(), WMT.r()], rb)
                t = TMPS[dc % 2]
                if sample:
                    ts(t.t[:], bk[:, 0:NS], prm("ln_g", dc), CS16.t[:, dc:dc + 1], ALU.mult, ALU.add, [rb, PRM.r(), CS16.r()], [t.r()])
                else:
                    stt(t.t[:].rearrange("p (a i) -> p a i", a=4), bk[:, :].rearrange("p (a i) -> p a i", a=4), prm("ln_g", dc),
                        CC.t[:, dc, :].unsqueeze(1).to_broadcast([128, 4, 128]), ALU.mult, ALU.add, [rb, PRM.r(), CC.r()], [t.r()])
                tt(UU.t[:, dc, :], t.t[:], UU.t[:, dc, :], ALU.mult, [t.r(), UU.r(dc)], [UU.r(dc)])
            proj_down(Tn, I["w_out_c"], 16, lambda k: (UU.t[:, k, :], UU.r(k)))

        def xattn_prompt(layer):
            QT = av("qT", [128, 8, 512], BF16, 8)

            def ev_q(m, bk, rb):
                cp(QT.t[:, m, :], bk, [rb], [QT.r(m)], ev_eng())
            proj_fm(512, I["w_xq"][layer], 8, 1024, hrhs(512), ev_q)
            PT = av("pT", [128, 4, 2, 512], BF16, 4)
            OT = av("oT", [128, 8, 512], BF16, 8)
            ASET = [dict(PF=av("pf%d" % k_, [128, 4, 256]), MX=av("mx%d" % k_, [128, 4]), NMX=av("nmx%d" % k_, [128, 4]),
                         SSUM=av("ssum%d" % k_, [128, 4]), RSUM=av("rsum%d" % k_, [128, 4])) for k_ in range(2)]

            def att_scores(c, S):
                banks = []
                for hp in range(2):
                    bk, rb = nb()
                    banks.append((bk, rb))
                    for hh in range(2):
                        hd = hp * 2 + hh
                        for dd in range(2):
                            mm(bk[:, hh * 256:(hh + 1) * 256], QT.t[:, 2 * hd + dd, c * 128:(c + 1) * 128], KT[layer].t[:, 2 * hd + dd, :],
                               dd == 0, dd == 1, [QT.r(2 * hd + dd), KT[layer].r()], rb)
                S["banks"] = banks

            def att_softmax(c, S):
                PF, MX, NMX, SSUM, RSUM = S["PF"], S["MX"], S["NMX"], S["SSUM"], S["RSUM"]
                for hp in range(2):
                    bk, rb = S["banks"][hp]
                    P.op("dve", lambda e, bk=bk, hp=hp, MX=MX: e.tensor_reduce(out=MX.t[:, hp * 2:hp * 2 + 2], in_=bk[:, :].rearrange("p (h m) -> p h m", h=2),
                                                                              axis=AX.X, op=ALU.max), reads=[rb], writes=[MX.r()])
                ts(NMX.t[:], MX.t[:], -1.0 / 16.0, None, ALU.mult, None, [MX.r()], [NMX.r()])
                for hd in range(4):
                    bk, rb = S["banks"][hd // 2]
                    act(PF.t[:, hd, :], bk[:, (hd % 2) * 256:(hd % 2 + 1) * 256], AF.Exp, [rb, NMX.r()], [PF.r(), SSUM.r()],
                        bias=NMX.t[:, hd:hd + 1], scale=1.0 / 16.0, accum=SSUM.t[:, hd:hd + 1])
                P.op("dve", lambda e, RSUM=RSUM, SSUM=SSUM: e.reciprocal(out=RSUM.t[:], in_=SSUM.t[:]), reads=[SSUM.r()], writes=[RSUM.r()])
                tt(PF.t[:], PF.t[:], RSUM.t[:].unsqueeze(2).to_broadcast([128, 4, 256]), ALU.mult, [PF.r(), RSUM.r()], [PF.r()])

            def att_transposes(c, S):
                PF = S["PF"]
                for hp in range(2):
                    bk, rb = nb()
                    for hh in range(2):
                        hd = hp * 2 + hh
                        for mc in range(2):
                            k = hh * 2 + mc
                            tr(bk[:, k * 128:(k + 1) * 128], PF.t[:, hd, mc * 128:(mc + 1) * 128], 128, [PF.r()], rb)
                    cp(PT.t[:, hp * 2:hp * 2 + 2, :, c * 128:(c + 1) * 128], bk[:, :].rearrange("p (h m t) -> p h m t", h=2, m=2),
                       [rb], [PT.r(hp * 2), PT.r(hp * 2 + 1)], ev_eng())

            att_scores(0, ASET[0])
            for c in range(4):
                if c + 1 < 4:
                    att_scores(c + 1, ASET[(c + 1) % 2])
                att_softmax(c, ASET[c % 2])
                att_transposes(c, ASET[c % 2])
            for j in range(8):
                hd = j // 2
                bk, rb = nb()
                for mc in range(2):
                    mm(bk[:, :], VM[layer].t[:, mc, j * 128:(j + 1) * 128], PT.t[:, hd, mc, :], mc == 0, mc == 1, [VM[layer].r(), PT.r(hd)], rb)
                cp(OT.t[:, j, :], bk[:, :], [rb], [OT.r(j)], ev_eng())
            proj_down(512, I["w_xo"][layer], 8, lambda k: (OT.t[:, k, :], OT.r(k)))

        def xattn_sample(layer):
            QTOK = av("qtok", [NS, 1024], BF16)
            SELA = av("sela", [16, 16, 128], BF16)
            DEL = av("del", [128, 16, 16])
            cp(DEL.t[:], ONES.t[:, 0:16].unsqueeze(2).to_broadcast([128, 16, 16]), [ONES.r()], [DEL.r()])
            P.op("pool", lambda e: e.affine_select(out=DEL.t[:], in_=DEL.t[:], pattern=[[1, 16], [-1, 16]], compare_op=ALU.is_equal,
                                                    fill=0.0, base=0, channel_multiplier=0), reads=[DEL.r()], writes=[DEL.r()])
            cp(SELA.t[:], IDN.t[0:16, 0:16].unsqueeze(2).to_broadcast([16, 16, 128]), [IDN.r()], [SELA.r()])
            for cg in range(2):
                s = wload(I["w_xq"][layer][:, cg * 512:(cg + 1) * 512], 512)
                bk, rb = nb()
                for kc in range(8):
                    mm(bk[0:NS, :], CUR["HT"].t[:, kc, 0:NS], s.t[:, kc, :], kc == 0, kc == 7, [s.r(), CUR["HT"].r(kc)], rb)
                cp(QTOK.t[:, cg * 512:(cg + 1) * 512], bk[0:NS, :], [rb], [QTOK.r()], ev_eng())
            KB_ = [av("kb%d" % k, [128, 2, 1024]) for k in range(2)]
            VB_ = [av("vb%d" % k, [128, 2, 1024], BF16) for k in range(2)]
            PRD = av("prd", [128, 1024])
            SC = av("sc", [128, 2, 64])
            for b in range(NS):
                kb = KB_[b % 2]
                P.dma("sp", kb.t[:], I["ck"][layer, b].rearrange("(a p) f -> p a f", p=128), writes=[kb.r()], owner=kb.r())
                bq = [nb(), nb()]
                for cg in range(2):
                    mm(bq[cg][0][:, :], SELA.t[:, b, :], QTOK.t[:, cg * 512:(cg + 1) * 512], True, True, [SELA.r(), QTOK.r()], bq[cg][1])
                for mc in range(2):
                    for cg in range(2):
                        tt(PRD.t[:, cg * 512:(cg + 1) * 512], kb.t[:, mc, cg * 512:(cg + 1) * 512], bq[cg][0][:, :], ALU.mult,
                           [kb.r(), bq[cg][1]], [PRD.r()])
                    P.op("dve", lambda e, mc=mc, b=b: e.tensor_reduce(out=SC.t[:, mc, b * 4:(b + 1) * 4], in_=PRD.t[:].rearrange("p (h d) -> p h d", h=4),
                                                                      axis=AX.X, op=ALU.add), reads=[PRD.r()], writes=[SC.r()])
            MX = av("mxs", [128, 1]); NMX = av("nmxs", [128, 1]); SSUM = av("ssums", [128, 1]); RSUM = av("rsums", [128, 1])
            PSM = av("psm", [128, 256]); PTS = av("pts", [128, 2, 64])
            PEX = av("pex", [128, 2, 16, 4, 16], BF16)
            bk, rb = nb()
            for mc in range(2):
                tr(bk[0:64, mc * 128:(mc + 1) * 128], SC.t[:, mc, :], 128, [SC.r()], rb)
            P.op("dve", lambda e: e.tensor_reduce(out=MX.t[0:64, :], in_=bk[0:64, 0:256], axis=AX.X, op=ALU.max), reads=[rb], writes=[MX.r()])
            ts(NMX.t[0:64, :], MX.t[0:64, :], -1.0 / 16.0, None, ALU.mult, None, [MX.r()], [NMX.r()])
            act(PSM.t[0:64, :], bk[0:64, 0:256], AF.Exp, [rb, NMX.r()], [PSM.r(), SSUM.r()], bias=NMX.t[0:64, 0:1], scale=1.0 / 16.0,
                accum=SSUM.t[0:64, 0:1])
            P.op("dve", lambda e: e.reciprocal(out=RSUM.t[0:64, :], in_=SSUM.t[0:64, :]), reads=[SSUM.r()], writes=[RSUM.r()])
            ts(PSM.t[0:64, :], PSM.t[0:64, :], RSUM.t[0:64, 0:1], None, ALU.mult, None, [PSM.r(), RSUM.r()], [PSM.r()])
            bk, rb = nb()
            for mc in range(2):
                tr(bk[:, mc * 64:(mc + 1) * 64], PSM.t[0:64, mc * 128:(mc + 1) * 128], 64, [PSM.r()], rb)
            cp(PTS.t[:].rearrange("p a b -> p (a b)"), bk[:, 0:128], [rb], [PTS.r()], "dve")
            for mc in range(2):
                tt(PEX.t[:, mc], PTS.t[:, mc, :].rearrange("p (b h) -> p b h", h=4).unsqueeze(3).to_broadcast([128, 16, 4, 16]),
                   DEL.t[:].unsqueeze(2).to_broadcast([128, 16, 4, 16]), ALU.mult, [PTS.r(), DEL.r()], [PEX.r()])
            bo = [nb() for _ in range(4)]
            for b in range(NS):
                vb = VB_[b % 2]
                P.dma("pool", vb.t[:], I["cv"][layer, b].rearrange("(a p) f -> p a f", p=128), writes=[vb.r()], owner=vb.r())
                for h in range(4):
                    for mc in range(2):
                        mm(bo[h][0][0:NS, 0:256], PEX.t[:, mc, b, h, :], vb.t[:, mc, h * 256:(h + 1) * 256], b == 0 and mc == 0,
                           b == NS - 1 and mc == 1, [PEX.r(), vb.r()], bo[h][1])
            OTOK = av("otok", [NS, 1024])
            for h in range(4):
                cp(OTOK.t[:, h * 256:(h + 1) * 256], bo[h][0][0:NS, 0:256], [bo[h][1]], [OTOK.r()], ev_eng())
            OTS = av("oTs", [128, 8, NS], BF16, 8)
            bk, rb = nb()
            for j in range(8):
                tr(bk[:, j * NS:(j + 1) * NS], OTOK.t[:, j * 128:(j + 1) * 128], NS, [OTOK.r()], rb)
            cp(OTS.t[:], bk[:, 0:8 * NS].rearrange("p (j b) -> p j b", j=8), [rb], OTS.res, "dve")
            proj_down(NS, I["w_xo"][layer], 8, lambda k: (OTS.t[:, k, :], OTS.r(k)))

        try:
            stage("memkv")
            for blk in range(3):
                run_block(blk, False)
                stage("b%d" % blk)
            run_joint()
        except StopBuild:
            pass
        P.finish("sp")
        P.emit()
    return nc


def _fm(v):
    v = np.asarray(v, np.float32).reshape(-1, 128)
    return np.ascontiguousarray(v.T)


_NC_CACHE = {}
STOP_AT = None


class StopBuild(Exception):
    pass


def stage(name):
    if STOP_AT is not None and name == STOP_AT:
        raise StopBuild()
_PREP_ONLY = False


def kernel(x_prompt, x_sample, mem_prompt, cache_mem_k, cache_mem_v, state_pool, state_shift, state_wkv,
           norm_mix_g, norm_xa_g, norm_mem_g, norm_ffn_g, norm_final_g,
           w_in_ab, w_out_ab, pool_w, pool_scale,
           rwkv_mu, rwkv_w0, rwkv_w2, rwkv_a0, rwkv_a2, rwkv_g2, rwkv_k_k, rwkv_k_a, rwkv_r_k,
           rwkv_gn_g, rwkv_gn_b,
           w_in_c, sgu_ln_g, sgu_ln_b, sgu_w_s, sgu_b_s, w_out_c,
           w_xq, w_xk, w_xv, w_xo, w_ff_up, w_ff_down):
    f = lambda a: np.ascontiguousarray(np.asarray(a, dtype=np.float32))
    cols = [_fm(norm_mix_g[0]), _fm(norm_xa_g[0]), _fm(norm_ffn_g[0]), _fm(norm_mix_g[1]), _fm(norm_xa_g[1]), _fm(norm_ffn_g[1]),
            _fm(norm_mem_g[0]), _fm(norm_mem_g[1]), _fm(norm_final_g), _fm(pool_scale[0]), _fm(rwkv_mu[0]), _fm(rwkv_w0[0]),
            _fm(rwkv_a0[0]), _fm(rwkv_k_k[0]), _fm(rwkv_k_a[0]), _fm(np.asarray(rwkv_r_k[0]).reshape(-1)), _fm(rwkv_gn_g[0]),
            _fm(rwkv_gn_b[0]), _fm(sgu_ln_g[0]), _fm(sgu_ln_b[0])]
    prm = np.ascontiguousarray(np.concatenate(cols, axis=1))
    assert prm.shape == (128, NPRM)
    ws = np.asarray(sgu_w_s[0], np.float32)
    bs = np.asarray(sgu_b_s[0], np.float32)
    shared = dict(
        prm=prm,
        bsb=np.ascontiguousarray(np.broadcast_to(bs.reshape(1, 512), (128, 512))),
        smp=np.ascontiguousarray(np.broadcast_to(np.concatenate([ws[:, 0, 0], bs[:, 0]]).reshape(1, 8), (128, 8))),
        lngb=np.ascontiguousarray(np.broadcast_to(np.asarray(sgu_ln_g[0], np.float32).reshape(1, 2048), (NS, 2048))),
        lnbb=np.ascontiguousarray(np.broadcast_to(np.asarray(sgu_ln_b[0], np.float32).reshape(1, 2048), (NS, 2048))),
        w_in_ab=f(w_in_ab[0]), w_out_ab=f(w_out_ab[0]),
        pool_w=np.ascontiguousarray(np.transpose(np.asarray(pool_w[0], np.float32), (1, 0, 2))),
        w2=f(rwkv_w2[0]), a2=f(rwkv_a2[0]), g2=f(rwkv_g2[0]),
        w_in_c=f(w_in_c[0]), wsT=np.ascontiguousarray(np.transpose(ws, (2, 0, 1))),
        w_out_c=f(w_out_c[0]),
        w_xq=f(w_xq), w_xk=f(w_xk), w_xv=f(w_xv), w_xo=f(w_xo), w_ff_up=f(w_ff_up), w_ff_down=f(w_ff_down),
    )
    xp = f(x_prompt); xs = f(x_sample); mp = f(mem_prompt); ck = f(cache_mem_k); cv = f(cache_mem_v)
    sp = f(state_pool); ssh = f(state_shift); swk = f(state_wkv)
    in_maps = []
    for c in range(NCORES):
        sl = slice(c * NS, (c + 1) * NS)
        m = dict(shared)
        m.update(
            xp=xp[c], xs=np.ascontiguousarray(xs[sl, 0, :]), memp=mp[c],
            ck=np.ascontiguousarray(ck[:, sl].reshape(2, NS, 256, D)), cv=np.ascontiguousarray(cv[:, sl].reshape(2, NS, 256, D)),
            spool=np.ascontiguousarray(sp[0, sl]), sshift=np.ascontiguousarray(ssh[0, sl]),
            swkv=np.ascontiguousarray(swk[0, sl].reshape(128, 4096)),
        )
        in_maps.append(m)
    if _PREP_ONLY:
        return in_maps
    if "nc" not in _NC_CACHE:
        _NC_CACHE["nc"] = build_nc()
    nc = _NC_CACHE["nc"]
    res = run_bass_kernel_spmd(nc, in_maps, core_ids=list(range(NCORES))).results
    return _gather(res)


def _gather(res):
    B = NCORES
    y_prompt = np.stack([res[c]["yp"] for c in range(B)]).astype(np.float32)
    y_sample = np.concatenate([res[c]["ys"] for c in range(B)]).reshape(128, 1, D).astype(np.float32)
    mem_k = np.stack([res[c]["mk"] for c in range(B)], axis=1).reshape(2, B, 256, 4, 256).astype(np.float32)
    mem_v = np.stack([res[c]["mv"] for c in range(B)], axis=1).reshape(2, B, 256, 4, 256).astype(np.float32)
    pool_p = np.stack([res[c]["poolp"] for c in range(B)])[None].astype(np.float32)
    pool_s = np.concatenate([res[c]["pools"] for c in range(B)])[None].astype(np.float32)
    shift_p = np.stack([res[c]["shiftp"].reshape(1792) for c in range(B)])[None].astype(np.float32)
    shift_s = np.concatenate([res[c]["shifts"] for c in range(B)])[None].astype(np.float32)
    wkv_p = np.stack([res[c]["wkvp"] for c in range(B)])[None].astype(np.float32)
    wkv_s = np.concatenate([res[c]["wkvs"].reshape(NS, 8, 64, 64) for c in range(B)])[None].astype(np.float32)
    sgu_v = np.concatenate([res[c]["sguv"] for c in range(B)]).reshape(1, 128, 1, 2048).astype(np.float32)
    return (y_prompt, y_sample, mem_k, mem_v, pool_p, pool_s, shift_p, shift_s, wkv_p, wkv_s, sgu_v)
```

```python
from contextlib import ExitStack
import numpy as np
import concourse.bass as bass
import concourse.mybir as mybir
from concourse.bass_utils import run_bass_kernel_spmd

F32 = mybir.dt.float32
F32R = mybir.dt.float32r
BF16 = mybir.dt.bfloat16
AF = mybir.ActivationFunctionType
ALU = mybir.AluOpType
AX = mybir.AxisListType

NCORES = 8
D = 1024
SEQ = 2048
NS = 16
LAM = 0.6065306597126334
WARM_LV = 2
WARM_PREP = 24
RMS_EPS = 1e-5
LN_EPS = 1e-5
GN_EPS = 64 * 1e-5
SAME_ENGINE_SYNC = True

PC = {}
_o = 0
for _n, _c in [("g_mix0", 8), ("g_xa0", 8), ("g_ffn0", 8), ("g_mix1", 8), ("g_xa1", 8), ("g_ffn1", 8),
               ("g_mem0", 8), ("g_mem1", 8), ("g_fin", 8), ("pscale", 4), ("mu", 14), ("w0", 4), ("a0", 4),
               ("k_k", 4), ("k_a", 4), ("r_k", 4), ("gn_g", 4), ("gn_b", 4), ("ln_g", 16), ("ln_b", 16)]:
    PC[_n] = _o
    _o += _c
NPRM = _o


class Res:
    __slots__ = ("name", "w", "r", "dsem", "dcnt", "psum")

    def __init__(self, name):
        self.name = name
        self.psum = False
        self.w = None
        self.r = {}
        self.dsem = None
        self.dcnt = 0


class Prog:
    ENG = ("pe", "act", "dve", "pool", "sp")

    def __init__(self, nc, stack):
        self.nc = nc
        self.stack = stack
        self.q = {e: [] for e in self.ENG}
        self.sems = {}
        self.cnt = {e: 0 for e in self.ENG}
        self.seen = {e: {} for e in self.ENG}
        self.nsem = 0
        for e in self.ENG:
            self.sems[e] = stack.enter_context(nc.semaphore("s_" + e))
            self.nsem += 1
        self.final = {}
        self.alld = {}

    def res(self, name):
        return Res(name)

    def _dsem(self, r):
        if r.dsem is None:
            key = "d%d" % self.nsem
            self.sems[key] = self.stack.enter_context(self.nc.semaphore(key))
            self.nsem += 1
            r.dsem = key
        return r.dsem

    def _need(self, e, reads, writes):
        need = {}

        def add(tok):
            if tok is None:
                return
            k, v = tok
            if need.get(k, 0) < v:
                need[k] = v
        for r in reads:
            add(r.w)
            if r.psum:
                for k, v in r.r.items():
                    if k != e:
                        add((k, v))
        for w in writes:
            add(w.w)
            for k, v in w.r.items():
                add((k, v))
        out = []
        for k, v in need.items():
            if k == e and (e == "pe" or not SAME_ENGINE_SYNC):
                continue
            if self.seen[e].get(k, 0) < v:
                self.seen[e][k] = v
                out.append((k, v))
        return out

    def _emit_waits(self, e, waits):
        for k, v in waits:
            sem = self.sems[k]
            self.q[e].append(lambda eng, sem=sem, v=v: eng.wait_ge(sem, v))

    def _mark(self, tok, reads, writes):
        k, v = tok
        for r in reads:
            if r.r.get(k, 0) < v:
                r.r[k] = v
        for w in writes:
            w.w = tok
            w.r = {}

    def op(self, e, fn, reads=(), writes=()):
        self._emit_waits(e, self._need(e, reads, writes))
        self.cnt[e] += 1
        sem = self.sems[e]
        self.q[e].append(lambda eng, fn=fn, sem=sem: fn(eng).then_inc(sem, 1))
        self._mark((e, self.cnt[e]), reads, writes)

    def dma(self, e, out, in_, reads=(), writes=(), owner=None, final=False, track=True, **kw):
        self._emit_waits(e, self._need(e, reads, writes))
        key = self._dsem(owner)
        owner.dcnt += 16
        sem = self.sems[key]
        self.q[e].append(lambda eng, out=out, in_=in_, sem=sem, kw=kw:
                         eng.dma_start(out=out, in_=in_, **kw).then_inc(sem, 16))
        self._mark((key, owner.dcnt), reads, writes)
        if final:
            self.final[key] = owner.dcnt
        if track:
            self.alld[key] = owner.dcnt

    def barrier_real(self, pool=True):
        engs = ("pe", "act", "dve", "pool", "sp") if pool else ("pe", "act", "dve", "sp")
        e0 = "sp"
        w = []
        for k, v in self.alld.items():
            if self.seen[e0].get(k, 0) < v:
                self.seen[e0][k] = v
                w.append((k, v))
        for o in engs:
            if o != e0 and self.seen[e0].get(o, 0) < self.cnt[o]:
                self.seen[e0][o] = self.cnt[o]
                w.append((o, self.cnt[o]))
        self._emit_waits(e0, w)
        self.cnt[e0] += 1
        sem = self.sems[e0]
        self.q[e0].append(lambda eng, sem=sem: eng.nop().then_inc(sem, 1))
        for o in engs:
            if o == e0:
                continue
            self.seen[o][e0] = self.cnt[e0]
            self._emit_waits(o, [(e0, self.cnt[e0])])
            for o2 in engs:
                self.seen[o][o2] = max(self.seen[o].get(o2, 0), self.cnt[o2])
            for k, v in self.alld.items():
                self.seen[o][k] = max(self.seen[o].get(k, 0), v)

    def barrier(self, pool=True):
        return

    def finish(self, e="sp"):
        for k, v in self.final.items():
            sem = self.sems[k]
            self.q[e].append(lambda eng, sem=sem, v=v: eng.wait_ge(sem, v))

    def emit(self):
        with self.nc.Block() as blk:
            def run(name):
                def f(eng):
                    for g in self.q[name]:
                        g(eng)
                return f
            blk.tensor(run("pe"))
            blk.scalar(run("act"))
            blk.vector(run("dve"))
            blk.gpsimd(run("pool"))
            blk.sync(run("sp"))


class T:
    def __init__(self, t, res):
        self.t = t
        self.res = res

    def r(self, i=0):
        return self.res[i if len(self.res) > 1 else 0]


def build_nc():
    nc = bass.Bass("TRN2", target_bir_lowering=False)

    def din(name, shape):
        return nc.dram_tensor(name, list(shape), F32, kind="ExternalInput").ap()

    def dout(name, shape):
        return nc.dram_tensor(name, list(shape), F32, kind="ExternalOutput").ap()

    I = dict(
        xp=din("xp", [SEQ, D]), xs=din("xs", [NS, D]), memp=din("memp", [256, D]),
        ck=din("ck", [2, NS, 256, D]), cv=din("cv", [2, NS, 256, D]),
        spool=din("spool", [NS, 15, 512]), sshift=din("sshift", [NS, 1792]), swkv=din("swkv", [128, 4096]),
        prm=din("prm", [128, NPRM]), bsb=din("bsb", [128, 512]), smp=din("smp", [128, 8]),
        lngb=din("lngb", [NS, 2048]), lnbb=din("lnbb", [NS, 2048]),
        w_in_ab=din("w_in_ab", [D, 2304]), w_out_ab=din("w_out_ab", [D, D]),
        pool_w=din("pool_w", [128, 4, 128]), w2=din("w2", [64, 512]), a2=din("a2", [64, 512]), g2=din("g2", [128, 512]),
        w_in_c=din("w_in_c", [D, 4096]), wsT=din("wsT", [128, 4, 128]), w_out_c=din("w_out_c", [2048, D]),
        w_xq=din("w_xq", [2, D, D]), w_xk=din("w_xk", [2, D, D]), w_xv=din("w_xv", [2, D, D]), w_xo=din("w_xo", [2, D, D]),
        w_ff_up=din("w_ff_up", [2, D, 4096]), w_ff_down=din("w_ff_down", [2, 4096, D]),
    )
    O = dict(
        yp=dout("yp", [SEQ, D]), ys=dout("ys", [NS, D]), mk=dout("mk", [2, 256, D]), mv=dout("mv", [2, 256, D]),
        poolp=dout("poolp", [15, 512]), pools=dout("pools", [NS, 15, 512]),
        shiftp=dout("shiftp", [14, 128]), shifts=dout("shifts", [NS, 1792]),
        wkvp=dout("wkvp", [8, 64, 64]), wkvs=dout("wkvs", [128, 4096]), sguv=dout("sguv", [NS, 2048]),
    )
    scr1 = nc.dram_tensor("scr1", [6, NS, 512], F32, kind="Internal").ap()
    scr2 = nc.dram_tensor("scr2", [NS, 512], F32, kind="Internal").ap()

    with ExitStack() as st:
        P = Prog(nc, st)

        def sb(name, shape, dt=F32, nres=1):
            t = st.enter_context(nc.sbuf_tensor("s_" + name, list(shape), dt))
            return T(t, [P.res("%s%d" % (name, i)) for i in range(nres)])

        PS = [st.enter_context(nc.psum_tensor("ps%d" % i, [128, 512], F32)) for i in range(8)]
        RPS = [P.res("ps%d" % i) for i in range(8)]
        for r_ in RPS:
            r_.psum = True
        pctr = [0]

        NBANK = [8]

        def nb():
            i = pctr[0] % NBANK[0]
            pctr[0] += 1
            return PS[i], RPS[i]

        def warm(n):
            for _ in range(n):
                P.op("pe", lambda e: e.matmul(PS[7][:, :], WARML.t[:], WARMR.t[:], start=True, stop=True),
                     reads=[WARML.r(), WARMR.r()], writes=[RPS[7]])

        NSLOT = 3
        WS = [sb("ws%d" % i, [128, 8, 512], BF16) for i in range(NSLOT)]
        wctr = [0]
        XT = sb("xT", [128, 8, 512], F32, 8)
        HT = sb("hT", [128, 8, 512], BF16, 8)
        XT_S = sb("xTs", [128, 8, NS], F32, 8)
        HT_S = sb("hTs", [128, 8, NS], BF16, 8)
        CUR = {"XT": XT, "HT": HT}

        def set_stream(sample):
            CUR["XT"] = XT_S if sample else XT
            CUR["HT"] = HT_S if sample else HT
        KT = [sb("kT%d" % l, [128, 8, 256], BF16) for l in range(2)]
        VM = [sb("vM%d" % l, [128, 2, 1024], BF16) for l in range(2)]
        PRM = sb("prm", [128, NPRM])
        IDN = sb("ident", [128, 128])
        MU2 = sb("mU2", [128, 512])
        MSL = sb("mSL", [128, 256])
        BO1 = sb("bo1", [128, 128], F32R)
        BO64 = sb("bo64", [128, 128], F32R)
        ON1K = sb("on1k", [128, 128], F32R)
        ONES = sb("ones", [128, 128])
        W2X = sb("w2x", [128, 512], F32R)
        A2X = sb("a2x", [128, 512], F32R)
        G2 = sb("g2", [128, 512], F32R)
        PW = sb("pw", [128, 4, 128], F32R)
        WMT = sb("wmT", [128, 4, 128], F32R)
        SMP = sb("smp", [128, 8])
        CS16 = sb("cs16", [128, 16])
        W00I = sb("w00i", [16, 4, 16], F32R)
        ICNT = sb("icnt", [128, 4, 16])
        SBLK = sb("sblk", [128, 4, 128], F32R, 4)
        SQ = [sb("sq%d" % i, [128, 512], F32R) for i in range(2)]
        RSTD = sb("rstd", [128, 512])
        ZHIS = sb("zhist", [128, 18, 16])
        WARML = sb("warml", [128, 128], BF16)
        WARMR = sb("warmr", [128, 512], BF16)
        sqc = [0]

        ARENA_R_W = 9216
        ARENA_F_W = 29520 - ARENA_R_W
        ARENAF = st.enter_context(nc.sbuf_tensor("arenaf", [128, ARENA_F_W], F32))
        ARENAR = st.enter_context(nc.sbuf_tensor("arenar", [128, ARENA_R_W], F32R))
        aoff = {"f": 0, "r": 0}
        LIVE = []
        REGION = {"f": [], "r": []}

        def _tokens(t):
            d = {}
            for r_ in t.res:
                if r_.w is not None and d.get(r_.w[0], 0) < r_.w[1]:
                    d[r_.w[0]] = r_.w[1]
                for k_, v_ in r_.r.items():
                    if d.get(k_, 0) < v_:
                        d[k_] = v_
            return d

        def _retire(pred):
            keep = []
            for ent in LIVE:
                key, o, end, t = ent
                if pred(ent):
                    reg = REGION[key]
                    reg[:] = [e_ for e_ in reg if not (e_[0] >= o and e_[1] <= end)]
                    reg.append((o, end, _tokens(t)))
                else:
                    keep.append(ent)
            LIVE[:] = keep

        def arena_reset():
            _retire(lambda ent: True)
            aoff["f"] = 0
            aoff["r"] = 0

        def amark():
            return dict(aoff)

        def arelease(m):
            _retire(lambda ent: ent[1] >= m[ent[0]])
            aoff.update(m)

        def av(name, shape, dt=F32, nres=1):
            nparts = shape[0]
            n = 1
            for s_ in shape[1:]:
                n *= s_
            words = n if dt in (F32, F32R) else (n + 1) // 2
            key = "r" if dt == F32R else "f"
            o = aoff[key]
            aoff[key] += (words + 7) // 8 * 8
            assert aoff[key] <= (ARENA_R_W if key == "r" else ARENA_F_W), (name, key, aoff[key])
            if key == "r":
                base = ARENAR[0:nparts, o:o + words]
            else:
                base = ARENAF[0:nparts, o:o + words]
                if dt == BF16:
                    base = base.bitcast(BF16)
            if len(shape) == 3:
                base = base.rearrange("p (a b) -> p a b", a=shape[1])
            elif len(shape) == 4:
                base = base.rearrange("p (a b c) -> p a b c", a=shape[1], b=shape[2])
            elif len(shape) == 5:
                base = base.rearrange("p (a b c d) -> p a b c d", a=shape[1], b=shape[2], c=shape[3])
            t_ = T(base, [P.res("%s%d" % (name, i)) for i in range(nres)])
            o_end = o + (words + 7) // 8 * 8
            inh = {}
            for (ro, rend, tok) in REGION[key]:
                if ro < o_end and rend > o:
                    for k_, v_ in tok.items():
                        if inh.get(k_, 0) < v_:
                            inh[k_] = v_
            for r_ in t_.res:
                r_.r = dict(inh)
            LIVE.append((key, o, o_end, t_))
            return t_

        def A(x):
            return x.t if isinstance(x, T) else x

        def mm(out, lhsT, rhs, start, stop, reads, wres):
            P.op("pe", lambda e: e.matmul(out, lhsT, rhs, start=start, stop=stop), reads=reads, writes=[wres])

        def tr(out, in_, n_in_part, reads, wres):
            P.op("pe", lambda e: e.transpose(out, in_, IDN.t[0:n_in_part, 0:n_in_part]), reads=list(reads) + [IDN.r()], writes=[wres])

        def act(out, in_, func, reads, writes, bias=None, scale=None, accum=None):
            kw = {}
            if bias is not None:
                kw["bias"] = bias
            if scale is not None:
                kw["scale"] = scale
            if accum is not None:
                kw["accum_out"] = accum
            P.op("act", lambda e: e.activation(out=out, in_=in_, func=func, **kw), reads=reads, writes=writes)

        def tt(out, in0, in1, op, reads, writes, eng="dve"):
            P.op(eng, lambda e: e.tensor_tensor(out=out, in0=in0, in1=in1, op=op), reads=reads, writes=writes)

        def ts(out, in0, s1, s2, op0, op1, reads, writes, eng="dve"):
            if s2 is None:
                P.op(eng, lambda e: e.tensor_scalar(out=out, in0=in0, scalar1=s1, scalar2=None, op0=op0), reads=reads, writes=writes)
            else:
                P.op(eng, lambda e: e.tensor_scalar(out=out, in0=in0, scalar1=s1, scalar2=s2, op0=op0, op1=op1), reads=reads, writes=writes)

        def stt(out, in0, scalar, in1, op0, op1, reads, writes):
            P.op("dve", lambda e: e.scalar_tensor_tensor(out=out, in0=in0, scalar=scalar, in1=in1, op0=op0, op1=op1), reads=reads, writes=writes)

        def cp(out, in_, reads, writes, eng="dve"):
            if eng == "act":
                P.op("act", lambda e: e.activation(out=out, in_=in_, func=AF.Copy), reads=reads, writes=writes)
            else:
                P.op(eng, lambda e: e.tensor_copy(out=out, in_=in_), reads=reads, writes=writes)

        def memset(ap, val, writes, eng="dve"):
            P.op(eng, lambda e: e.memset(ap, val), writes=writes)

        def fill_r(ap2d, val, writes, p0=0):
            p, n = ap2d.shape
            ts(ap2d, ONES.t[p0:p0 + p, 0:1].to_broadcast([p, n]), float(val), None, ALU.mult, None, [ONES.r()], writes)

        def prm(name, c, n=1):
            return PRM.t[:, PC[name] + c:PC[name] + c + n]

        def wload(src2d, ncols):
            s = WS[wctr[0] % NSLOT]
            wctr[0] += 1
            P.dma("pool", s.t[:, :, 0:ncols], src2d.rearrange("(kc p) n -> p kc n", p=128), writes=[s.r()], owner=s.r(), track=False)
            return s

        evq = [0]

        def ev_eng():
            evq[0] += 1
            return "act" if evq[0] % 2 else "dve"

        memset(ONES.t[:], 1.0, [ONES.r()], "dve")
        memset(WARML.t[:], 1.0, [WARML.r()], "dve")
        memset(WARMR.t[:], 1.0, [WARMR.r()], "dve")
        P.dma("sp", PRM.t[:], I["prm"], writes=[PRM.r()], owner=PRM.r())
        P.dma("sp", SMP.t[:], I["smp"], writes=[SMP.r()], owner=SMP.r())
        P.dma("pool", G2.t[:], I["g2"], writes=[G2.r()], owner=G2.r())
        P.dma("pool", PW.t[:], I["pool_w"], writes=[PW.r()], owner=PW.r())
        fill_r(W2X.t[:], 0.0, [W2X.r()])
        fill_r(A2X.t[:], 0.0, [A2X.r()])
        P.dma("pool", W2X.t[0:64, :], I["w2"], writes=[W2X.r()], owner=W2X.r())
        P.dma("pool", A2X.t[64:128, :], I["a2"], writes=[A2X.r()], owner=A2X.r())
        memset(IDN.t[:], 0.0, [IDN.r()], "pool")
        P.op("pool", lambda e: e.affine_select(out=IDN.t[:], in_=IDN.t[:], pattern=[[-1, 128]], compare_op=ALU.not_equal,
                                                fill=1.0, base=0, channel_multiplier=1), reads=[IDN.r()], writes=[IDN.r()])
        fill_r(ON1K.t[:], 1.0 / 1024.0, [ON1K.r()])
        for h in range(2):
            o = h * 256
            memset(MU2.t[:, o:o + 256], 1.0, [MU2.r()], "pool")
            P.op("pool", lambda e, o=o: e.affine_select(out=MU2.t[:, o:o + 128], in_=MU2.t[:, o:o + 128], pattern=[[1, 128]],
                                                        compare_op=ALU.is_gt, fill=0.0, base=0, channel_multiplier=-1),
                 reads=[MU2.r()], writes=[MU2.r()])
            P.op("pool", lambda e, o=o: e.affine_select(out=MU2.t[:, o + 128:o + 256], in_=MU2.t[:, o + 128:o + 256], pattern=[[1, 128]],
                                                        compare_op=ALU.is_ge, fill=0.0, base=0, channel_multiplier=-1),
                 reads=[MU2.r()], writes=[MU2.r()])
            memset(MSL.t[:, h * 128:(h + 1) * 128], 1.0, [MSL.r()], "pool")
            P.op("pool", lambda e, h=h: e.affine_select(out=MSL.t[:, h * 128:(h + 1) * 128], in_=MSL.t[:, h * 128:(h + 1) * 128],
                                                        pattern=[[-1, 128]], compare_op=ALU.is_gt, fill=0.0, base=0, channel_multiplier=1),
                 reads=[MSL.r()], writes=[MSL.r()])
        fill_r(BO1.t[:], 0.0, [BO1.r()])
        fill_r(BO1.t[0:64, 0:64], 1.0, [BO1.r()])
        fill_r(BO1.t[64:128, 64:128], 1.0, [BO1.r()], 64)
        fill_r(BO64.t[:], 0.0, [BO64.r()])
        fill_r(BO64.t[0:64, 0:64], 1.0 / 64.0, [BO64.r()])
        fill_r(BO64.t[64:128, 64:128], 1.0 / 64.0, [BO64.r()], 64)
        if STOP_AT == "c1":
            P.finish("sp"); P.emit(); return nc
        P.op("pool", lambda e: e.iota(ICNT.t[:, 0, :], pattern=[[1, 16]], base=1, channel_multiplier=0, allow_small_or_imprecise_dtypes=True),
             writes=[ICNT.r()])
        for g, win in enumerate((2, 4, 8, 16)):
            if g > 0:
                cp(ICNT.t[:, g, :], ICNT.t[:, 0, :], [ICNT.r()], [ICNT.r()])
        for g, win in reversed(list(enumerate((2, 4, 8, 16)))):
            ts(ICNT.t[:, g, :], ICNT.t[:, g, :], float(win), None, ALU.min, None, [ICNT.r()], [ICNT.r()])
        P.op("dve", lambda e: e.reciprocal(out=ICNT.t[:], in_=ICNT.t[:]), reads=[ICNT.r()], writes=[ICNT.r()])
        fill_r(SBLK.t[:].rearrange("p a b -> p (a b)"), 0.0, SBLK.res)
        if STOP_AT == "c2":
            P.finish("sp"); P.emit(); return nc
        arena_reset()
        WST = av("wst", [128, 4, 128])
        P.dma("sp", WST.t[:], I["wsT"], writes=[WST.r()], owner=WST.r())
        tt(WMT.t[:], WST.t[:], MU2.t[:, 128:256].unsqueeze(1).to_broadcast([128, 4, 128]), ALU.mult, [WST.r(), MU2.r()], [WMT.r()])
        for g in range(4):
            ts(CS16.t[:, g * 4:g * 4 + 4], prm("ln_b", g * 4, 4), SMP.t[:, g:g + 1], SMP.t[:, 4 + g:5 + g], ALU.mult, ALU.add,
               [PRM.r(), SMP.r()], [CS16.r()])
            ts(W00I.t[:, g, :], IDN.t[0:16, 0:16], SMP.t[0:16, g:g + 1], None, ALU.mult, None, [IDN.r(), SMP.r()], [W00I.r()])
        if STOP_AT in ("c3", "c3x1", "c3x2", "c3x3"):
            P.finish("sp"); P.emit(); return nc
        def rmsnorm(Tn, gname, out_fn):
            bk, rb = nb()
            for c in range(8):
                s = SQ[sqc[0] % 2]
                sqc[0] += 1
                if c % 2 == 0:
                    act(s.t[:, 0:Tn], CUR["XT"].t[:, c, 0:Tn], AF.Square, [CUR["XT"].r(c)], [s.r()])
                else:
                    tt(s.t[:, 0:Tn], CUR["XT"].t[:, c, 0:Tn], CUR["XT"].t[:, c, 0:Tn], ALU.mult, [CUR["XT"].r(c)], [s.r()])
                mm(bk[:, 0:Tn], ON1K.t[:], s.t[:, 0:Tn], c == 0, c == 7, [ON1K.r(), s.r()], rb)
            act(RSTD.t[:, 0:Tn], bk[:, 0:Tn], AF.Ln, [rb], [RSTD.r()], bias=RMS_EPS)
            act(RSTD.t[:, 0:Tn], RSTD.t[:, 0:Tn], AF.Exp, [RSTD.r()], [RSTD.r()], scale=-0.5)
            for c in range(8):
                out_fn(c, CUR["XT"].t[:, c, 0:Tn], prm(gname, c), RSTD.t[:, 0:Tn])

        def norm_to_h(Tn, gname):
            def f(c, x, g, rs):
                stt(CUR["HT"].t[:, c, 0:Tn], x, g, rs, ALU.mult, ALU.mult, [CUR["XT"].r(c), PRM.r(), RSTD.r()], [CUR["HT"].r(c)])
            rmsnorm(Tn, gname, f)

        def proj_fm_multi(wsrc, kin, nout, targets):
            for c0 in range(0, nout, 512):
                ncols = min(512, nout - c0)
                s_ = wload(wsrc[:, c0:c0 + ncols], ncols)
                for mq in range(ncols // 128):
                    for (Tn, rhs_fn, evac_fn) in targets:
                        bk, rb = nb()
                        for kc in range(kin):
                            rap, rr = rhs_fn(kc)
                            mm(bk[:, 0:Tn], s_.t[:, kc, mq * 128:(mq + 1) * 128], rap, kc == 0, kc == kin - 1, [s_.r(), rr], rb)
                        evac_fn(c0 // 128 + mq, bk[:, 0:Tn], rb)

        def proj_fm(Tn, wsrc, kin, nout, rhs_fn, evac_fn):
            proj_fm_multi(wsrc, kin, nout, [(Tn, rhs_fn, evac_fn)])

        def proj_down_multi(wsrc, kchunks, targets):
            nkq = kchunks // 8
            for cg in range(2):
                banks = [[nb() for _ in range(4)] for _t in targets]
                for kq in range(nkq):
                    s_ = wload(wsrc[kq * 1024:(kq + 1) * 1024, cg * 512:(cg + 1) * 512], 512)
                    for mq in range(4):
                        for ti, (Tn, rhs_fn, xt) in enumerate(targets):
                            bk, rb = banks[ti][mq]
                            for kc in range(8):
                                rap, rr = rhs_fn(kq * 8 + kc)
                                mm(bk[:, 0:Tn], s_.t[:, kc, mq * 128:(mq + 1) * 128], rap, kq == 0 and kc == 0,
                                   kq == nkq - 1 and kc == 7, [s_.r(), rr], rb)
                for ti, (Tn, rhs_fn, xt) in enumerate(targets):
                    for mq in range(4):
                        m = cg * 4 + mq
                        bk, rb = banks[ti][mq]
                        tt(xt.t[:, m, 0:Tn], xt.t[:, m, 0:Tn], bk[:, 0:Tn], ALU.add, [xt.r(m), rb], [xt.r(m)])

        def proj_down(Tn, wsrc, kchunks, rhs_fn):
            proj_down_multi(wsrc, kchunks, [(Tn, rhs_fn, CUR["XT"])])

        def hrhs(Tn):
            return lambda kc: (CUR["HT"].t[:, kc, 0:Tn], CUR["HT"].r(kc))

        P.barrier(pool=True)
        arena_reset()
        MTOK = av("mtok", [128, 2, 1024])
        MT = av("memT", [128, 8, 256], F32, 8)
        MN = [av("mn%d" % l, [128, 8, 256], BF16, 8) for l in range(2)]
        KVS = [av("kvs%d" % i, [128, 2, 1024]) for i in range(2)]
        P.dma("sp", MTOK.t[:], I["memp"].rearrange("(a p) f -> p a f", p=128), writes=[MTOK.r()], owner=MTOK.r())
        for c in range(8):
            bk, rb = nb()
            for a in range(2):
                tr(bk[:, a * 128:(a + 1) * 128], MTOK.t[:, a, c * 128:(c + 1) * 128], 128, [MTOK.r()], rb)
            cp(MT.t[:, c, :], bk[:, 0:256], [rb], [MT.r(c)], ev_eng())
        bk, rb = nb()
        for c in range(8):
            s = SQ[sqc[0] % 2]
            sqc[0] += 1
            act(s.t[:, 0:256], MT.t[:, c, :], AF.Square, [MT.r(c)], [s.r()])
            mm(bk[:, 0:256], ON1K.t[:], s.t[:, 0:256], c == 0, c == 7, [ON1K.r(), s.r()], rb)
        act(RSTD.t[:, 0:256], bk[:, 0:256], AF.Ln, [rb], [RSTD.r()], bias=RMS_EPS)
        act(RSTD.t[:, 0:256], RSTD.t[:, 0:256], AF.Exp, [RSTD.r()], [RSTD.r()], scale=-0.5)
        for l in range(2):
            for c in range(8):
                stt(MN[l].t[:, c, :], MT.t[:, c, :], prm("g_mem%d" % l, c), RSTD.t[:, 0:256], ALU.mult, ALU.mult,
                    [MT.r(c), PRM.r(), RSTD.r()], [MN[l].r(c)])
        if STOP_AT == "mk1":
            P.finish("sp"); P.emit(); return nc
        kvi = 0
        for l in range(2):
            for which, wname, oname in (("k", "w_xk", "mk"), ("v", "w_xv", "mv")):
                for cg in range(2):
                    s = wload(I[wname][l][:, cg * 512:(cg + 1) * 512], 512)
                    if which == "k":
                        for mq in range(4):
                            bk, rb = nb()
                            for kc in range(8):
                                mm(bk[:, 0:256], s.t[:, kc, mq * 128:(mq + 1) * 128], MN[l].t[:, kc, :], kc == 0, kc == 7,
                                   [s.r(), MN[l].r(kc)], rb)
                            cp(KT[l].t[:, cg * 4 + mq, :], bk[:, 0:256], [rb], [KT[l].r()], ev_eng())
                    for a in range(2):
                        bk, rb = nb()
                        for kc in range(8):
                            mm(bk[:, :], MN[l].t[:, kc, a * 128:(a + 1) * 128], s.t[:, kc, :], kc == 0, kc == 7, [s.r(), MN[l].r(kc)], rb)
                        stg = KVS[kvi % 2]
                        cp(stg.t[:, a, cg * 512:(cg + 1) * 512], bk[:, :], [rb], [stg.r()], "act")
                        if which == "v":
                            cp(VM[l].t[:, a, cg * 512:(cg + 1) * 512], bk[:, :], [rb], [VM[l].r()], "dve")
                    if STOP_AT == "mk2" or (STOP_AT == "mk3c" and which == "v" and cg == 0) or (STOP_AT == "mk3d" and which == "v" and cg == 1):
                        P.finish("sp"); P.emit(); return nc
                stg = KVS[kvi % 2]
                kvi += 1
                P.dma("sp", O[oname][l].rearrange("(a p) f -> p a f", p=128), stg.t[:], reads=[stg.r()], owner=stg.r(), final=True)
                if STOP_AT == "mk3":
                    P.finish("sp"); P.emit(); return nc
        if STOP_AT == "mk4":
            P.finish("sp"); P.emit(); return nc
        P.barrier()
        try:
            stage("setup")
        except StopBuild:
            P.finish("sp")
            P.emit()
            return nc

        def dims(sample):
            return (NS, 1, NS) if sample else (512, 4, 128)

        def load_x(blk, sample, pool):
            Tn, nch, CH = dims(sample)
            set_stream(sample)
            arena_reset()
            XTOK = av("xtok", [128, nch, 1024])
            if sample:
                P.dma("sp", XTOK.t[0:NS, 0, :], I["xs"], writes=[XTOK.r()], owner=XTOK.r())
            else:
                P.dma("sp", XTOK.t[:], I["xp"][blk * 512:(blk + 1) * 512, :].rearrange("(a p) f -> p a f", p=128),
                      writes=[XTOK.r()], owner=XTOK.r())
            for c in range(8):
                bk, rb = nb()
                for a in range(nch):
                    tr(bk[:, a * CH:(a + 1) * CH], XTOK.t[0:CH, a, c * 128:(c + 1) * 128], CH, [XTOK.r()], rb)
                cp(CUR["XT"].t[:, c, 0:Tn], bk[:, 0:Tn], [rb], [CUR["XT"].r(c)], ev_eng())
            P.barrier(pool=pool)

        def phase_mixer(blk, sample, layer, pool):
            Tn, nch, CH = dims(sample)
            set_stream(sample)
            arena_reset()
            norm_to_h(Tn, "g_mix%d" % layer)
            if layer == 0:
                mixer_ab(blk, sample, Tn, nch, CH)
            else:
                mixer_c(blk, sample, Tn, nch, CH)
            P.barrier(pool=pool)

        def phase_attn(blk, sample, layer, pool):
            Tn, nch, CH = dims(sample)
            set_stream(sample)
            arena_reset()
            norm_to_h(Tn, "g_xa%d" % layer)
            if sample:
                xattn_sample(layer)
            else:
                xattn_prompt(layer)
            P.barrier(pool=pool)

        def phase_mlp(streams, layer, pool):
            arena_reset()
            up_t, dn_t = [], []
            for sample in streams:
                Tn, nch, CH = dims(sample)
                set_stream(sample)
                norm_to_h(Tn, "g_ffn%d" % layer)
                tagn = "s" if sample else "p"
                HID = av("hid" + tagn, [128, 32, Tn], BF16, 32)
                TMP = [av("mlptmp%s%d" % (tagn, i_), [128, Tn]) for i_ in range(2)]

                def ev_up(m, bk, rb, HID=HID, TMP=TMP, tctr=[0]):
                    t = TMP[tctr[0] % 2]
                    tctr[0] += 1
                    act(t.t[:], bk, AF.Square, [rb], [t.r()])
                    stt(HID.t[:, m, :], bk, 0.0, t.t[:], ALU.is_gt, ALU.mult, [rb, t.r()], [HID.r(m)])
                ht = CUR["HT"]
                up_t.append((Tn, (lambda kc, ht=ht, Tn=Tn: (ht.t[:, kc, 0:Tn], ht.r(kc))), ev_up))
                dn_t.append((Tn, (lambda k, HID=HID: (HID.t[:, k, :], HID.r(k))), CUR["XT"]))
            proj_fm_multi(I["w_ff_up"][layer], 8, 4096, up_t)
            proj_down_multi(I["w_ff_down"][layer], 32, dn_t)
            P.barrier(pool=pool)

        def final_out(blk, sample, pool):
            Tn, nch, CH = dims(sample)
            set_stream(sample)
            arena_reset()
            YF = av("yf", [128, 8, Tn], F32, 8)
            YTOK = av("ytok", [128, nch, 1024])
            xt = CUR["XT"]

            def fo(c, x, g, rs):
                stt(YF.t[:, c, :], x, g, rs, ALU.mult, ALU.mult, [xt.r(c), PRM.r(), RSTD.r()], [YF.r(c)])
            rmsnorm(Tn, "g_fin", fo)
            for a in range(nch):
                for cg in range(2):
                    bk, rb = nb()
                    for q in range(4):
                        c = cg * 4 + q
                        tr(bk[0:CH, q * 128:(q + 1) * 128], YF.t[:, c, a * CH:(a + 1) * CH], 128, [YF.r(c)], rb)
                    cp(YTOK.t[0:CH, a, cg * 512:(cg + 1) * 512], bk[0:CH, :], [rb], [YTOK.r()], ev_eng())
            if sample:
                P.dma("sp", O["ys"], YTOK.t[0:NS, 0, :], reads=[YTOK.r()], owner=YTOK.r(), final=True)
            else:
                P.dma("sp", O["yp"][blk * 512:(blk + 1) * 512, :].rearrange("(a p) f -> p a f", p=128), YTOK.t[:],
                      reads=[YTOK.r()], owner=YTOK.r(), final=True)
            P.barrier(pool=pool)

        def run_block(blk, sample):
            load_x(blk, sample, True)
            for layer in range(2):
                phase_mixer(blk, sample, layer, True)
                phase_attn(blk, sample, layer, True)
                phase_mlp([sample], layer, True)
            final_out(blk, sample, True)

        def run_joint():
            load_x(3, False, True)
            load_x(0, True, True)
            for layer in range(2):
                phase_mixer(3, False, layer, True)
                phase_attn(3, False, layer, True)
                phase_mixer(0, True, layer, True)
                phase_attn(0, True, layer, True)
                phase_mlp([False, True], layer, True)
            final_out(3, False, True)
            final_out(0, True, True)

        HOLD = {}

        def mixer_ab(blk, sample, Tn, nch, CH):
            NBANK[0] = 8 if sample else 7
            mixer_ab_(blk, sample, Tn, nch, CH)
            NBANK[0] = 8

        def mixer_ab_(blk, sample, Tn, nch, CH):
            H0 = 16
            ZB = av("zb", [128, 18, H0 + Tn], F32, 18)
            if not sample:
                HIS = ZHIS
                if "hist" not in HOLD:
                    HOLD["hist"] = True
                    memset(HIS.t[:], 0.0, [HIS.r()])
                cp(ZB.t[:, :, 0:H0], HIS.t[:], [HIS.r()], ZB.res, "dve")
            YC = av("ycat", [128, 8, Tn], BF16, 8)

            def ev_z(m, bk, rb):
                cp(ZB.t[:, m, H0:H0 + Tn], bk, [rb], [ZB.r(m)], ev_eng())
            proj_fm(Tn, I["w_in_ab"], 8, 2304, hrhs(Tn), ev_z)

            if sample:
                Q6 = av("q6", [128, 6, 512])
                QR = av("qr", [128, 6, 64])
                SAV = av("sav", [128, 64]); YV = av("yv", [128, 64]); YTS = av("yts", [128, 512])
                ZPS = av("zps", [128, 14, NS])
            PA = av("pa", [128, H0 + Tn])
            PB = av("pb", [128, H0 + Tn])
            P16 = av("p16", [128, 16])
            LOR = av("lor", [128, Tn], F32R)
            SG = av("sg", [128, Tn], F32R)
            TMPL = T(PA.t[:, 0:Tn], PA.res)
            ZSR = av("zsr", [128, Tn]); ZSK = av("zsk", [128, Tn]); ZSV = av("zsv", [128, Tn])
            SIG = av("sig", [128, Tn]); CSUM = av("csum", [128, Tn]); E1 = T(PA.t[:, 0:Tn], PA.res)
            AA = av("aa", [128, Tn]); KKN = av("kkn", [128, Tn]); XX = av("xx", [128, Tn], F32R)
            DP = XX
            XF = T(PB.t[:, 0:Tn], PB.res)
            YTP = av("ytp", [128, Tn], F32R)
            NLC = av("nlc", [128, 4]); DCT = av("dct", [128, 4])
            if not sample:
                EB = av("eb", [128, Tn]); ED = av("ed", [128, Tn])
                AR = av("ar", [128, 4, 2, 128], F32R)
                BT = av("bt", [128, 512], F32R); KTt = av("ktt", [128, 512], F32R)
                SETS = []
                for k_ in range(2):
                    S_ = dict(
                        TOK=av("tok%d" % k_, [128, 4, 128], F32R), M1=av("m1%d" % k_, [128, 2, 128], F32R), M3=av("m3%d" % k_, [128, 512], F32R),
                        ZQ=[av("zq%d%d" % (k_, q_), [128, 2, 2, 128], BF16) for q_ in range(2)],
                        PP=[av("pp%d%d" % (k_, q_), [128, 2, 128], BF16) for q_ in range(2)],
                        ARX=av("arx%d" % k_, [128, 4, 128], F32R), BTX=av("btx%d" % k_, [128, 2, 128], F32R),
                        APT=av("apt%d" % k_, [128, 128], F32R), UT=av("ut%d" % k_, [128, 128], F32R),
                        APC=av("apc%d" % k_, [128, 128]), YTK=av("ytk%d" % k_, [128, 128]),
                        BH=av("bh%d" % k_, [128, 128]), KH=av("kh%d" % k_, [128, 128]))
                    fill_r(S_["ARX"].t[:].rearrange("p a b -> p (a b)"), 0.0, [S_["ARX"].r()])
                    fill_r(S_["BTX"].t[:].rearrange("p a b -> p (a b)"), 0.0, [S_["BTX"].r()])
                    SETS.append(S_)
            else:
                pass

            AMARK = amark()
            if sample:
                SPT = av("spt", [128, 2, 512])
                PH = av("ph", [128, 4, 240])
                spf = I["spool"].rearrange("b p f -> (b p) f")
                P.dma("sp", SPT.t[:, 0, :], spf[0:128, :], writes=[SPT.r()], owner=SPT.r())
                P.dma("sp", SPT.t[0:112, 1, :], spf[128:240, :], writes=[SPT.r()], owner=SPT.r())
                for g in range(4):
                    bk, rb = nb()
                    tr(bk[:, 0:128], SPT.t[:, 0, g * 128:(g + 1) * 128], 128, [SPT.r()], rb)
                    tr(bk[:, 128:240], SPT.t[0:112, 1, g * 128:(g + 1) * 128], 112, [SPT.r()], rb)
                    cp(PH.t[:, g, :], bk[:, 0:240], [rb], [PH.r()], ev_eng())
                PCP = P.res("pcopy")
                P.dma("sp", O["pools"][:, 0:14, :], I["spool"][:, 1:15, :], owner=PCP, final=True)
            for g, win in enumerate((2, 4, 8, 16)):
                zg = ZB.t[:, g, :]
                zn = ZB.t[:, g, H0:H0 + Tn]
                if sample:
                    P.op("dve", lambda e, g=g, win=win: e.tensor_reduce(
                        out=PA.t[:, 0:NS], in_=PH.t[:, g, :].rearrange("p (b q) -> p b q", q=15)[:, :, 16 - win:15], axis=AX.X, op=ALU.add),
                        reads=[PH.r()], writes=[PA.r()])
                    tt(PA.t[:, 0:NS], PA.t[:, 0:NS], zn, ALU.add, [PA.r(), ZB.r(g)], [PA.r()])
                    S_ap = PA.t[:, 0:NS]
                    S_r = PA.r()
                else:
                    lo = {2: 16, 4: 14, 8: 10, 16: 2}[win]
                    tt(PA.t[:, lo:H0 + Tn], zg[:, lo:H0 + Tn], zg[:, lo - 1:H0 + Tn - 1], ALU.add, [ZB.r(g)], [PA.r()])
                    cur, oth = PA, PB
                    sh = 2
                    lo2 = lo
                    while sh < win:
                        lo2 = lo2 + sh
                        tt(oth.t[:, lo2:H0 + Tn], cur.t[:, lo2:H0 + Tn], cur.t[:, lo2 - sh:H0 + Tn - sh], ALU.add, [cur.r()], [oth.r()])
                        cur, oth = oth, cur
                        sh *= 2
                    S_ap = cur.t[:, H0:H0 + Tn]
                    S_r = cur.r()
                stt(DP.t[:], S_ap, 1.0 / win, zn, ALU.mult, ALU.subtract, [S_r, ZB.r(g)], [DP.r()])
                if (not sample) and blk == 0:
                    tt(P16.t[:], S_ap[:, 0:16], ICNT.t[:, g, :], ALU.mult, [S_r, ICNT.r()], [P16.r()])
                    tt(DP.t[:, 0:16], P16.t[:], zn[:, 0:16], ALU.subtract, [P16.r(), ZB.r(g)], [DP.r()])
                bk, rb = nb()
                mm(bk[:, 0:Tn], PW.t[:, g, :], DP.t[:], True, True, [PW.r(), DP.r()], rb)
                ts(YC.t[:, g, :], bk[:, 0:Tn], prm("pscale", g), None, ALU.mult, None, [rb, PRM.r()], [YC.r(g)])

            if sample:
                SHT = av("sht", [128, 1792])
                P.dma("sp", SHT.t[0:NS, :], I["sshift"], writes=[SHT.r()], owner=SHT.r())
                for q in range(0, 14, 7):
                    bk, rb = nb()
                    for j in range(q, q + 7):
                        tr(bk[:, (j - q) * NS:(j - q + 1) * NS], SHT.t[0:NS, j * 128:(j + 1) * 128], NS, [SHT.r()], rb)
                    cp(ZPS.t[:, q:q + 7, :], bk[:, 0:7 * NS].rearrange("p (a b) -> p a b", a=7), [rb], [ZPS.r()], ev_eng())

            def zshift(j, out, out_r, eng="dve"):
                m = 4 + j
                zc = ZB.t[:, m, H0:H0 + Tn]
                zp = ZPS.t[:, j, :] if sample else ZB.t[:, m, H0 - 1:H0 + Tn - 1]
                rd = [ZB.r(m)] + ([ZPS.r()] if sample else [])
                if eng == "pool" and not sample:
                    tt(out, zp, zc, ALU.subtract, rd, [out_r], "pool")
                    ts(out, out, prm("mu", j), 0.0, ALU.mult, ALU.add, [out_r, PRM.r()], [out_r], "pool")
                    tt(out, out, zc, ALU.add, [out_r, ZB.r(m)], [out_r], "pool")
                    return
                tt(out, zp, zc, ALU.subtract, rd, [out_r])
                stt(out, out, prm("mu", j), zc, ALU.mult, ALU.add, [out_r, PRM.r(), ZB.r(m)], [out_r])

            zshift(12, TMPL.t[:], TMPL.r())
            act(LOR.t[0:64, :], TMPL.t[0:64, :], AF.Tanh, [TMPL.r()], [LOR.r()])
            cp(LOR.t[64:128, :], TMPL.t[64:128, :], [TMPL.r()], [LOR.r()], "dve")
            zshift(13, TMPL.t[:], TMPL.r())
            act(SG.t[:], TMPL.t[:], AF.Sigmoid, [TMPL.r()], [SG.r()])

            def prep_pair(i, bonus_only):
                zshift(4 + i, ZSK.t[:], ZSK.r())
                zshift(i, ZSR.t[:], ZSR.r(), "pool")
                zshift(8 + i, ZSV.t[:], ZSV.r(), "pool")
                bk, rb = nb()
                mm(bk[:, 0:Tn], A2X.t[:, i * 128:(i + 1) * 128], LOR.t[:], True, True, [A2X.r(), LOR.r()], rb)
                act(AA.t[:], bk[:, 0:Tn], AF.Sigmoid, [rb, PRM.r()], [AA.r()], bias=prm("a0", i))
                if not sample:
                    warm(WARM_PREP)
                if not bonus_only:
                    bk, rb = nb()
                    mm(bk[:, 0:Tn], W2X.t[:, i * 128:(i + 1) * 128], LOR.t[:], True, True, [W2X.r(), LOR.r()], rb)
                    if not sample:
                        warm(WARM_PREP)
                    act(SIG.t[:], bk[:, 0:Tn], AF.Sigmoid, [rb, PRM.r()], [SIG.r()], bias=prm("w0", i))
                    ts(KKN.t[:], ZSK.t[:], prm("k_k", i), None, ALU.mult, None, [ZSK.r(), PRM.r()], [KKN.r()])
                    act(XX.t[:], KKN.t[:], AF.Square, [KKN.r()], [XX.r()])
                    bk, rb = nb()
                    mm(bk[:, 0:Tn], BO1.t[:], XX.t[:], True, True, [BO1.r(), XX.r()], rb)
                    if not sample:
                        warm(WARM_PREP)
                    ts(XF.t[:], bk[:, 0:Tn], 1e-24, None, ALU.max, None, [rb], [XF.r()])
                    act(XF.t[:], XF.t[:], AF.Ln, [XF.r()], [XF.r()])
                    act(XF.t[:], XF.t[:], AF.Exp, [XF.r()], [XF.r()], scale=-0.5)
                    tt(KKN.t[:], KKN.t[:], XF.t[:], ALU.mult, [KKN.r(), XF.r()], [KKN.r()])
                ts(XF.t[:], AA.t[:], 1.0, prm("k_a", i), ALU.subtract, ALU.mult, [AA.r(), PRM.r()], [XF.r()])
                stt(ZSK.t[:], XF.t[:], 1.0, ZSK.t[:], ALU.add, ALU.mult, [XF.r(), ZSK.r()], [ZSK.r()])
                if not bonus_only:
                    tt(AA.t[:], KKN.t[:], AA.t[:], ALU.mult, [KKN.r(), AA.r()], [AA.r()])

            def gn_pair(i):
                bk, rb = nb()
                mm(bk[:, 0:Tn], BO64.t[:], YTP.t[:], True, True, [BO64.r(), YTP.r()], rb)
                tt(SIG.t[:], YTP.t[:], bk[:, 0:Tn], ALU.subtract, [YTP.r(), rb], [SIG.r()])
                act(XX.t[:], SIG.t[:], AF.Square, [SIG.r()], [XX.r()])
                bk, rb = nb()
                mm(bk[:, 0:Tn], BO64.t[:], XX.t[:], True, True, [BO64.r(), XX.r()], rb)
                act(CSUM.t[:], bk[:, 0:Tn], AF.Ln, [rb], [CSUM.r()], bias=GN_EPS)
                act(CSUM.t[:], CSUM.t[:], AF.Exp, [CSUM.r()], [CSUM.r()], scale=-0.5)
                tt(SIG.t[:], SIG.t[:], CSUM.t[:], ALU.mult, [SIG.r(), CSUM.r()], [SIG.r()])
                ts(SIG.t[:], SIG.t[:], prm("gn_g", i), prm("gn_b", i), ALU.mult, ALU.add, [SIG.r(), PRM.r()], [SIG.r()])
                stt(XX.t[:], ZSR.t[:], prm("r_k", i), ZSK.t[:], ALU.mult, ALU.mult, [ZSR.r(), PRM.r(), ZSK.r()], [XX.r()])
                bk, rb = nb()
                mm(bk[:, 0:Tn], BO1.t[:], XX.t[:], True, True, [BO1.r(), XX.r()], rb)
                tt(E1.t[:], bk[:, 0:Tn], ZSV.t[:], ALU.mult, [rb, ZSV.r()], [E1.r()])
                tt(SIG.t[:], SIG.t[:], E1.t[:], ALU.add, [SIG.r(), E1.r()], [SIG.r()])
                bk, rb = nb()
                mm(bk[:, 0:Tn], G2.t[:, i * 128:(i + 1) * 128], SG.t[:], True, True, [G2.r(), SG.r()], rb)
                tt(YC.t[:, 4 + i, :], SIG.t[:], bk[:, 0:Tn], ALU.mult, [SIG.r(), rb], [YC.r(4 + i)])

            c3 = lambda ap: ap.rearrange("p (c t) -> p c t", c=4)
            for i in range(4):
                prep_pair(i, False)
                if not sample:
                    for c in range(4):
                        P.op("dve", lambda e, c=c: e.tensor_tensor_scan(
                            out=CSUM.t[:, c * 128:(c + 1) * 128], data0=ONES.t[:, 0:128], data1=SIG.t[:, c * 128:(c + 1) * 128],
                            initial=0.0, op0=ALU.mult, op1=ALU.add), reads=[ONES.r(), SIG.r()], writes=[CSUM.r()])
                    ts(NLC.t[:], c3(CSUM.t[:])[:, :, 127], -LAM, None, ALU.mult, None, [CSUM.r()], [NLC.r()])
                    act(DCT.t[:], NLC.t[:], AF.Exp, [NLC.r()], [DCT.r()])
                    act(E1.t[:], CSUM.t[:], AF.Exp, [CSUM.r()], [E1.r()], scale=-LAM)
                    tt(AR.t[:, :, 1, :], c3(ZSR.t[:]), c3(E1.t[:]), ALU.mult, [ZSR.r(), E1.r()], [AR.r()])
                    act(EB.t[:], CSUM.t[:], AF.Exp, [CSUM.r()], [EB.r()], scale=LAM)
                    tt(BT.t[:], AA.t[:], EB.t[:], ALU.mult, [AA.r(), EB.r()], [BT.r()])
                    tt(KTt.t[:], ZSK.t[:], EB.t[:], ALU.mult, [ZSK.r(), EB.r()], [KTt.r()])
                    tt(SIG.t[:], CSUM.t[:], SIG.t[:], ALU.subtract, [CSUM.r(), SIG.r()], [SIG.r()])
                    act(E1.t[:], SIG.t[:], AF.Exp, [SIG.r()], [E1.r()], scale=-LAM)
                    stt(AR.t[:, :, 0, :], c3(KKN.t[:]), -1.0, c3(E1.t[:]), ALU.mult, ALU.mult, [KKN.r(), E1.r()], [AR.r()])
                    for c in range(4):
                        act(ED.t[:, c * 128:(c + 1) * 128], CSUM.t[:, c * 128:(c + 1) * 128], AF.Exp, [CSUM.r(), NLC.r()], [ED.r()],
                            bias=NLC.t[:, c:c + 1], scale=LAM)

                    def front_a(c, S):
                        csl = slice(c * 128, (c + 1) * 128)
                        ARX, BTX = S["ARX"], S["BTX"]
                        tt(S["BH"].t[:], AA.t[:, csl], ED.t[:, csl], ALU.mult, [AA.r(), ED.r()], [S["BH"].r()])
                        tt(S["KH"].t[:], ZSK.t[:, csl], ED.t[:, csl], ALU.mult, [ZSK.r(), ED.r()], [S["KH"].r()])
                        cp(ARX.t[0:64, 0:2, :], AR.t[0:64, c, :, :], [AR.r()], [ARX.r()], "pool")
                        cp(ARX.t[64:128, 2:4, :], AR.t[64:128, c, :, :], [AR.r()], [ARX.r()], "pool")
                        cp(BTX.t[0:64, 0, :], BT.t[0:64, csl], [BT.r()], [BTX.r()], "pool")
                        cp(BTX.t[64:128, 1, :], BT.t[64:128, csl], [BT.r()], [BTX.r()], "pool")

                    def front_b(c, S):
                        csl = slice(c * 128, (c + 1) * 128)
                        TOK, M1, M3, ZQ, ARX, BTX = S["TOK"], S["M1"], S["M3"], S["ZQ"], S["ARX"], S["BTX"]
                        bk, rb = nb()
                        tr(bk[:, 0:128], AR.t[:, c, 0, :].bitcast(F32), 128, [AR.r()], rb)
                        tr(bk[:, 128:256], ZSV.t[:, csl], 128, [ZSV.r()], rb)
                        tr(bk[:, 256:384], S["BH"].t[:], 128, [S["BH"].r()], rb)
                        tr(bk[:, 384:512], S["KH"].t[:], 128, [S["KH"].r()], rb)
                        cp(TOK.t[:].rearrange("p a b -> p (a b)"), bk[:, :], [rb], [TOK.r()], "act")
                        arx_c = ARX.t[:, :, :].rearrange("p a b -> p (a b)")
                        mu4 = MU2.t[:].rearrange("p (j q t) -> p j q t", j=2, q=2)
                        bk1, rb1 = nb()
                        mm(bk1[:, :], BT.t[:, csl], arx_c, True, True, [BT.r(), ARX.r()], rb1)
                        b14 = bk1[:, :].rearrange("p (j q t) -> p j q t", j=2, q=2)
                        P0 = S["PP"][1]
                        tt(P0.t[:], b14[:, :, 0, :], mu4[:, :, 0, :], ALU.mult, [rb1, MU2.r()], [P0.r()])
                        tt(M1.t[:], b14[:, :, 1, :], mu4[:, :, 1, :], ALU.mult, [rb1, MU2.r()], [M1.r()])
                        bk3, rb3 = nb()
                        mm(bk3[:, :], KTt.t[:, csl], arx_c, True, True, [KTt.r(), ARX.r()], rb3)
                        tt(M3.t[:], bk3[:, :], MU2.t[:], ALU.mult, [rb3, MU2.r()], [M3.r()])
                        bk2, rb2 = nb()
                        mm(bk2[:, 0:256], AR.t[:, c, 0, :], BTX.t[:, :, :].rearrange("p a b -> p (a b)"), True, True, [AR.r(), BTX.r()], rb2)
                        z0 = ZQ[0]
                        tt(z0.t[:, :, 1, :], bk2[:, 0:256].rearrange("p (a b) -> p a b", a=2), MSL.t[:].rearrange("p (a b) -> p a b", a=2),
                           ALU.mult, [rb2, MSL.r()], [z0.r()])
                        S["m3v"] = M3.t[:].rearrange("p (j q t) -> p j q t", j=2, q=2)
                        bkw, rbw = nb()
                        for j in range(2):
                            mm(bkw[:, j * 64:(j + 1) * 64], S["m3v"][:, j, 0, :], TOK.t[:, 1, j * 64:(j + 1) * 64], True, True, [M3.r(), TOK.r()], rbw)
                        cp(z0.t[:, :, 0, 0:64], TOK.t[:, 0, :].rearrange("p (j k) -> p j k", j=2), [TOK.r()], [z0.r()], "pool")
                        cp(z0.t[:, :, 0, 64:128], bkw[:, 0:128].rearrange("p (j k) -> p j k", j=2), [rbw], [z0.r()], "act")
                        S["zc"], S["zn"], S["pcur"] = ZQ[0], ZQ[1], P0

                    def neumann_mm(lv, S):
                        zc_ = S["zc"]
                        bkz, rbz = nb()
                        bkp, rbp = (None, None)
                        if lv < 6:
                            bkp, rbp = nb()
                        nz = 256 if lv < 6 else 128
                        for j in range(2):
                            lhs = S["pcur"].t[:, j, :]
                            lr = S["pcur"].r()
                            mm(bkz[:, j * 256:j * 256 + nz], lhs, zc_.t[:, j, :, :].rearrange("p a b -> p (a b)")[:, 0:nz], True, True,
                               [lr, zc_.r()], rbz)
                            if lv < 6:
                                mm(bkp[:, j * 128:(j + 1) * 128], zc_.t[:, j, 1, :], lhs, True, True, [lr, zc_.r()], rbp)
                        S["banks"] = (bkz, rbz, bkp, rbp)

                    def neumann_ev(lv, S, par):
                        zc_, zn_ = S["zc"], S["zn"]
                        bkz, rbz, bkp, rbp = S["banks"]
                        bz4 = bkz[:, :].rearrange("p (j q t) -> p j q t", j=2, q=2)
                        tt(zn_.t[:, :, 0, :], zc_.t[:, :, 0, :], bz4[:, :, 0, :], ALU.add, [zc_.r(), rbz], [zn_.r()])
                        if lv < 6:
                            cp(zn_.t[:, :, 1, :], bz4[:, :, 1, :], [rbz], [zn_.r()], "act")
                            pn = S["PP"][lv % 2]
                            cp(pn.t[:], bkp[:, 0:256].rearrange("p (j t) -> p j t", j=2), [rbp], [pn.r()], "act" if (lv + par) % 2 else "dve")
                            S["pcur"] = pn
                        S["zc"], S["zn"] = zn_, zc_

                    def chunk_back(c, S):
                        csl = slice(c * 128, (c + 1) * 128)
                        zf = S["zc"]
                        TOK, APC, APT, UT, YTK = S["TOK"], S["APC"], S["APT"], S["UT"], S["YTK"]
                        m3v = S["m3v"]
                        cp(APC.t[:].rearrange("p (j k) -> p j k", j=2), zf.t[:, :, 0, 0:64], [zf.r()], [APC.r()], "pool")
                        bka, rba = nb()
                        tr(bka[:, 0:128], APC.t[:], 128, [APC.r()], rba)
                        cp(APT.t[:], bka[:, 0:128], [rba], [APT.r()], "act")
                        bku, rbu = nb()
                        mm(bku[:, 0:128], APT.t[:], SBLK.t[:, i, :], True, True, [APT.r(), SBLK.r(i)], rbu)
                        tt(UT.t[:].rearrange("p (j k) -> p j k", j=2), bku[:, 0:128].rearrange("p (j k) -> p j k", j=2),
                           zf.t[:, :, 0, 64:128], ALU.add, [rbu, zf.r()], [UT.r()])
                        bky, rby = nb()
                        mm(bky[:, 0:128], AR.t[:, c, 1, :], SBLK.t[:, i, :], True, False, [AR.r(), SBLK.r(i)], rby)
                        for j in range(2):
                            mm(bky[:, j * 64:(j + 1) * 64], S["M1"].t[:, j, :], UT.t[:, j * 64:(j + 1) * 64], False, False, [S["M1"].r(), UT.r()], rby)
                            mm(bky[:, j * 64:(j + 1) * 64], m3v[:, j, 1, :], TOK.t[:, 1, j * 64:(j + 1) * 64], False, j == 1, [S["M3"].r(), TOK.r()], rby)
                        bks, rbs = nb()
                        mm(bks[:, 0:128], TOK.t[:, 2, :], UT.t[:], True, False, [TOK.r(), UT.r()], rbs)
                        mm(bks[:, 0:128], TOK.t[:, 3, :], TOK.t[:, 1, :], False, True, [TOK.r()], rbs)
                        cp(YTK.t[:], bky[:, 0:128], [rby], [YTK.r()], "act")
                        for j in range(2):
                            js = slice(j * 64, (j + 1) * 64)
                            stt(SBLK.t[js, i, js], SBLK.t[js, i, js], DCT.t[js, c:c + 1], bks[js, js], ALU.mult, ALU.add,
                                [SBLK.r(i), DCT.r(), rbs], [SBLK.r(i)])
                        bkt, rbt = nb()
                        tr(bkt[:, 0:128], YTK.t[:], 128, [YTK.r()], rbt)
                        cp(YTP.t[:, csl], bkt[:, 0:128], [rbt], [YTP.r()], "dve")

                    front_a(0, SETS[0])
                    front_a(1, SETS[1])
                    for c0 in (0, 2):
                        front_b(c0, SETS[0])
                        front_b(c0 + 1, SETS[1])
                        if c0 == 0:
                            front_a(2, SETS[0])
                            front_a(3, SETS[1])
                        for lv in range(7):
                            neumann_mm(lv, SETS[0])
                            neumann_mm(lv, SETS[1])
                            warm(WARM_LV)
                            neumann_ev(lv, SETS[0], 0)
                            neumann_ev(lv, SETS[1], 1)
                        chunk_back(c0, SETS[0])
                        chunk_back(c0 + 1, SETS[1])
                    gn_pair(i)
                else:
                    act(E1.t[:], SIG.t[:], AF.Exp, [SIG.r()], [E1.r()], scale=-LAM)
                    bk, rb = nb()
                    for q, src in enumerate((ZSR, ZSK, ZSV, E1)):
                        tr(bk[0:NS, q * 128:(q + 1) * 128], src.t[:], 128, [src.r()], rb)
                    cp(Q6.t[0:NS, 0:4, i * 128:(i + 1) * 128], bk[0:NS, :].rearrange("p (a b) -> p a b", a=4), [rb], [Q6.r()], "act")
                    bk, rb = nb()
                    for q, src in enumerate((KKN, AA)):
                        tr(bk[0:NS, q * 128:(q + 1) * 128], src.t[:], 128, [src.r()], rb)
                    cp(Q6.t[0:NS, 4:6, i * 128:(i + 1) * 128], bk[0:NS, 0:256].rearrange("p (a b) -> p a b", a=2), [rb], [Q6.r()], "act")

            if sample:
                STG = av("stg", [128, 2304])
                for q0 in range(0, 18, 4):
                    bk, rb = nb()
                    n = min(4, 18 - q0)
                    for q in range(n):
                        tr(bk[0:NS, q * 128:(q + 1) * 128], ZB.t[:, q0 + q, H0:H0 + NS], 128, [ZB.r(q0 + q)], rb)
                    cp(STG.t[0:NS, q0 * 128:(q0 + n) * 128], bk[0:NS, 0:n * 128], [rb], [STG.r()], ev_eng())
                P.dma("sp", O["pools"][:, 14, :], STG.t[0:NS, 0:512], reads=[STG.r()], owner=STG.r(), final=True)
                P.dma("sp", O["shifts"], STG.t[0:NS, 512:2304], reads=[STG.r()], owner=STG.r(), final=True)
                P.barrier(pool=True)
                arelease(AMARK)
                SST = av("sst", [128, 64, 64]); ST1 = av("st1", [128, 64, 64]); ST2 = av("st2", [128, 64, 64])
                S1 = P.res("scr1")
                P.dma("sp", scr1.rearrange("q b f -> b q f"), Q6.t[0:NS, :, :], reads=[Q6.r()], writes=[S1], owner=S1)
                P.dma("sp", QR.t[:], scr1.rearrange("q b (h k) -> (b h) q k", k=64), reads=[S1], writes=[QR.r()], owner=QR.r())
                P.dma("sp", SST.t[:].rearrange("p a b -> p (a b)"), I["swkv"], writes=[SST.r()], owner=SST.r())
                bc_k = lambda q: QR.t[:, q, :].unsqueeze(1).to_broadcast([128, 64, 64])
                bc_v = lambda ap: ap.unsqueeze(2).to_broadcast([128, 64, 64])
                tt(ST1.t[:], SST.t[:], bc_k(4), ALU.mult, [SST.r(), QR.r()], [ST1.r()])
                P.op("dve", lambda e: e.tensor_reduce(out=SAV.t[:], in_=ST1.t[:], axis=AX.X, op=ALU.add, negate=True), reads=[ST1.r()], writes=[SAV.r()])
                tt(ST2.t[:], SST.t[:], bc_k(3), ALU.mult, [SST.r(), QR.r()], [ST2.r()])
                tt(ST1.t[:], bc_v(SAV.t[:]), bc_k(5), ALU.mult, [SAV.r(), QR.r()], [ST1.r()])
                tt(ST2.t[:], ST2.t[:], ST1.t[:], ALU.add, [ST2.r(), ST1.r()], [ST2.r()])
                tt(ST1.t[:], bc_v(QR.t[:, 2, :]), bc_k(1), ALU.mult, [QR.r()], [ST1.r()])
                tt(ST2.t[:], ST2.t[:], ST1.t[:], ALU.add, [ST2.r(), ST1.r()], [ST2.r()])
                P.dma("sp", O["wkvs"], ST2.t[:].rearrange("p a b -> p (a b)"), reads=[ST2.r()], owner=ST2.r(), final=True)
                tt(ST1.t[:], ST2.t[:], bc_k(0), ALU.mult, [ST2.r(), QR.r()], [ST1.r()])
                P.op("dve", lambda e: e.tensor_reduce(out=YV.t[:], in_=ST1.t[:], axis=AX.X, op=ALU.add), reads=[ST1.r()], writes=[YV.r()])
                S2 = P.res("scr2")
                P.dma("sp", scr2.rearrange("b (h k) -> (b h) k", k=64), YV.t[:], reads=[YV.r()], writes=[S2], owner=S2)
                P.dma("sp", YTS.t[0:NS, :], scr2, reads=[S2], writes=[YTS.r()], owner=YTS.r())
                for i in range(4):
                    bk, rb = nb()
                    tr(bk[:, 0:NS], YTS.t[0:NS, i * 128:(i + 1) * 128], NS, [YTS.r()], rb)
                    cp(YTP.t[:], bk[:, 0:NS], [rb], [YTP.r()], "dve")
                    prep_pair(i, True)
                    gn_pair(i)

            proj_down(Tn, I["w_out_ab"], 8, lambda k: (YC.t[:, k, :], YC.r(k)))

            if not sample:
                cp(HIS.t[:], ZB.t[:, :, H0 + Tn - 16:H0 + Tn], ZB.res, [HIS.r()], "dve")
                if blk == 3:
                    STP = T(SIG.t[:], SIG.res); STS = T(CSUM.t[:, 0:128], CSUM.res); WK = T(AA.t[:].rearrange("p (a b) -> p a b", a=4), AA.res)
                    bk, rb = nb()
                    for g in range(4):
                        tr(bk[0:15, g * 128:(g + 1) * 128], ZB.t[:, g, H0 + Tn - 15:H0 + Tn], 128, [ZB.r(g)], rb)
                    cp(STP.t[0:15, :], bk[0:15, :], [rb], [STP.r()], "dve")
                    P.dma("sp", O["poolp"], STP.t[0:15, :], reads=[STP.r()], owner=STP.r(), final=True)
                    bk, rb = nb()
                    tr(bk[0:14, 0:128], ZB.t[:, 4:18, H0 + Tn - 1], 128, ZB.res, rb)
                    cp(STS.t[0:14, :], bk[0:14, 0:128], [rb], [STS.r()], "dve")
                    P.dma("sp", O["shiftp"], STS.t[0:14, :], reads=[STS.r()], owner=STS.r(), final=True)
                    for i in range(4):
                        bk, rb = nb()
                        tr(bk[:, 0:128], SBLK.t[:, i, :].bitcast(F32), 128, [SBLK.r(i)], rb)
                        cp(WK.t[:, i, :], bk[:, 0:128], [rb], [WK.r()], ev_eng())
                    wv = O["wkvp"].rearrange("(i j) v k -> j v i k", j=2)
                    for j in range(2):
                        js = slice(j * 64, (j + 1) * 64)
                        P.dma("sp", wv[j], WK.t[js, :, js], reads=[WK.r()], owner=WK.r(), final=True)

        def mixer_c(blk, sample, Tn, nch, CH):
            UU = av("uT", [128, 16, Tn], BF16, 16)
            VT = [av("vtok%d" % a, [128, 2048], F32R) for a in range(nch)]
            TMPS = [av("sptmp%d" % k, [128, Tn]) for k in range(2)]
            if not sample:
                CC = av("cc", [128, 16, 128]); BSB = av("bsb", [128, 512]); ONF = av("onf", [128, 128], F32R)
                P.dma("sp", BSB.t[:], I["bsb"], writes=[BSB.r()], owner=BSB.r())
                fill_r(ONF.t[:], 1.0, [ONF.r()])
                for g in range(4):
                    bk, rb = nb()
                    mm(bk[:, 0:128], ONF.t[:], WMT.t[:, g, :], True, True, [ONF.r(), WMT.r()], rb)
                    RS = av("rs%d" % g, [128, 128])
                    cp(RS.t[:], bk[:, 0:128], [rb], [RS.r()])
                    for q in range(4):
                        dcq = g * 4 + q
                        stt(CC.t[:, dcq, :], RS.t[:], prm("ln_b", dcq), BSB.t[:, g * 128:(g + 1) * 128], ALU.mult, ALU.add,
                            [RS.r(), PRM.r(), BSB.r()], [CC.r()])
            for cgp in range(4):
                s = wload(I["w_in_c"][:, 2048 + cgp * 512:2048 + (cgp + 1) * 512], 512)
                for a in range(nch):
                    bk, rb = nb()
                    for kc in range(8):
                        mm(bk[0:CH, :], CUR["HT"].t[:, kc, a * CH:(a + 1) * CH], s.t[:, kc, :], kc == 0, kc == 7, [s.r(), CUR["HT"].r(kc)], rb)
                    act(VT[a].t[0:CH, cgp * 512:(cgp + 1) * 512], bk[0:CH, :], AF.Gelu, [rb], [VT[a].r()])
            for a in range(nch):
                ST6 = av("st6_%d" % a, [128, 24]); MV = av("mvv%d" % a, [128, 2]); RSD = av("rsd%d" % a, [128, 1])
                vf = VT[a].t[0:CH, :].bitcast(F32)
                for q in range(4):
                    P.op("dve", lambda e, q=q, vf=vf, ST6=ST6: e.bn_stats(out=ST6.t[0:CH, q * 6:(q + 1) * 6], in_=vf[:, q * 512:(q + 1) * 512]),
                         reads=[VT[a].r()], writes=[ST6.r()])
                P.op("dve", lambda e, ST6=ST6, MV=MV: e.bn_aggr(out=MV.t[0:CH, :], in_=ST6.t[0:CH, :]), reads=[ST6.r()], writes=[MV.r()])
                act(RSD.t[0:CH, :], MV.t[0:CH, 1:2], AF.Ln, [MV.r()], [RSD.r()], bias=LN_EPS)
                act(RSD.t[0:CH, :], RSD.t[0:CH, :], AF.Exp, [RSD.r()], [RSD.r()], scale=-0.5)
                ts(VT[a].t[0:CH, :], vf, MV.t[0:CH, 0:1], RSD.t[0:CH, 0:1], ALU.subtract, ALU.mult, [VT[a].r(), MV.r(), RSD.r()], [VT[a].r()])
            if sample:
                LGB = av("lgb", [NS, 2048]); LBB = av("lbb", [NS, 2048]); SV = av("sv", [NS, 2048])
                P.dma("sp", LGB.t[:], I["lngb"], writes=[LGB.r()], owner=LGB.r())
                P.dma("sp", LBB.t[:], I["lnbb"], writes=[LBB.r()], owner=LBB.r())
                tt(SV.t[:], VT[0].t[0:NS, :].bitcast(F32), LGB.t[:], ALU.mult, [VT[0].r(), LGB.r()], [SV.r()])
                tt(SV.t[:], SV.t[:], LBB.t[:], ALU.add, [SV.r(), LBB.r()], [SV.r()])
                P.dma("sp", O["sguv"], SV.t[:], reads=[SV.r()], owner=SV.r(), final=True)

            def ev_u(m, bk, rb):
                act(UU.t[:, m, :], bk, AF.Gelu, [rb], [UU.r(m)])
            proj_fm(Tn, I["w_in_c"][:, 0:2048], 8, 2048, hrhs(Tn), ev_u)
            for dc in range(16):
                g = dc // 4
                bk, rb = nb()
                if sample:
                    mm(bk[:, 0:NS], VT[0].t[0:NS, dc * 128:(dc + 1) * 128], W00I.t[:, g, :], True, True, [VT[0].r(), W00I.r()], rb)
                else:
                    for a in range(nch):
                        mm(bk[:, a * 128:(a + 1) * 128], VT[a].t[:, dc * 128:(dc + 1) * 128], WMT.t[:, g, :], True, True, [VT[a].r(), WMT.r()], rb)
                t = TMPS[dc % 2]
                if sample:
                    ts(t.t[:], bk[:, 0:NS], prm("ln_g", dc), CS16.t[:, dc:dc + 1], ALU.mult, ALU.add, [rb, PRM.r(), CS16.r()], [t.r()])
                else:
                    stt(t.t[:].rearrange("p (a i) -> p a i", a=4), bk[:, :].rearrange("p (a i) -> p a i", a=4), prm("ln_g", dc),
                        CC.t[:, dc, :].unsqueeze(1).to_broadcast([128, 4, 128]), ALU.mult, ALU.add, [rb, PRM.r(), CC.r()], [t.r()])
                tt(UU.t[:, dc, :], t.t[:], UU.t[:, dc, :], ALU.mult, [t.r(), UU.r(dc)], [UU.r(dc)])
            proj_down(Tn, I["w_out_c"], 16, lambda k: (UU.t[:, k, :], UU.r(k)))

        def xattn_prompt(layer):
            QT = av("qT", [128, 8, 512], BF16, 8)

            def ev_q(m, bk, rb):
                cp(QT.t[:, m, :], bk, [rb], [QT.r(m)], ev_eng())
            proj_fm(512, I["w_xq"][layer], 8, 1024, hrhs(512), ev_q)
            PT = av("pT", [128, 4, 2, 512], BF16, 4)
            OT = av("oT", [128, 8, 512], BF16, 8)
            ASET = [dict(PF=av("pf%d" % k_, [128, 4, 256]), MX=av("mx%d" % k_, [128, 4]), NMX=av("nmx%d" % k_, [128, 4]),
                         SSUM=av("ssum%d" % k_, [128, 4]), RSUM=av("rsum%d" % k_, [128, 4])) for k_ in range(2)]

            def att_scores(c, S):
                banks = []
                for hp in range(2):
                    bk, rb = nb()
                    banks.append((bk, rb))
                    for hh in range(2):
                        hd = hp * 2 + hh
                        for dd in range(2):
                            mm(bk[:, hh * 256:(hh + 1) * 256], QT.t[:, 2 * hd + dd, c * 128:(c + 1) * 128], KT[layer].t[:, 2 * hd + dd, :],
                               dd == 0, dd == 1, [QT.r(2 * hd + dd), KT[layer].r()], rb)
                S["banks"] = banks

            def att_softmax(c, S):
                PF, MX, NMX, SSUM, RSUM = S["PF"], S["MX"], S["NMX"], S["SSUM"], S["RSUM"]
                for hp in range(2):
                    bk, rb = S["banks"][hp]
                    P.op("dve", lambda e, bk=bk, hp=hp, MX=MX: e.tensor_reduce(out=MX.t[:, hp * 2:hp * 2 + 2], in_=bk[:, :].rearrange("p (h m) -> p h m", h=2),
                                                                              axis=AX.X, op=ALU.max), reads=[rb], writes=[MX.r()])
                ts(NMX.t[:], MX.t[:], -1.0 / 16.0, None, ALU.mult, None, [MX.r()], [NMX.r()])
                for hd in range(4):
                    bk, rb = S["banks"][hd // 2]
                    act(PF.t[:, hd, :], bk[:, (hd % 2) * 256:(hd % 2 + 1) * 256], AF.Exp, [rb, NMX.r()], [PF.r(), SSUM.r()],
                        bias=NMX.t[:, hd:hd + 1], scale=1.0 / 16.0, accum=SSUM.t[:, hd:hd + 1])
                P.op("dve", lambda e, RSUM=RSUM, SSUM=SSUM: e.reciprocal(out=RSUM.t[:], in_=SSUM.t[:]), reads=[SSUM.r()], writes=[RSUM.r()])
                tt(PF.t[:], PF.t[:], RSUM.t[:].unsqueeze(2).to_broadcast([128, 4, 256]), ALU.mult, [PF.r(), RSUM.r()], [PF.r()])

            def att_transposes(c, S):
                PF = S["PF"]
                for hp in range(2):
                    bk, rb = nb()
                    for hh in range(2):
                        hd = hp * 2 + hh
                        for mc in range(2):
                            k = hh * 2 + mc
                            tr(bk[:, k * 128:(k + 1) * 128], PF.t[:, hd, mc * 128:(mc + 1) * 128], 128, [PF.r()], rb)
                    cp(PT.t[:, hp * 2:hp * 2 + 2, :, c * 128:(c + 1) * 128], bk[:, :].rearrange("p (h m t) -> p h m t", h=2, m=2),
                       [rb], [PT.r(hp * 2), PT.r(hp * 2 + 1)], ev_eng())

            att_scores(0, ASET[0])
            for c in range(4):
                if c + 1 < 4:
                    att_scores(c + 1, ASET[(c + 1) % 2])
                att_softmax(c, ASET[c % 2])
                att_transposes(c, ASET[c % 2])
            for j in range(8):
                hd = j // 2
                bk, rb = nb()
                for mc in range(2):
                    mm(bk[:, :], VM[layer].t[:, mc, j * 128:(j + 1) * 128], PT.t[:, hd, mc, :], mc == 0, mc == 1, [VM[layer].r(), PT.r(hd)], rb)
                cp(OT.t[:, j, :], bk[:, :], [rb], [OT.r(j)], ev_eng())
            proj_down(512, I["w_xo"][layer], 8, lambda k: (OT.t[:, k, :], OT.r(k)))

        def xattn_sample(layer):
            QTOK = av("qtok", [NS, 1024], BF16)
            SELA = av("sela", [16, 16, 128], BF16)
            DEL = av("del", [128, 16, 16])
            cp(DEL.t[:], ONES.t[:, 0:16].unsqueeze(2).to_broadcast([128, 16, 16]), [ONES.r()], [DEL.r()])
            P.op("pool", lambda e: e.affine_select(out=DEL.t[:], in_=DEL.t[:], pattern=[[1, 16], [-1, 16]], compare_op=ALU.is_equal,
                                                    fill=0.0, base=0, channel_multiplier=0), reads=[DEL.r()], writes=[DEL.r()])
            cp(SELA.t[:], IDN.t[0:16, 0:16].unsqueeze(2).to_broadcast([16, 16, 128]), [IDN.r()], [SELA.r()])
            for cg in range(2):
                s = wload(I["w_xq"][layer][:, cg * 512:(cg + 1) * 512], 512)
                bk, rb = nb()
                for kc in range(8):
                    mm(bk[0:NS, :], CUR["HT"].t[:, kc, 0:NS], s.t[:, kc, :], kc == 0, kc == 7, [s.r(), CUR["HT"].r(kc)], rb)
                cp(QTOK.t[:, cg * 512:(cg + 1) * 512], bk[0:NS, :], [rb], [QTOK.r()], ev_eng())
            KB_ = [av("kb%d" % k, [128, 2, 1024]) for k in range(2)]
            VB_ = [av("vb%d" % k, [128, 2, 1024], BF16) for k in range(2)]
            PRD = av("prd", [128, 1024])
            SC = av("sc", [128, 2, 64])
            for b in range(NS):
                kb = KB_[b % 2]
                P.dma("sp", kb.t[:], I["ck"][layer, b].rearrange("(a p) f -> p a f", p=128), writes=[kb.r()], owner=kb.r())
                bq = [nb(), nb()]
                for cg in range(2):
                    mm(bq[cg][0][:, :], SELA.t[:, b, :], QTOK.t[:, cg * 512:(cg + 1) * 512], True, True, [SELA.r(), QTOK.r()], bq[cg][1])
                for mc in range(2):
                    for cg in range(2):
                        tt(PRD.t[:, cg * 512:(cg + 1) * 512], kb.t[:, mc, cg * 512:(cg + 1) * 512], bq[cg][0][:, :], ALU.mult,
                           [kb.r(), bq[cg][1]], [PRD.r()])
                    P.op("dve", lambda e, mc=mc, b=b: e.tensor_reduce(out=SC.t[:, mc, b * 4:(b + 1) * 4], in_=PRD.t[:].rearrange("p (h d) -> p h d", h=4),
                                                                      axis=AX.X, op=ALU.add), reads=[PRD.r()], writes=[SC.r()])
            MX = av("mxs", [128, 1]); NMX = av("nmxs", [128, 1]); SSUM = av("ssums", [128, 1]); RSUM = av("rsums", [128, 1])
            PSM = av("psm", [128, 256]); PTS = av("pts", [128, 2, 64])
            PEX = av("pex", [128, 2, 16, 4, 16], BF16)
            bk, rb = nb()
            for mc in range(2):
                tr(bk[0:64, mc * 128:(mc + 1) * 128], SC.t[:, mc, :], 128, [SC.r()], rb)
            P.op("dve", lambda e: e.tensor_reduce(out=MX.t[0:64, :], in_=bk[0:64, 0:256], axis=AX.X, op=ALU.max), reads=[rb], writes=[MX.r()])
            ts(NMX.t[0:64, :], MX.t[0:64, :], -1.0 / 16.0, None, ALU.mult, None, [MX.r()], [NMX.r()])
            act(PSM.t[0:64, :], bk[0:64, 0:256], AF.Exp, [rb, NMX.r()], [PSM.r(), SSUM.r()], bias=NMX.t[0:64, 0:1], scale=1.0 / 16.0,
                accum=SSUM.t[0:64, 0:1])
            P.op("dve", lambda e: e.reciprocal(out=RSUM.t[0:64, :], in_=SSUM.t[0:64, :]), reads=[SSUM.r()], writes=[RSUM.r()])
            ts(PSM.t[0:64, :], PSM.t[0:64, :], RSUM.t[0:64, 0:1], None, ALU.mult, None, [PSM.r(), RSUM.r()], [PSM.r()])
            bk, rb = nb()
            for mc in range(2):
                tr(bk[:, mc * 64:(mc + 1) * 64], PSM.t[0:64, mc * 128:(mc + 1) * 128], 64, [PSM.r()], rb)
            cp(PTS.t[:].rearrange("p a b -> p (a b)"), bk[:, 0:128], [rb], [PTS.r()], "dve")
            for mc in range(2):
                tt(PEX.t[:, mc], PTS.t[:, mc, :].rearrange("p (b h) -> p b h", h=4).unsqueeze(3).to_broadcast([128, 16, 4, 16]),
                   DEL.t[:].unsqueeze(2).to_broadcast([128, 16, 4, 16]), ALU.mult, [PTS.r(), DEL.r()], [PEX.r()])
            bo = [nb() for _ in range(4)]
            for b in range(NS):
                vb = VB_[b % 2]
                P.dma("pool", vb.t[:], I["cv"][layer, b].rearrange("(a p) f -> p a f", p=128), writes=[vb.r()], owner=vb.r())
                for h in range(4):
                    for mc in range(2):
                        mm(bo[h][0][0:NS, 0:256], PEX.t[:, mc, b, h, :], vb.t[:, mc, h * 256:(h + 1) * 256], b == 0 and mc == 0,
                           b == NS - 1 and mc == 1, [PEX.r(), vb.r()], bo[h][1])
            OTOK = av("otok", [NS, 1024])
            for h in range(4):
                cp(OTOK.t[:, h * 256:(h + 1) * 256], bo[h][0][0:NS, 0:256], [bo[h][1]], [OTOK.r()], ev_eng())
            OTS = av("oTs", [128, 8, NS], BF16, 8)
            bk, rb = nb()
            for j in range(8):
                tr(bk[:, j * NS:(j + 1) * NS], OTOK.t[:, j * 128:(j + 1) * 128], NS, [OTOK.r()], rb)
            cp(OTS.t[:], bk[:, 0:8 * NS].rearrange("p (j b) -> p j b", j=8), [rb], OTS.res, "dve")
            proj_down(NS, I["w_xo"][layer], 8, lambda k: (OTS.t[:, k, :], OTS.r(k)))

        try:
            stage("memkv")
            for blk in range(3):
                run_block(blk, False)
                stage("b%d" % blk)
            run_joint()
        except StopBuild:
            pass
        P.finish("sp")
        P.emit()
    return nc


def _fm(v):
    v = np.asarray(v, np.float32).reshape(-1, 128)
    return np.ascontiguousarray(v.T)


_NC_CACHE = {}
STOP_AT = None


class StopBuild(Exception):
    pass


def stage(name):
    if STOP_AT is not None and name == STOP_AT:
        raise StopBuild()
_PREP_ONLY = False


def kernel(x_prompt, x_sample, mem_prompt, cache_mem_k, cache_mem_v, state_pool, state_shift, state_wkv,
           norm_mix_g, norm_xa_g, norm_mem_g, norm_ffn_g, norm_final_g,
           w_in_ab, w_out_ab, pool_w, pool_scale,
           rwkv_mu, rwkv_w0, rwkv_w2, rwkv_a0, rwkv_a2, rwkv_g2, rwkv_k_k, rwkv_k_a, rwkv_r_k,
           rwkv_gn_g, rwkv_gn_b,
           w_in_c, sgu_ln_g, sgu_ln_b, sgu_w_s, sgu_b_s, w_out_c,
           w_xq, w_xk, w_xv, w_xo, w_ff_up, w_ff_down):
    f = lambda a: np.ascontiguousarray(np.asarray(a, dtype=np.float32))
    cols = [_fm(norm_mix_g[0]), _fm(norm_xa_g[0]), _fm(norm_ffn_g[0]), _fm(norm_mix_g[1]), _fm(norm_xa_g[1]), _fm(norm_ffn_g[1]),
            _fm(norm_mem_g[0]), _fm(norm_mem_g[1]), _fm(norm_final_g), _fm(pool_scale[0]), _fm(rwkv_mu[0]), _fm(rwkv_w0[0]),
            _fm(rwkv_a0[0]), _fm(rwkv_k_k[0]), _fm(rwkv_k_a[0]), _fm(np.asarray(rwkv_r_k[0]).reshape(-1)), _fm(rwkv_gn_g[0]),
            _fm(rwkv_gn_b[0]), _fm(sgu_ln_g[0]), _fm(sgu_ln_b[0])]
    prm = np.ascontiguousarray(np.concatenate(cols, axis=1))
    assert prm.shape == (128, NPRM)
    ws = np.asarray(sgu_w_s[0], np.float32)
    bs = np.asarray(sgu_b_s[0], np.float32)
    shared = dict(
        prm=prm,
        bsb=np.ascontiguousarray(np.broadcast_to(bs.reshape(1, 512), (128, 512))),
        smp=np.ascontiguousarray(np.broadcast_to(np.concatenate([ws[:, 0, 0], bs[:, 0]]).reshape(1, 8), (128, 8))),
        lngb=np.ascontiguousarray(np.broadcast_to(np.asarray(sgu_ln_g[0], np.float32).reshape(1, 2048), (NS, 2048))),
        lnbb=np.ascontiguousarray(np.broadcast_to(np.asarray(sgu_ln_b[0], np.float32).reshape(1, 2048), (NS, 2048))),
        w_in_ab=f(w_in_ab[0]), w_out_ab=f(w_out_ab[0]),
        pool_w=np.ascontiguousarray(np.transpose(np.asarray(pool_w[0], np.float32), (1, 0, 2))),
        w2=f(rwkv_w2[0]), a2=f(rwkv_a2[0]), g2=f(rwkv_g2[0]),
        w_in_c=f(w_in_c[0]), wsT=np.ascontiguousarray(np.transpose(ws, (2, 0, 1))),
        w_out_c=f(w_out_c[0]),
        w_xq=f(w_xq), w_xk=f(w_xk), w_xv=f(w_xv), w_xo=f(w_xo), w_ff_up=f(w_ff_up), w_ff_down=f(w_ff_down),
    )
    xp = f(x_prompt); xs = f(x_sample); mp = f(mem_prompt); ck = f(cache_mem_k); cv = f(cache_mem_v)
    sp = f(state_pool); ssh = f(state_shift); swk = f(state_wkv)
    in_maps = []
    for c in range(NCORES):
        sl = slice(c * NS, (c + 1) * NS)
        m = dict(shared)
        m.update(
            xp=xp[c], xs=np.ascontiguousarray(xs[sl, 0, :]), memp=mp[c],
            ck=np.ascontiguousarray(ck[:, sl].reshape(2, NS, 256, D)), cv=np.ascontiguousarray(cv[:, sl].reshape(2, NS, 256, D)),
            spool=np.ascontiguousarray(sp[0, sl]), sshift=np.ascontiguousarray(ssh[0, sl]),
            swkv=np.ascontiguousarray(swk[0, sl].reshape(128, 4096)),
        )
        in_maps.append(m)
    if _PREP_ONLY:
        return in_maps
    if "nc" not in _NC_CACHE:
        _NC_CACHE["nc"] = build_nc()
    nc = _NC_CACHE["nc"]
    res = run_bass_kernel_spmd(nc, in_maps, core_ids=list(range(NCORES))).results
    return _gather(res)


def _gather(res):
    B = NCORES
    y_prompt = np.stack([res[c]["yp"] for c in range(B)]).astype(np.float32)
    y_sample = np.concatenate([res[c]["ys"] for c in range(B)]).reshape(128, 1, D).astype(np.float32)
    mem_k = np.stack([res[c]["mk"] for c in range(B)], axis=1).reshape(2, B, 256, 4, 256).astype(np.float32)
    mem_v = np.stack([res[c]["mv"] for c in range(B)], axis=1).reshape(2, B, 256, 4, 256).astype(np.float32)
    pool_p = np.stack([res[c]["poolp"] for c in range(B)])[None].astype(np.float32)
    pool_s = np.concatenate([res[c]["pools"] for c in range(B)])[None].astype(np.float32)
    shift_p = np.stack([res[c]["shiftp"].reshape(1792) for c in range(B)])[None].astype(np.float32)
    shift_s = np.concatenate([res[c]["shifts"] for c in range(B)])[None].astype(np.float32)
    wkv_p = np.stack([res[c]["wkvp"] for c in range(B)])[None].astype(np.float32)
    wkv_s = np.concatenate([res[c]["wkvs"].reshape(NS, 8, 64, 64) for c in range(B)])[None].astype(np.float32)
    sgu_v = np.concatenate([res[c]["sguv"] for c in range(B)]).reshape(1, 128, 1, 2048).astype(np.float32)
    return (y_prompt, y_sample, mem_k, mem_v, pool_p, pool_s, shift_p, shift_s, wkv_p, wkv_s, sgu_v)
```

```python
from contextlib import ExitStack
import numpy as np
import concourse.bass as bass
import concourse.mybir as mybir
from concourse.bass_utils import run_bass_kernel_spmd

F32 = mybir.dt.float32
F32R = mybir.dt.float32r
BF16 = mybir.dt.bfloat16
AF = mybir.ActivationFunctionType
ALU = mybir.AluOpType
AX = mybir.AxisListType

NCORES = 8
D = 1024
SEQ = 2048
NS = 16
LAM = 0.6065306597126334
WARM_LV = 2
WARM_PREP = 24
RMS_EPS = 1e-5
LN_EPS = 1e-5
GN_EPS = 64 * 1e-5
SAME_ENGINE_SYNC = True

PC = {}
_o = 0
for _n, _c in [("g_mix0", 8), ("g_xa0", 8), ("g_ffn0", 8), ("g_mix1", 8), ("g_xa1", 8), ("g_ffn1", 8),
               ("g_mem0", 8), ("g_mem1", 8), ("g_fin", 8), ("pscale", 4), ("mu", 14), ("w0", 4), ("a0", 4),
               ("k_k", 4), ("k_a", 4), ("r_k", 4), ("gn_g", 4), ("gn_b", 4), ("ln_g", 16), ("ln_b", 16)]:
    PC[_n] = _o
    _o += _c
NPRM = _o


class Res:
    __slots__ = ("name", "w", "r", "dsem", "dcnt", "psum")

    def __init__(self, name):
        self.name = name
        self.psum = False
        self.w = None
        self.r = {}
        self.dsem = None
        self.dcnt = 0


class Prog:
    ENG = ("pe", "act", "dve", "pool", "sp")

    def __init__(self, nc, stack):
        self.nc = nc
        self.stack = stack
        self.q = {e: [] for e in self.ENG}
        self.sems = {}
        self.cnt = {e: 0 for e in self.ENG}
        self.seen = {e: {} for e in self.ENG}
        self.nsem = 0
        for e in self.ENG:
            self.sems[e] = stack.enter_context(nc.semaphore("s_" + e))
            self.nsem += 1
        self.final = {}
        self.alld = {}

    def res(self, name):
        return Res(name)

    def _dsem(self, r):
        if r.dsem is None:
            key = "d%d" % self.nsem
            self.sems[key] = self.stack.enter_context(self.nc.semaphore(key))
            self.nsem += 1
            r.dsem = key
        return r.dsem

    def _need(self, e, reads, writes):
        need = {}

        def add(tok):
            if tok is None:
                return
            k, v = tok
            if need.get(k, 0) < v:
                need[k] = v
        for r in reads:
            add(r.w)
            if r.psum:
                for k, v in r.r.items():
                    if k != e:
                        add((k, v))
        for w in writes:
            add(w.w)
            for k, v in w.r.items():
                add((k, v))
        out = []
        for k, v in need.items():
            if k == e and (e == "pe" or not SAME_ENGINE_SYNC):
                continue
            if self.seen[e].get(k, 0) < v:
                self.seen[e][k] = v
                out.append((k, v))
        return out

    def _emit_waits(self, e, waits):
        for k, v in waits:
            sem = self.sems[k]
            self.q[e].append(lambda eng, sem=sem, v=v: eng.wait_ge(sem, v))

    def _mark(self, tok, reads, writes):
        k, v = tok
        for r in reads:
            if r.r.get(k, 0) < v:
                r.r[k] = v
        for w in writes:
            w.w = tok
            w.r = {}

    def op(self, e, fn, reads=(), writes=()):
        self._emit_waits(e, self._need(e, reads, writes))
        self.cnt[e] += 1
        sem = self.sems[e]
        self.q[e].append(lambda eng, fn=fn, sem=sem: fn(eng).then_inc(sem, 1))
        self._mark((e, self.cnt[e]), reads, writes)

    def dma(self, e, out, in_, reads=(), writes=(), owner=None, final=False, track=True, **kw):
        self._emit_waits(e, self._need(e, reads, writes))
        key = self._dsem(owner)
        owner.dcnt += 16
        sem = self.sems[key]
        self.q[e].append(lambda eng, out=out, in_=in_, sem=sem, kw=kw:
                         eng.dma_start(out=out, in_=in_, **kw).then_inc(sem, 16))
        self._mark((key, owner.dcnt), reads, writes)
        if final:
            self.final[key] = owner.dcnt
        if track:
            self.alld[key] = owner.dcnt

    def barrier_real(self, pool=True):
        engs = ("pe", "act", "dve", "pool", "sp") if pool else ("pe", "act", "dve", "sp")
        e0 = "sp"
        w = []
        for k, v in self.alld.items():
            if self.seen[e0].get(k, 0) < v:
                self.seen[e0][k] = v
                w.append((k, v))
        for o in engs:
            if o != e0 and self.seen[e0].get(o, 0) < self.cnt[o]:
                self.seen[e0][o] = self.cnt[o]
                w.append((o, self.cnt[o]))
        self._emit_waits(e0, w)
        self.cnt[e0] += 1
        sem = self.sems[e0]
        self.q[e0].append(lambda eng, sem=sem: eng.nop().then_inc(sem, 1))
        for o in engs:
            if o == e0:
                continue
            self.seen[o][e0] = self.cnt[e0]
            self._emit_waits(o, [(e0, self.cnt[e0])])
            for o2 in engs:
                self.seen[o][o2] = max(self.seen[o].get(o2, 0), self.cnt[o2])
            for k, v in self.alld.items():
                self.seen[o][k] = max(self.seen[o].get(k, 0), v)

    def barrier(self, pool=True):
        return

    def finish(self, e="sp"):
        for k, v in self.final.items():
            sem = self.sems[k]
            self.q[e].append(lambda eng, sem=sem, v=v: eng.wait_ge(sem, v))

    def emit(self):
        with self.nc.Block() as blk:
            def run(name):
                def f(eng):
                    for g in self.q[name]:
                        g(eng)
                return f
            blk.tensor(run("pe"))
            blk.scalar(run("act"))
            blk.vector(run("dve"))
            blk.gpsimd(run("pool"))
            blk.sync(run("sp"))


class T:
    def __init__(self, t, res):
        self.t = t
        self.res = res

    def r(self, i=0):
        return self.res[i if len(self.res) > 1 else 0]


def build_nc():
    nc = bass.Bass("TRN2", target_bir_lowering=False)

    def din(name, shape):
        return nc.dram_tensor(name, list(shape), F32, kind="ExternalInput").ap()

    def dout(name, shape):
        return nc.dram_tensor(name, list(shape), F32, kind="ExternalOutput").ap()

    I = dict(
        xp=din("xp", [SEQ, D]), xs=din("xs", [NS, D]), memp=din("memp", [256, D]),
        ck=din("ck", [2, NS, 256, D]), cv=din("cv", [2, NS, 256, D]),
        spool=din("spool", [NS, 15, 512]), sshift=din("sshift", [NS, 1792]), swkv=din("swkv", [128, 4096]),
        prm=din("prm", [128, NPRM]), bsb=din("bsb", [128, 512]), smp=din("smp", [128, 8]),
        lngb=din("lngb", [NS, 2048]), lnbb=din("lnbb", [NS, 2048]),
        w_in_ab=din("w_in_ab", [D, 2304]), w_out_ab=din("w_out_ab", [D, D]),
        pool_w=din("pool_w", [128, 4, 128]), w2=din("w2", [64, 512]), a2=din("a2", [64, 512]), g2=din("g2", [128, 512]),
        w_in_c=din("w_in_c", [D, 4096]), wsT=din("wsT", [128, 4, 128]), w_out_c=din("w_out_c", [2048, D]),
        w_xq=din("w_xq", [2, D, D]), w_xk=din("w_xk", [2, D, D]), w_xv=din("w_xv", [2, D, D]), w_xo=din("w_xo", [2, D, D]),
        w_ff_up=din("w_ff_up", [2, D, 4096]), w_ff_down=din("w_ff_down", [2, 4096, D]),
    )
    O = dict(
        yp=dout("yp", [SEQ, D]), ys=dout("ys", [NS, D]), mk=dout("mk", [2, 256, D]), mv=dout("mv", [2, 256, D]),
        poolp=dout("poolp", [15, 512]), pools=dout("pools", [NS, 15, 512]),
        shiftp=dout("shiftp", [14, 128]), shifts=dout("shifts", [NS, 1792]),
        wkvp=dout("wkvp", [8, 64, 64]), wkvs=dout("wkvs", [128, 4096]), sguv=dout("sguv", [NS, 2048]),
    )
    scr1 = nc.dram_tensor("scr1", [6, NS, 512], F32, kind="Internal").ap()
    scr2 = nc.dram_tensor("scr2", [NS, 512], F32, kind="Internal").ap()

    with ExitStack() as st:
        P = Prog(nc, st)

        def sb(name, shape, dt=F32, nres=1):
            t = st.enter_context(nc.sbuf_tensor("s_" + name, list(shape), dt))
            return T(t, [P.res("%s%d" % (name, i)) for i in range(nres)])

        PS = [st.enter_context(nc.psum_tensor("ps%d" % i, [128, 512], F32)) for i in range(8)]
        RPS = [P.res("ps%d" % i) for i in range(8)]
        for r_ in RPS:
            r_.psum = True
        pctr = [0]

        NBANK = [8]

        def nb():
            i = pctr[0] % NBANK[0]
            pctr[0] += 1
            return PS[i], RPS[i]

        def warm(n):
            for _ in range(n):
                P.op("pe", lambda e: e.matmul(PS[7][:, :], WARML.t[:], WARMR.t[:], start=True, stop=True),
                     reads=[WARML.r(), WARMR.r()], writes=[RPS[7]])

        NSLOT = 3
        WS = [sb("ws%d" % i, [128, 8, 512], BF16) for i in range(NSLOT)]
        wctr = [0]
        XT = sb("xT", [128, 8, 512], F32, 8)
        HT = sb("hT", [128, 8, 512], BF16, 8)
        XT_S = sb("xTs", [128, 8, NS], F32, 8)
        HT_S = sb("hTs", [128, 8, NS], BF16, 8)
        CUR = {"XT": XT, "HT": HT}

        def set_stream(sample):
            CUR["XT"] = XT_S if sample else XT
            CUR["HT"] = HT_S if sample else HT
        KT = [sb("kT%d" % l, [128, 8, 256], BF16) for l in range(2)]
        VM = [sb("vM%d" % l, [128, 2, 1024], BF16) for l in range(2)]
        PRM = sb("prm", [128, NPRM])
        IDN = sb("ident", [128, 128])
        MU2 = sb("mU2", [128, 512])
        MSL = sb("mSL", [128, 256])
        BO1 = sb("bo1", [128, 128], F32R)
        BO64 = sb("bo64", [128, 128], F32R)
        ON1K = sb("on1k", [128, 128], F32R)
        ONES = sb("ones", [128, 128])
        W2X = sb("w2x", [128, 512], F32R)
        A2X = sb("a2x", [128, 512], F32R)
        G2 = sb("g2", [128, 512], F32R)
        PW = sb("pw", [128, 4, 128], F32R)
        WMT = sb("wmT", [128, 4, 128], F32R)
        SMP = sb("smp", [128, 8])
        CS16 = sb("cs16", [128, 16])
        W00I = sb("w00i", [16, 4, 16], F32R)
        ICNT = sb("icnt", [128, 4, 16])
        SBLK = sb("sblk", [128, 4, 128], F32R, 4)
        SQ = [sb("sq%d" % i, [128, 512], F32R) for i in range(2)]
        RSTD = sb("rstd", [128, 512])
        ZHIS = sb("zhist", [128, 18, 16])
        WARML = sb("warml", [128, 128], BF16)
        WARMR = sb("warmr", [128, 512], BF16)
        sqc = [0]

        ARENA_R_W = 9216
        ARENA_F_W = 29520 - ARENA_R_W
        ARENAF = st.enter_context(nc.sbuf_tensor("arenaf", [128, ARENA_F_W], F32))
        ARENAR = st.enter_context(nc.sbuf_tensor("arenar", [128, ARENA_R_W], F32R))
        aoff = {"f": 0, "r": 0}
        LIVE = []
        REGION = {"f": [], "r": []}

        def _tokens(t):
            d = {}
            for r_ in t.res:
                if r_.w is not None and d.get(r_.w[0], 0) < r_.w[1]:
                    d[r_.w[0]] = r_.w[1]
                for k_, v_ in r_.r.items():
                    if d.get(k_, 0) < v_:
                        d[k_] = v_
            return d

        def _retire(pred):
            keep = []
            for ent in LIVE:
                key, o, end, t = ent
                if pred(ent):
                    reg = REGION[key]
                    reg[:] = [e_ for e_ in reg if not (e_[0] >= o and e_[1] <= end)]
                    reg.append((o, end, _tokens(t)))
                else:
                    keep.append(ent)
            LIVE[:] = keep

        def arena_reset():
            _retire(lambda ent: True)
            aoff["f"] = 0
            aoff["r"] = 0

        def amark():
            return dict(aoff)

        def arelease(m):
            _retire(lambda ent: ent[1] >= m[ent[0]])
            aoff.update(m)

        def av(name, shape, dt=F32, nres=1):
            nparts = shape[0]
            n = 1
            for s_ in shape[1:]:
                n *= s_
            words = n if dt in (F32, F32R) else (n + 1) // 2
            key = "r" if dt == F32R else "f"
            o = aoff[key]
            aoff[key] += (words + 7) // 8 * 8
            assert aoff[key] <= (ARENA_R_W if key == "r" else ARENA_F_W), (name, key, aoff[key])
            if key == "r":
                base = ARENAR[0:nparts, o:o + words]
            else:
                base = ARENAF[0:nparts, o:o + words]
                if dt == BF16:
                    base = base.bitcast(BF16)
            if len(shape) == 3:
                base = base.rearrange("p (a b) -> p a b", a=shape[1])
            elif len(shape) == 4:
                base = base.rearrange("p (a b c) -> p a b c", a=shape[1], b=shape[2])
            elif len(shape) == 5:
                base = base.rearrange("p (a b c d) -> p a b c d", a=shape[1], b=shape[2], c=shape[3])
            t_ = T(base, [P.res("%s%d" % (name, i)) for i in range(nres)])
            o_end = o + (words + 7) // 8 * 8
            inh = {}
            for (ro, rend, tok) in REGION[key]:
                if ro < o_end and rend > o:
                    for k_, v_ in tok.items():
                        if inh.get(k_, 0) < v_:
                            inh[k_] = v_
            for r_ in t_.res:
                r_.r = dict(inh)
            LIVE.append((key, o, o_end, t_))
            return t_

        def A(x):
            return x.t if isinstance(x, T) else x

        def mm(out, lhsT, rhs, start, stop, reads, wres):
            P.op("pe", lambda e: e.matmul(out, lhsT, rhs, start=start, stop=stop), reads=reads, writes=[wres])

        def tr(out, in_, n_in_part, reads, wres):
            P.op("pe", lambda e: e.transpose(out, in_, IDN.t[0:n_in_part, 0:n_in_part]), reads=list(reads) + [IDN.r()], writes=[wres])

        def act(out, in_, func, reads, writes, bias=None, scale=None, accum=None):
            kw = {}
            if bias is not None:
                kw["bias"] = bias
            if scale is not None:
                kw["scale"] = scale
            if accum is not None:
                kw["accum_out"] = accum
            P.op("act", lambda e: e.activation(out=out, in_=in_, func=func, **kw), reads=reads, writes=writes)

        def tt(out, in0, in1, op, reads, writes, eng="dve"):
            P.op(eng, lambda e: e.tensor_tensor(out=out, in0=in0, in1=in1, op=op), reads=reads, writes=writes)

        def ts(out, in0, s1, s2, op0, op1, reads, writes, eng="dve"):
            if s2 is None:
                P.op(eng, lambda e: e.tensor_scalar(out=out, in0=in0, scalar1=s1, scalar2=None, op0=op0), reads=reads, writes=writes)
            else:
                P.op(eng, lambda e: e.tensor_scalar(out=out, in0=in0, scalar1=s1, scalar2=s2, op0=op0, op1=op1), reads=reads, writes=writes)

        def stt(out, in0, scalar, in1, op0, op1, reads, writes):
            P.op("dve", lambda e: e.scalar_tensor_tensor(out=out, in0=in0, scalar=scalar, in1=in1, op0=op0, op1=op1), reads=reads, writes=writes)

        def cp(out, in_, reads, writes, eng="dve"):
            if eng == "act":
                P.op("act", lambda e: e.activation(out=out, in_=in_, func=AF.Copy), reads=reads, writes=writes)
            else:
                P.op(eng, lambda e: e.tensor_copy(out=out, in_=in_), reads=reads, writes=writes)

        def memset(ap, val, writes, eng="dve"):
            P.op(eng, lambda e: e.memset(ap, val), writes=writes)

        def fill_r(ap2d, val, writes, p0=0):
            p, n = ap2d.shape
            ts(ap2d, ONES.t[p0:p0 + p, 0:1].to_broadcast([p, n]), float(val), None, ALU.mult, None, [ONES.r()], writes)

        def prm(name, c, n=1):
            return PRM.t[:, PC[name] + c:PC[name] + c + n]

        def wload(src2d, ncols):
            s = WS[wctr[0] % NSLOT]
            wctr[0] += 1
            P.dma("pool", s.t[:, :, 0:ncols], src2d.rearrange("(kc p) n -> p kc n", p=128), writes=[s.r()], owner=s.r(), track=False)
            return s

        evq = [0]

        def ev_eng():
            evq[0] += 1
            return "act" if evq[0] % 2 else "dve"

        memset(ONES.t[:], 1.0, [ONES.r()], "dve")
        memset(WARML.t[:], 1.0, [WARML.r()], "dve")
        memset(WARMR.t[:], 1.0, [WARMR.r()], "dve")
        P.dma("sp", PRM.t[:], I["prm"], writes=[PRM.r()], owner=PRM.r())
        P.dma("sp", SMP.t[:], I["smp"], writes=[SMP.r()], owner=SMP.r())
        P.dma("pool", G2.t[:], I["g2"], writes=[G2.r()], owner=G2.r())
        P.dma("pool", PW.t[:], I["pool_w"], writes=[PW.r()], owner=PW.r())
        fill_r(W2X.t[:], 0.0, [W2X.r()])
        fill_r(A2X.t[:], 0.0, [A2X.r()])
        P.dma("pool", W2X.t[0:64, :], I["w2"], writes=[W2X.r()], owner=W2X.r())
        P.dma("pool", A2X.t[64:128, :], I["a2"], writes=[A2X.r()], owner=A2X.r())
        memset(IDN.t[:], 0.0, [IDN.r()], "pool")
        P.op("pool", lambda e: e.affine_select(out=IDN.t[:], in_=IDN.t[:], pattern=[[-1, 128]], compare_op=ALU.not_equal,
                                                fill=1.0, base=0, channel_multiplier=1), reads=[IDN.r()], writes=[IDN.r()])
        fill_r(ON1K.t[:], 1.0 / 1024.0, [ON1K.r()])
        for h in range(2):
            o = h * 256
            memset(MU2.t[:, o:o + 256], 1.0, [MU2.r()], "pool")
            P.op("pool", lambda e, o=o: e.affine_select(out=MU2.t[:, o:o + 128], in_=MU2.t[:, o:o + 128], pattern=[[1, 128]],
                                                        compare_op=ALU.is_gt, fill=0.0, base=0, channel_multiplier=-1),
                 reads=[MU2.r()], writes=[MU2.r()])
            P.op("pool", lambda e, o=o: e.affine_select(out=MU2.t[:, o + 128:o + 256], in_=MU2.t[:, o + 128:o + 256], pattern=[[1, 128]],
                                                        compare_op=ALU.is_ge, fill=0.0, base=0, channel_multiplier=-1),
                 reads=[MU2.r()], writes=[MU2.r()])
            memset(MSL.t[:, h * 128:(h + 1) * 128], 1.0, [MSL.r()], "pool")
            P.op("pool", lambda e, h=h: e.affine_select(out=MSL.t[:, h * 128:(h + 1) * 128], in_=MSL.t[:, h * 128:(h + 1) * 128],
                                                        pattern=[[-1, 128]], compare_op=ALU.is_gt, fill=0.0, base=0, channel_multiplier=1),
                 reads=[MSL.r()], writes=[MSL.r()])
        fill_r(BO1.t[:], 0.0, [BO1.r()])
        fill_r(BO1.t[0:64, 0:64], 1.0, [BO1.r()])
        fill_r(BO1.t[64:128, 64:128], 1.0, [BO1.r()], 64)
        fill_r(BO64.t[:], 0.0, [BO64.r()])
        fill_r(BO64.t[0:64, 0:64], 1.0 / 64.0, [BO64.r()])
        fill_r(BO64.t[64:128, 64:128], 1.0 / 64.0, [BO64.r()], 64)
        if STOP_AT == "c1":
            P.finish("sp"); P.emit(); return nc
        P.op("pool", lambda e: e.iota(ICNT.t[:, 0, :], pattern=[[1, 16]], base=1, channel_multiplier=0, allow_small_or_imprecise_dtypes=True),
             writes=[ICNT.r()])
        for g, win in enumerate((2, 4, 8, 16)):
            if g > 0:
                cp(ICNT.t[:, g, :], ICNT.t[:, 0, :], [ICNT.r()], [ICNT.r()])
        for g, win in reversed(list(enumerate((2, 4, 8, 16)))):
            ts(ICNT.t[:, g, :], ICNT.t[:, g, :], float(win), None, ALU.min, None, [ICNT.r()], [ICNT.r()])
        P.op("dve", lambda e: e.reciprocal(out=ICNT.t[:], in_=ICNT.t[:]), reads=[ICNT.r()], writes=[ICNT.r()])
        fill_r(SBLK.t[:].rearrange("p a b -> p (a b)"), 0.0, SBLK.res)
        if STOP_AT == "c2":
            P.finish("sp"); P.emit(); return nc
        arena_reset()
        WST = av("wst", [128, 4, 128])
        P.dma("sp", WST.t[:], I["wsT"], writes=[WST.r()], owner=WST.r())
        tt(WMT.t[:], WST.t[:], MU2.t[:, 128:256].unsqueeze(1).to_broadcast([128, 4, 128]), ALU.mult, [WST.r(), MU2.r()], [WMT.r()])
        for g in range(4):
            ts(CS16.t[:, g * 4:g * 4 + 4], prm("ln_b", g * 4, 4), SMP.t[:, g:g + 1], SMP.t[:, 4 + g:5 + g], ALU.mult, ALU.add,
               [PRM.r(), SMP.r()], [CS16.r()])
            ts(W00I.t[:, g, :], IDN.t[0:16, 0:16], SMP.t[0:16, g:g + 1], None, ALU.mult, None, [IDN.r(), SMP.r()], [W00I.r()])
        if STOP_AT in ("c3", "c3x1", "c3x2", "c3x3"):
            P.finish("sp"); P.emit(); return nc
        def rmsnorm(Tn, gname, out_fn):
            bk, rb = nb()
            for c in range(8):
                s = SQ[sqc[0] % 2]
                sqc[0] += 1
                if c % 2 == 0:
                    act(s.t[:, 0:Tn], CUR["XT"].t[:, c, 0:Tn], AF.Square, [CUR["XT"].r(c)], [s.r()])
                else:
                    tt(s.t[:, 0:Tn], CUR["XT"].t[:, c, 0:Tn], CUR["XT"].t[:, c, 0:Tn], ALU.mult, [CUR["XT"].r(c)], [s.r()])
                mm(bk[:, 0:Tn], ON1K.t[:], s.t[:, 0:Tn], c == 0, c == 7, [ON1K.r(), s.r()], rb)
            act(RSTD.t[:, 0:Tn], bk[:, 0:Tn], AF.Ln, [rb], [RSTD.r()], bias=RMS_EPS)
            act(RSTD.t[:, 0:Tn], RSTD.t[:, 0:Tn], AF.Exp, [RSTD.r()], [RSTD.r()], scale=-0.5)
            for c in range(8):
                out_fn(c, CUR["XT"].t[:, c, 0:Tn], prm(gname, c), RSTD.t[:, 0:Tn])

        def norm_to_h(Tn, gname):
            def f(c, x, g, rs):
                stt(CUR["HT"].t[:, c, 0:Tn], x, g, rs, ALU.mult, ALU.mult, [CUR["XT"].r(c), PRM.r(), RSTD.r()], [CUR["HT"].r(c)])
            rmsnorm(Tn, gname, f)

        def proj_fm_multi(wsrc, kin, nout, targets):
            for c0 in range(0, nout, 512):
                ncols = min(512, nout - c0)
                s_ = wload(wsrc[:, c0:c0 + ncols], ncols)
                for mq in range(ncols // 128):
                    for (Tn, rhs_fn, evac_fn) in targets:
                        bk, rb = nb()
                        for kc in range(kin):
                            rap, rr = rhs_fn(kc)
                            mm(bk[:, 0:Tn], s_.t[:, kc, mq * 128:(mq + 1) * 128], rap, kc == 0, kc == kin - 1, [s_.r(), rr], rb)
                        evac_fn(c0 // 128 + mq, bk[:, 0:Tn], rb)

        def proj_fm(Tn, wsrc, kin, nout, rhs_fn, evac_fn):
            proj_fm_multi(wsrc, kin, nout, [(Tn, rhs_fn, evac_fn)])

        def proj_down_multi(wsrc, kchunks, targets):
            nkq = kchunks // 8
            for cg in range(2):
                banks = [[nb() for _ in range(4)] for _t in targets]
                for kq in range(nkq):
                    s_ = wload(wsrc[kq * 1024:(kq + 1) * 1024, cg * 512:(cg + 1) * 512], 512)
                    for mq in range(4):
                        for ti, (Tn, rhs_fn, xt) in enumerate(targets):
                            bk, rb = banks[ti][mq]
                            for kc in range(8):
                                rap, rr = rhs_fn(kq * 8 + kc)
                                mm(bk[:, 0:Tn], s_.t[:, kc, mq * 128:(mq + 1) * 128], rap, kq == 0 and kc == 0,
                                   kq == nkq - 1 and kc == 7, [s_.r(), rr], rb)
                for ti, (Tn, rhs_fn, xt) in enumerate(targets):
                    for mq in range(4):
                        m = cg * 4 + mq
                        bk, rb = banks[ti][mq]
                        tt(xt.t[:, m, 0:Tn], xt.t[:, m, 0:Tn], bk[:, 0:Tn], ALU.add, [xt.r(m), rb], [xt.r(m)])

        def proj_down(Tn, wsrc, kchunks, rhs_fn):
            proj_down_multi(wsrc, kchunks, [(Tn, rhs_fn, CUR["XT"])])

        def hrhs(Tn):
            return lambda kc: (CUR["HT"].t[:, kc, 0:Tn], CUR["HT"].r(kc))

        P.barrier(pool=True)
        arena_reset()
        MTOK = av("mtok", [128, 2, 1024])
        MT = av("memT", [128, 8, 256], F32, 8)
        MN = [av("mn%d" % l, [128, 8, 256], BF16, 8) for l in range(2)]
        KVS = [av("kvs%d" % i, [128, 2, 1024]) for i in range(2)]
        P.dma("sp", MTOK.t[:], I["memp"].rearrange("(a p) f -> p a f", p=128), writes=[MTOK.r()], owner=MTOK.r())
        for c in range(8):
            bk, rb = nb()
            for a in range(2):
                tr(bk[:, a * 128:(a + 1) * 128], MTOK.t[:, a, c * 128:(c + 1) * 128], 128, [MTOK.r()], rb)
            cp(MT.t[:, c, :], bk[:, 0:256], [rb], [MT.r(c)], ev_eng())
        bk, rb = nb()
        for c in range(8):
            s = SQ[sqc[0] % 2]
            sqc[0] += 1
            act(s.t[:, 0:256], MT.t[:, c, :], AF.Square, [MT.r(c)], [s.r()])
            mm(bk[:, 0:256], ON1K.t[:], s.t[:, 0:256], c == 0, c == 7, [ON1K.r(), s.r()], rb)
        act(RSTD.t[:, 0:256], bk[:, 0:256], AF.Ln, [rb], [RSTD.r()], bias=RMS_EPS)
        act(RSTD.t[:, 0:256], RSTD.t[:, 0:256], AF.Exp, [RSTD.r()], [RSTD.r()], scale=-0.5)
        for l in range(2):
            for c in range(8):
                stt(MN[l].t[:, c, :], MT.t[:, c, :], prm("g_mem%d" % l, c), RSTD.t[:, 0:256], ALU.mult, ALU.mult,
                    [MT.r(c), PRM.r(), RSTD.r()], [MN[l].r(c)])
        if STOP_AT == "mk1":
            P.finish("sp"); P.emit(); return nc
        kvi = 0
        for l in range(2):
            for which, wname, oname in (("k", "w_xk", "mk"), ("v", "w_xv", "mv")):
                for cg in range(2):
                    s = wload(I[wname][l][:, cg * 512:(cg + 1) * 512], 512)
                    if which == "k":
                        for mq in range(4):
                            bk, rb = nb()
                            for kc in range(8):
                                mm(bk[:, 0:256], s.t[:, kc, mq * 128:(mq + 1) * 128], MN[l].t[:, kc, :], kc == 0, kc == 7,
                                   [s.r(), MN[l].r(kc)], rb)
                            cp(KT[l].t[:, cg * 4 + mq, :], bk[:, 0:256], [rb], [KT[l].r()], ev_eng())
                    for a in range(2):
                        bk, rb = nb()
                        for kc in range(8):
                            mm(bk[:, :], MN[l].t[:, kc, a * 128:(a + 1) * 128], s.t[:, kc, :], kc == 0, kc == 7, [s.r(), MN[l].r(kc)], rb)
                        stg = KVS[kvi % 2]
                        cp(stg.t[:, a, cg * 512:(cg + 1) * 512], bk[:, :], [rb], [stg.r()], "act")
                        if which == "v":
                            cp(VM[l].t[:, a, cg * 512:(cg + 1) * 512], bk[:, :], [rb], [VM[l].r()], "dve")
                    if STOP_AT == "mk2" or (STOP_AT == "mk3c" and which == "v" and cg == 0) or (STOP_AT == "mk3d" and which == "v" and cg == 1):
                        P.finish("sp"); P.emit(); return nc
                stg = KVS[kvi % 2]
                kvi += 1
                P.dma("sp", O[oname][l].rearrange("(a p) f -> p a f", p=128), stg.t[:], reads=[stg.r()], owner=stg.r(), final=True)
                if STOP_AT == "mk3":
                    P.finish("sp"); P.emit(); return nc
        if STOP_AT == "mk4":
            P.finish("sp"); P.emit(); return nc
        P.barrier()
        try:
            stage("setup")
        except StopBuild:
            P.finish("sp")
            P.emit()
            return nc

        def dims(sample):
            return (NS, 1, NS) if sample else (512, 4, 128)

        def load_x(blk, sample, pool):
            Tn, nch, CH = dims(sample)
            set_stream(sample)
            arena_reset()
            XTOK = av("xtok", [128, nch, 1024])
            if sample:
                P.dma("sp", XTOK.t[0:NS, 0, :], I["xs"], writes=[XTOK.r()], owner=XTOK.r())
            else:
                P.dma("sp", XTOK.t[:], I["xp"][blk * 512:(blk + 1) * 512, :].rearrange("(a p) f -> p a f", p=128),
                      writes=[XTOK.r()], owner=XTOK.r())
            for c in range(8):
                bk, rb = nb()
                for a in range(nch):
                    tr(bk[:, a * CH:(a + 1) * CH], XTOK.t[0:CH, a, c * 128:(c + 1) * 128], CH, [XTOK.r()], rb)
                cp(CUR["XT"].t[:, c, 0:Tn], bk[:, 0:Tn], [rb], [CUR["XT"].r(c)], ev_eng())
            P.barrier(pool=pool)

        def phase_mixer(blk, sample, layer, pool):
            Tn, nch, CH = dims(sample)
            set_stream(sample)
            arena_reset()
            norm_to_h(Tn, "g_mix%d" % layer)
            if layer == 0:
                mixer_ab(blk, sample, Tn, nch, CH)
            else:
                mixer_c(blk, sample, Tn, nch, CH)
            P.barrier(pool=pool)

        def phase_attn(blk, sample, layer, pool):
            Tn, nch, CH = dims(sample)
            set_stream(sample)
            arena_reset()
            norm_to_h(Tn, "g_xa%d" % layer)
            if sample:
                xattn_sample(layer)
            else:
                xattn_prompt(layer)
            P.barrier(pool=pool)

        def phase_mlp(streams, layer, pool):
            arena_reset()
            up_t, dn_t = [], []
            for sample in streams:
                Tn, nch, CH = dims(sample)
                set_stream(sample)
                norm_to_h(Tn, "g_ffn%d" % layer)
                tagn = "s" if sample else "p"
                HID = av("hid" + tagn, [128, 32, Tn], BF16, 32)
                TMP = [av("mlptmp%s%d" % (tagn, i_), [128, Tn]) for i_ in range(2)]

                def ev_up(m, bk, rb, HID=HID, TMP=TMP, tctr=[0]):
                    t = TMP[tctr[0] % 2]
                    tctr[0] += 1
                    act(t.t[:], bk, AF.Square, [rb], [t.r()])
                    stt(HID.t[:, m, :], bk, 0.0, t.t[:], ALU.is_gt, ALU.mult, [rb, t.r()], [HID.r(m)])
                ht = CUR["HT"]
                up_t.append((Tn, (lambda kc, ht=ht, Tn=Tn: (ht.t[:, kc, 0:Tn], ht.r(kc))), ev_up))
                dn_t.append((Tn, (lambda k, HID=HID: (HID.t[:, k, :], HID.r(k))), CUR["XT"]))
            proj_fm_multi(I["w_ff_up"][layer], 8, 4096, up_t)
            proj_down_multi(I["w_ff_down"][layer], 32, dn_t)
            P.barrier(pool=pool)

        def final_out(blk, sample, pool):
            Tn, nch, CH = dims(sample)
            set_stream(sample)
            arena_reset()
            YF = av("yf", [128, 8, Tn], F32, 8)
            YTOK = av("ytok", [128, nch, 1024])
            xt = CUR["XT"]

            def fo(c, x, g, rs):
                stt(YF.t[:, c, :], x, g, rs, ALU.mult, ALU.mult, [xt.r(c), PRM.r(), RSTD.r()], [YF.r(c)])
            rmsnorm(Tn, "g_fin", fo)
            for a in range(nch):
                for cg in range(2):
                    bk, rb = nb()
                    for q in range(4):
                        c = cg * 4 + q
                        tr(bk[0:CH, q * 128:(q + 1) * 128], YF.t[:, c, a * CH:(a + 1) * CH], 128, [YF.r(c)], rb)
                    cp(YTOK.t[0:CH, a, cg * 512:(cg + 1) * 512], bk[0:CH, :], [rb], [YTOK.r()], ev_eng())
            if sample:
                P.dma("sp", O["ys"], YTOK.t[0:NS, 0, :], reads=[YTOK.r()], owner=YTOK.r(), final=True)
            else:
                P.dma("sp", O["yp"][blk * 512:(blk + 1) * 512, :].rearrange("(a p) f -> p a f", p=128), YTOK.t[:],
                      reads=[YTOK.r()], owner=YTOK.r(), final=True)
            P.barrier(pool=pool)

        def run_block(blk, sample):
            load_x(blk, sample, True)
            for layer in range(2):
                phase_mixer(blk, sample, layer, True)
                phase_attn(blk, sample, layer, True)
                phase_mlp([sample], layer, True)
            final_out(blk, sample, True)

        def run_joint():
            load_x(3, False, True)
            load_x(0, True, True)
            for layer in range(2):
                phase_mixer(3, False, layer, True)
                phase_attn(3, False, layer, True)
                phase_mixer(0, True, layer, True)
                phase_attn(0, True, layer, True)
                phase_mlp([False, True], layer, True)
            final_out(3, False, True)
            final_out(0, True, True)

        HOLD = {}

        def mixer_ab(blk, sample, Tn, nch, CH):
            NBANK[0] = 8 if sample else 7
            mixer_ab_(blk, sample, Tn, nch, CH)
            NBANK[0] = 8

        def mixer_ab_(blk, sample, Tn, nch, CH):
            H0 = 16
            ZB = av("zb", [128, 18, H0 + Tn], F32, 18)
            if not sample:
                HIS = ZHIS
                if "hist" not in HOLD:
                    HOLD["hist"] = True
                    memset(HIS.t[:], 0.0, [HIS.r()])
                cp(ZB.t[:, :, 0:H0], HIS.t[:], [HIS.r()], ZB.res, "dve")
            YC = av("ycat", [128, 8, Tn], BF16, 8)

            def ev_z(m, bk, rb):
                cp(ZB.t[:, m, H0:H0 + Tn], bk, [rb], [ZB.r(m)], ev_eng())
            proj_fm(Tn, I["w_in_ab"], 8, 2304, hrhs(Tn), ev_z)

            if sample:
                Q6 = av("q6", [128, 6, 512])
                QR = av("qr", [128, 6, 64])
                SAV = av("sav", [128, 64]); YV = av("yv", [128, 64]); YTS = av("yts", [128, 512])
                ZPS = av("zps", [128, 14, NS])
            PA = av("pa", [128, H0 + Tn])
            PB = av("pb", [128, H0 + Tn])
            P16 = av("p16", [128, 16])
            LOR = av("lor", [128, Tn], F32R)
            SG = av("sg", [128, Tn], F32R)
            TMPL = T(PA.t[:, 0:Tn], PA.res)
            ZSR = av("zsr", [128, Tn]); ZSK = av("zsk", [128, Tn]); ZSV = av("zsv", [128, Tn])
            SIG = av("sig", [128, Tn]); CSUM = av("csum", [128, Tn]); E1 = T(PA.t[:, 0:Tn], PA.res)
            AA = av("aa", [128, Tn]); KKN = av("kkn", [128, Tn]); XX = av("xx", [128, Tn], F32R)
            DP = XX
            XF = T(PB.t[:, 0:Tn], PB.res)
            YTP = av("ytp", [128, Tn], F32R)
            NLC = av("nlc", [128, 4]); DCT = av("dct", [128, 4])
            if not sample:
                EB = av("eb", [128, Tn]); ED = av("ed", [128, Tn])
                AR = av("ar", [128, 4, 2, 128], F32R)
                BT = av("bt", [128, 512], F32R); KTt = av("ktt", [128, 512], F32R)
                SETS = []
                for k_ in range(2):
                    S_ = dict(
                        TOK=av("tok%d" % k_, [128, 4, 128], F32R), M1=av("m1%d" % k_, [128, 2, 128], F32R), M3=av("m3%d" % k_, [128, 512], F32R),
                        ZQ=[av("zq%d%d" % (k_, q_), [128, 2, 2, 128], BF16) for q_ in range(2)],
                        PP=[av("pp%d%d" % (k_, q_), [128, 2, 128], BF16) for q_ in range(2)],
                        ARX=av("arx%d" % k_, [128, 4, 128], F32R), BTX=av("btx%d" % k_, [128, 2, 128], F32R),
                        APT=av("apt%d" % k_, [128, 128], F32R), UT=av("ut%d" % k_, [128, 128], F32R),
                        APC=av("apc%d" % k_, [128, 128]), YTK=av("ytk%d" % k_, [128, 128]),
                        BH=av("bh%d" % k_, [128, 128]), KH=av("kh%d" % k_, [128, 128]))
                    fill_r(S_["ARX"].t[:].rearrange("p a b -> p (a b)"), 0.0, [S_["ARX"].r()])
                    fill_r(S_["BTX"].t[:].rearrange("p a b -> p (a b)"), 0.0, [S_["BTX"].r()])
                    SETS.append(S_)
            else:
                pass

            AMARK = amark()
            if sample:
                SPT = av("spt", [128, 2, 512])
                PH = av("ph", [128, 4, 240])
                spf = I["spool"].rearrange("b p f -> (b p) f")
                P.dma("sp", SPT.t[:, 0, :], spf[0:128, :], writes=[SPT.r()], owner=SPT.r())
                P.dma("sp", SPT.t[0:112, 1, :], spf[128:240, :], writes=[SPT.r()], owner=SPT.r())
                for g in range(4):
                    bk, rb = nb()
                    tr(bk[:, 0:128], SPT.t[:, 0, g * 128:(g + 1) * 128], 128, [SPT.r()], rb)
                    tr(bk[:, 128:240], SPT.t[0:112, 1, g * 128:(g + 1) * 128], 112, [SPT.r()], rb)
                    cp(PH.t[:, g, :], bk[:, 0:240], [rb], [PH.r()], ev_eng())
                PCP = P.res("pcopy")
                P.dma("sp", O["pools"][:, 0:14, :], I["spool"][:, 1:15, :], owner=PCP, final=True)
            for g, win in enumerate((2, 4, 8, 16)):
                zg = ZB.t[:, g, :]
                zn = ZB.t[:, g, H0:H0 + Tn]
                if sample:
                    P.op("dve", lambda e, g=g, win=win: e.tensor_reduce(
                        out=PA.t[:, 0:NS], in_=PH.t[:, g, :].rearrange("p (b q) -> p b q", q=15)[:, :, 16 - win:15], axis=AX.X, op=ALU.add),
                        reads=[PH.r()], writes=[PA.r()])
                    tt(PA.t[:, 0:NS], PA.t[:, 0:NS], zn, ALU.add, [PA.r(), ZB.r(g)], [PA.r()])
                    S_ap = PA.t[:, 0:NS]
                    S_r = PA.r()
                else:
                    lo = {2: 16, 4: 14, 8: 10, 16: 2}[win]
                    tt(PA.t[:, lo:H0 + Tn], zg[:, lo:H0 + Tn], zg[:, lo - 1:H0 + Tn - 1], ALU.add, [ZB.r(g)], [PA.r()])
                    cur, oth = PA, PB
                    sh = 2
                    lo2 = lo
                    while sh < win:
                        lo2 = lo2 + sh
                        tt(oth.t[:, lo2:H0 + Tn], cur.t[:, lo2:H0 + Tn], cur.t[:, lo2 - sh:H0 + Tn - sh], ALU.add, [cur.r()], [oth.r()])
                        cur, oth = oth, cur
                        sh *= 2
                    S_ap = cur.t[:, H0:H0 + Tn]
                    S_r = cur.r()
                stt(DP.t[:], S_ap, 1.0 / win, zn, ALU.mult, ALU.subtract, [S_r, ZB.r(g)], [DP.r()])
                if (not sample) and blk == 0:
                    tt(P16.t[:], S_ap[:, 0:16], ICNT.t[:, g, :], ALU.mult, [S_r, ICNT.r()], [P16.r()])
                    tt(DP.t[:, 0:16], P16.t[:], zn[:, 0:16], ALU.subtract, [P16.r(), ZB.r(g)], [DP.r()])
                bk, rb = nb()
                mm(bk[:, 0:Tn], PW.t[:, g, :], DP.t[:], True, True, [PW.r(), DP.r()], rb)
                ts(YC.t[:, g, :], bk[:, 0:Tn], prm("pscale", g), None, ALU.mult, None, [rb, PRM.r()], [YC.r(g)])

            if sample:
                SHT = av("sht", [128, 1792])
                P.dma("sp", SHT.t[0:NS, :], I["sshift"], writes=[SHT.r()], owner=SHT.r())
                for q in range(0, 14, 7):
                    bk, rb = nb()
                    for j in range(q, q + 7):
                        tr(bk[:, (j - q) * NS:(j - q + 1) * NS], SHT.t[0:NS, j * 128:(j + 1) * 128], NS, [SHT.r()], rb)
                    cp(ZPS.t[:, q:q + 7, :], bk[:, 0:7 * NS].rearrange("p (a b) -> p a b", a=7), [rb], [ZPS.r()], ev_eng())

            def zshift(j, out, out_r):
                m = 4 + j
                zc = ZB.t[:, m, H0:H0 + Tn]
                zp = ZPS.t[:, j, :] if sample else ZB.t[:, m, H0 - 1:H0 + Tn - 1]
                rd = [ZB.r(m)] + ([ZPS.r()] if sample else [])
                tt(out, zp, zc, ALU.subtract, rd, [out_r])
                stt(out, out, prm("mu", j), zc, ALU.mult, ALU.add, [out_r, PRM.r(), ZB.r(m)], [out_r])

            zshift(12, TMPL.t[:], TMPL.r())
            act(LOR.t[0:64, :], TMPL.t[0:64, :], AF.Tanh, [TMPL.r()], [LOR.r()])
            cp(LOR.t[64:128, :], TMPL.t[64:128, :], [TMPL.r()], [LOR.r()], "dve")
            zshift(13, TMPL.t[:], TMPL.r())
            act(SG.t[:], TMPL.t[:], AF.Sigmoid, [TMPL.r()], [SG.r()])

            def prep_pair(i, bonus_only):
                zshift(i, ZSR.t[:], ZSR.r())
                zshift(4 + i, ZSK.t[:], ZSK.r())
                zshift(8 + i, ZSV.t[:], ZSV.r())
                bk, rb = nb()
                mm(bk[:, 0:Tn], A2X.t[:, i * 128:(i + 1) * 128], LOR.t[:], True, True, [A2X.r(), LOR.r()], rb)
                act(AA.t[:], bk[:, 0:Tn], AF.Sigmoid, [rb, PRM.r()], [AA.r()], bias=prm("a0", i))
                if not sample:
                    warm(WARM_PREP)
                if not bonus_only:
                    bk, rb = nb()
                    mm(bk[:, 0:Tn], W2X.t[:, i * 128:(i + 1) * 128], LOR.t[:], True, True, [W2X.r(), LOR.r()], rb)
                    if not sample:
                        warm(WARM_PREP)
                    act(SIG.t[:], bk[:, 0:Tn], AF.Sigmoid, [rb, PRM.r()], [SIG.r()], bias=prm("w0", i))
                    ts(KKN.t[:], ZSK.t[:], prm("k_k", i), None, ALU.mult, None, [ZSK.r(), PRM.r()], [KKN.r()])
                    act(XX.t[:], KKN.t[:], AF.Square, [KKN.r()], [XX.r()])
                    bk, rb = nb()
                    mm(bk[:, 0:Tn], BO1.t[:], XX.t[:], True, True, [BO1.r(), XX.r()], rb)
                    if not sample:
                        warm(WARM_PREP)
                    ts(XF.t[:], bk[:, 0:Tn], 1e-24, None, ALU.max, None, [rb], [XF.r()])
                    act(XF.t[:], XF.t[:], AF.Ln, [XF.r()], [XF.r()])
                    act(XF.t[:], XF.t[:], AF.Exp, [XF.r()], [XF.r()], scale=-0.5)
                    tt(KKN.t[:], KKN.t[:], XF.t[:], ALU.mult, [KKN.r(), XF.r()], [KKN.r()])
                ts(XF.t[:], AA.t[:], 1.0, prm("k_a", i), ALU.subtract, ALU.mult, [AA.r(), PRM.r()], [XF.r()])
                stt(ZSK.t[:], XF.t[:], 1.0, ZSK.t[:], ALU.add, ALU.mult, [XF.r(), ZSK.r()], [ZSK.r()])
                if not bonus_only:
                    tt(AA.t[:], KKN.t[:], AA.t[:], ALU.mult, [KKN.r(), AA.r()], [AA.r()])

            def gn_pair(i):
                bk, rb = nb()
                mm(bk[:, 0:Tn], BO64.t[:], YTP.t[:], True, True, [BO64.r(), YTP.r()], rb)
                tt(SIG.t[:], YTP.t[:], bk[:, 0:Tn], ALU.subtract, [YTP.r(), rb], [SIG.r()])
                act(XX.t[:], SIG.t[:], AF.Square, [SIG.r()], [XX.r()])
                bk, rb = nb()
                mm(bk[:, 0:Tn], BO64.t[:], XX.t[:], True, True, [BO64.r(), XX.r()], rb)
                act(CSUM.t[:], bk[:, 0:Tn], AF.Ln, [rb], [CSUM.r()], bias=GN_EPS)
                act(CSUM.t[:], CSUM.t[:], AF.Exp, [CSUM.r()], [CSUM.r()], scale=-0.5)
                tt(SIG.t[:], SIG.t[:], CSUM.t[:], ALU.mult, [SIG.r(), CSUM.r()], [SIG.r()])
                ts(SIG.t[:], SIG.t[:], prm("gn_g", i), prm("gn_b", i), ALU.mult, ALU.add, [SIG.r(), PRM.r()], [SIG.r()])
                stt(XX.t[:], ZSR.t[:], prm("r_k", i), ZSK.t[:], ALU.mult, ALU.mult, [ZSR.r(), PRM.r(), ZSK.r()], [XX.r()])
                bk, rb = nb()
                mm(bk[:, 0:Tn], BO1.t[:], XX.t[:], True, True, [BO1.r(), XX.r()], rb)
                tt(E1.t[:], bk[:, 0:Tn], ZSV.t[:], ALU.mult, [rb, ZSV.r()], [E1.r()])
                tt(SIG.t[:], SIG.t[:], E1.t[:], ALU.add, [SIG.r(), E1.r()], [SIG.r()])
                bk, rb = nb()
                mm(bk[:, 0:Tn], G2.t[:, i * 128:(i + 1) * 128], SG.t[:], True, True, [G2.r(), SG.r()], rb)
                tt(YC.t[:, 4 + i, :], SIG.t[:], bk[:, 0:Tn], ALU.mult, [SIG.r(), rb], [YC.r(4 + i)])

            c3 = lambda ap: ap.rearrange("p (c t) -> p c t", c=4)
            for i in range(4):
                prep_pair(i, False)
                if not sample:
                    for c in range(4):
                        P.op("dve", lambda e, c=c: e.tensor_tensor_scan(
                            out=CSUM.t[:, c * 128:(c + 1) * 128], data0=ONES.t[:, 0:128], data1=SIG.t[:, c * 128:(c + 1) * 128],
                            initial=0.0, op0=ALU.mult, op1=ALU.add), reads=[ONES.r(), SIG.r()], writes=[CSUM.r()])
                    ts(NLC.t[:], c3(CSUM.t[:])[:, :, 127], -LAM, None, ALU.mult, None, [CSUM.r()], [NLC.r()])
                    act(DCT.t[:], NLC.t[:], AF.Exp, [NLC.r()], [DCT.r()])
                    act(E1.t[:], CSUM.t[:], AF.Exp, [CSUM.r()], [E1.r()], scale=-LAM)
                    tt(AR.t[:, :, 1, :], c3(ZSR.t[:]), c3(E1.t[:]), ALU.mult, [ZSR.r(), E1.r()], [AR.r()])
                    act(EB.t[:], CSUM.t[:], AF.Exp, [CSUM.r()], [EB.r()], scale=LAM)
                    tt(BT.t[:], AA.t[:], EB.t[:], ALU.mult, [AA.r(), EB.r()], [BT.r()])
                    tt(KTt.t[:], ZSK.t[:], EB.t[:], ALU.mult, [ZSK.r(), EB.r()], [KTt.r()])
                    tt(SIG.t[:], CSUM.t[:], SIG.t[:], ALU.subtract, [CSUM.r(), SIG.r()], [SIG.r()])
                    act(E1.t[:], SIG.t[:], AF.Exp, [SIG.r()], [E1.r()], scale=-LAM)
                    stt(AR.t[:, :, 0, :], c3(KKN.t[:]), -1.0, c3(E1.t[:]), ALU.mult, ALU.mult, [KKN.r(), E1.r()], [AR.r()])
                    for c in range(4):
                        act(ED.t[:, c * 128:(c + 1) * 128], CSUM.t[:, c * 128:(c + 1) * 128], AF.Exp, [CSUM.r(), NLC.r()], [ED.r()],
                            bias=NLC.t[:, c:c + 1], scale=LAM)

                    def front_a(c, S):
                        csl = slice(c * 128, (c + 1) * 128)
                        ARX, BTX = S["ARX"], S["BTX"]
                        tt(S["BH"].t[:], AA.t[:, csl], ED.t[:, csl], ALU.mult, [AA.r(), ED.r()], [S["BH"].r()])
                        tt(S["KH"].t[:], ZSK.t[:, csl], ED.t[:, csl], ALU.mult, [ZSK.r(), ED.r()], [S["KH"].r()])
                        cp(ARX.t[0:64, 0:2, :], AR.t[0:64, c, :, :], [AR.r()], [ARX.r()], "pool")
                        cp(ARX.t[64:128, 2:4, :], AR.t[64:128, c, :, :], [AR.r()], [ARX.r()], "pool")
                        cp(BTX.t[0:64, 0, :], BT.t[0:64, csl], [BT.r()], [BTX.r()], "pool")
                        cp(BTX.t[64:128, 1, :], BT.t[64:128, csl], [BT.r()], [BTX.r()], "pool")

                    def front_b(c, S):
                        csl = slice(c * 128, (c + 1) * 128)
                        TOK, M1, M3, ZQ, ARX, BTX = S["TOK"], S["M1"], S["M3"], S["ZQ"], S["ARX"], S["BTX"]
                        bk, rb = nb()
                        tr(bk[:, 0:128], AR.t[:, c, 0, :].bitcast(F32), 128, [AR.r()], rb)
                        tr(bk[:, 128:256], ZSV.t[:, csl], 128, [ZSV.r()], rb)
                        tr(bk[:, 256:384], S["BH"].t[:], 128, [S["BH"].r()], rb)
                        tr(bk[:, 384:512], S["KH"].t[:], 128, [S["KH"].r()], rb)
                        cp(TOK.t[:].rearrange("p a b -> p (a b)"), bk[:, :], [rb], [TOK.r()], "act")
                        arx_c = ARX.t[:, :, :].rearrange("p a b -> p (a b)")
                        mu4 = MU2.t[:].rearrange("p (j q t) -> p j q t", j=2, q=2)
                        bk1, rb1 = nb()
                        mm(bk1[:, :], BT.t[:, csl], arx_c, True, True, [BT.r(), ARX.r()], rb1)
                        b14 = bk1[:, :].rearrange("p (j q t) -> p j q t", j=2, q=2)
                        P0 = S["PP"][1]
                        tt(P0.t[:], b14[:, :, 0, :], mu4[:, :, 0, :], ALU.mult, [rb1, MU2.r()], [P0.r()])
                        tt(M1.t[:], b14[:, :, 1, :], mu4[:, :, 1, :], ALU.mult, [rb1, MU2.r()], [M1.r()])
                        bk3, rb3 = nb()
                        mm(bk3[:, :], KTt.t[:, csl], arx_c, True, True, [KTt.r(), ARX.r()], rb3)
                        tt(M3.t[:], bk3[:, :], MU2.t[:], ALU.mult, [rb3, MU2.r()], [M3.r()])
                        bk2, rb2 = nb()
                        mm(bk2[:, 0:256], AR.t[:, c, 0, :], BTX.t[:, :, :].rearrange("p a b -> p (a b)"), True, True, [AR.r(), BTX.r()], rb2)
                        z0 = ZQ[0]
                        tt(z0.t[:, :, 1, :], bk2[:, 0:256].rearrange("p (a b) -> p a b", a=2), MSL.t[:].rearrange("p (a b) -> p a b", a=2),
                           ALU.mult, [rb2, MSL.r()], [z0.r()])
                        S["m3v"] = M3.t[:].rearrange("p (j q t) -> p j q t", j=2, q=2)
                        bkw, rbw = nb()
                        for j in range(2):
                            mm(bkw[:, j * 64:(j + 1) * 64], S["m3v"][:, j, 0, :], TOK.t[:, 1, j * 64:(j + 1) * 64], True, True, [M3.r(), TOK.r()], rbw)
                        cp(z0.t[:, :, 0, 0:64], TOK.t[:, 0, :].rearrange("p (j k) -> p j k", j=2), [TOK.r()], [z0.r()], "pool")
                        cp(z0.t[:, :, 0, 64:128], bkw[:, 0:128].rearrange("p (j k) -> p j k", j=2), [rbw], [z0.r()], "act")
                        S["zc"], S["zn"], S["pcur"] = ZQ[0], ZQ[1], P0

                    def neumann_mm(lv, S):
                        zc_ = S["zc"]
                        bkz, rbz = nb()
                        bkp, rbp = (None, None)
                        if lv < 6:
                            bkp, rbp = nb()
                        nz = 256 if lv < 6 else 128
                        for j in range(2):
                            lhs = S["pcur"].t[:, j, :]
                            lr = S["pcur"].r()
                            mm(bkz[:, j * 256:j * 256 + nz], lhs, zc_.t[:, j, :, :].rearrange("p a b -> p (a b)")[:, 0:nz], True, True,
                               [lr, zc_.r()], rbz)
                            if lv < 6:
                                mm(bkp[:, j * 128:(j + 1) * 128], zc_.t[:, j, 1, :], lhs, True, True, [lr, zc_.r()], rbp)
                        S["banks"] = (bkz, rbz, bkp, rbp)

                    def neumann_ev(lv, S, par):
                        zc_, zn_ = S["zc"], S["zn"]
                        bkz, rbz, bkp, rbp = S["banks"]
                        bz4 = bkz[:, :].rearrange("p (j q t) -> p j q t", j=2, q=2)
                        tt(zn_.t[:, :, 0, :], zc_.t[:, :, 0, :], bz4[:, :, 0, :], ALU.add, [zc_.r(), rbz], [zn_.r()])
                        if lv < 6:
                            cp(zn_.t[:, :, 1, :], bz4[:, :, 1, :], [rbz], [zn_.r()], "act")
                            pn = S["PP"][lv % 2]
                            cp(pn.t[:], bkp[:, 0:256].rearrange("p (j t) -> p j t", j=2), [rbp], [pn.r()], "act" if (lv + par) % 2 else "dve")
                            S["pcur"] = pn
                        S["zc"], S["zn"] = zn_, zc_

                    def chunk_back(c, S):
                        csl = slice(c * 128, (c + 1) * 128)
                        zf = S["zc"]
                        TOK, APC, APT, UT, YTK = S["TOK"], S["APC"], S["APT"], S["UT"], S["YTK"]
                        m3v = S["m3v"]
                        cp(APC.t[:].rearrange("p (j k) -> p j k", j=2), zf.t[:, :, 0, 0:64], [zf.r()], [APC.r()], "pool")
                        bka, rba = nb()
                        tr(bka[:, 0:128], APC.t[:], 128, [APC.r()], rba)
                        cp(APT.t[:], bka[:, 0:128], [rba], [APT.r()], "act")
                        bku, rbu = nb()
                        mm(bku[:, 0:128], APT.t[:], SBLK.t[:, i, :], True, True, [APT.r(), SBLK.r(i)], rbu)
                        tt(UT.t[:].rearrange("p (j k) -> p j k", j=2), bku[:, 0:128].rearrange("p (j k) -> p j k", j=2),
                           zf.t[:, :, 0, 64:128], ALU.add, [rbu, zf.r()], [UT.r()])
                        bky, rby = nb()
                        mm(bky[:, 0:128], AR.t[:, c, 1, :], SBLK.t[:, i, :], True, False, [AR.r(), SBLK.r(i)], rby)
                        for j in range(2):
                            mm(bky[:, j * 64:(j + 1) * 64], S["M1"].t[:, j, :], UT.t[:, j * 64:(j + 1) * 64], False, False, [S["M1"].r(), UT.r()], rby)
                            mm(bky[:, j * 64:(j + 1) * 64], m3v[:, j, 1, :], TOK.t[:, 1, j * 64:(j + 1) * 64], False, j == 1, [S["M3"].r(), TOK.r()], rby)
                        bks, rbs = nb()
                        mm(bks[:, 0:128], TOK.t[:, 2, :], UT.t[:], True, False, [TOK.r(), UT.r()], rbs)
                        mm(bks[:, 0:128], TOK.t[:, 3, :], TOK.t[:, 1, :], False, True, [TOK.r()], rbs)
                        cp(YTK.t[:], bky[:, 0:128], [rby], [YTK.r()], "act")
                        for j in range(2):
                            js = slice(j * 64, (j + 1) * 64)
                            stt(SBLK.t[js, i, js], SBLK.t[js, i, js], DCT.t[js, c:c + 1], bks[js, js], ALU.mult, ALU.add,
                                [SBLK.r(i), DCT.r(), rbs], [SBLK.r(i)])
                        bkt, rbt = nb()
                        tr(bkt[:, 0:128], YTK.t[:], 128, [YTK.r()], rbt)
                        cp(YTP.t[:, csl], bkt[:, 0:128], [rbt], [YTP.r()], "dve")

                    front_a(0, SETS[0])
                    front_a(1, SETS[1])
                    for c0 in (0, 2):
                        front_b(c0, SETS[0])
                        front_b(c0 + 1, SETS[1])
                        if c0 == 0:
                            front_a(2, SETS[0])
                            front_a(3, SETS[1])
                        for lv in range(7):
                            neumann_mm(lv, SETS[0])
                            neumann_mm(lv, SETS[1])
                            warm(WARM_LV)
                            neumann_ev(lv, SETS[0], 0)
                            neumann_ev(lv, SETS[1], 1)
                        chunk_back(c0, SETS[0])
                        chunk_back(c0 + 1, SETS[1])
                    gn_pair(i)
                else:
                    act(E1.t[:], SIG.t[:], AF.Exp, [SIG.r()], [E1.r()], scale=-LAM)
                    bk, rb = nb()
                    for q, src in enumerate((ZSR, ZSK, ZSV, E1)):
                        tr(bk[0:NS, q * 128:(q + 1) * 128], src.t[:], 128, [src.r()], rb)
                    cp(Q6.t[0:NS, 0:4, i * 128:(i + 1) * 128], bk[0:NS, :].rearrange("p (a b) -> p a b", a=4), [rb], [Q6.r()], "act")
                    bk, rb = nb()
                    for q, src in enumerate((KKN, AA)):
                        tr(bk[0:NS, q * 128:(q + 1) * 128], src.t[:], 128, [src.r()], rb)
                    cp(Q6.t[0:NS, 4:6, i * 128:(i + 1) * 128], bk[0:NS, 0:256].rearrange("p (a b) -> p a b", a=2), [rb], [Q6.r()], "act")

            if sample:
                STG = av("stg", [128, 2304])
                for q0 in range(0, 18, 4):
                    bk, rb = nb()
                    n = min(4, 18 - q0)
                    for q in range(n):
                        tr(bk[0:NS, q * 128:(q + 1) * 128], ZB.t[:, q0 + q, H0:H0 + NS], 128, [ZB.r(q0 + q)], rb)
                    cp(STG.t[0:NS, q0 * 128:(q0 + n) * 128], bk[0:NS, 0:n * 128], [rb], [STG.r()], ev_eng())
                P.dma("sp", O["pools"][:, 14, :], STG.t[0:NS, 0:512], reads=[STG.r()], owner=STG.r(), final=True)
                P.dma("sp", O["shifts"], STG.t[0:NS, 512:2304], reads=[STG.r()], owner=STG.r(), final=True)
                P.barrier(pool=True)
                arelease(AMARK)
                SST = av("sst", [128, 64, 64]); ST1 = av("st1", [128, 64, 64]); ST2 = av("st2", [128, 64, 64])
                S1 = P.res("scr1")
                P.dma("sp", scr1.rearrange("q b f -> b q f"), Q6.t[0:NS, :, :], reads=[Q6.r()], writes=[S1], owner=S1)
                P.dma("sp", QR.t[:], scr1.rearrange("q b (h k) -> (b h) q k", k=64), reads=[S1], writes=[QR.r()], owner=QR.r())
                P.dma("sp", SST.t[:].rearrange("p a b -> p (a b)"), I["swkv"], writes=[SST.r()], owner=SST.r())
                bc_k = lambda q: QR.t[:, q, :].unsqueeze(1).to_broadcast([128, 64, 64])
                bc_v = lambda ap: ap.unsqueeze(2).to_broadcast([128, 64, 64])
                tt(ST1.t[:], SST.t[:], bc_k(4), ALU.mult, [SST.r(), QR.r()], [ST1.r()])
                P.op("dve", lambda e: e.tensor_reduce(out=SAV.t[:], in_=ST1.t[:], axis=AX.X, op=ALU.add, negate=True), reads=[ST1.r()], writes=[SAV.r()])
                tt(ST2.t[:], SST.t[:], bc_k(3), ALU.mult, [SST.r(), QR.r()], [ST2.r()])
                tt(ST1.t[:], bc_v(SAV.t[:]), bc_k(5), ALU.mult, [SAV.r(), QR.r()], [ST1.r()])
                tt(ST2.t[:], ST2.t[:], ST1.t[:], ALU.add, [ST2.r(), ST1.r()], [ST2.r()])
                tt(ST1.t[:], bc_v(QR.t[:, 2, :]), bc_k(1), ALU.mult, [QR.r()], [ST1.r()])
                tt(ST2.t[:], ST2.t[:], ST1.t[:], ALU.add, [ST2.r(), ST1.r()], [ST2.r()])
                P.dma("sp", O["wkvs"], ST2.t[:].rearrange("p a b -> p (a b)"), reads=[ST2.r()], owner=ST2.r(), final=True)
                tt(ST1.t[:], ST2.t[:], bc_k(0), ALU.mult, [ST2.r(), QR.r()], [ST1.r()])
                P.op("dve", lambda e: e.tensor_reduce(out=YV.t[:], in_=ST1.t[:], axis=AX.X, op=ALU.add), reads=[ST1.r()], writes=[YV.r()])
                S2 = P.res("scr2")
                P.dma("sp", scr2.rearrange("b (h k) -> (b h) k", k=64), YV.t[:], reads=[YV.r()], writes=[S2], owner=S2)
                P.dma("sp", YTS.t[0:NS, :], scr2, reads=[S2], writes=[YTS.r()], owner=YTS.r())
                for i in range(4):
                    bk, rb = nb()
                    tr(bk[:, 0:NS], YTS.t[0:NS, i * 128:(i + 1) * 128], NS, [YTS.r()], rb)
                    cp(YTP.t[:], bk[:, 0:NS], [rb], [YTP.r()], "dve")
                    prep_pair(i, True)
                    gn_pair(i)

            proj_down(Tn, I["w_out_ab"], 8, lambda k: (YC.t[:, k, :], YC.r(k)))

            if not sample:
                cp(HIS.t[:], ZB.t[:, :, H0 + Tn - 16:H0 + Tn], ZB.res, [HIS.r()], "dve")
                if blk == 3:
                    STP = T(SIG.t[:], SIG.res); STS = T(CSUM.t[:, 0:128], CSUM.res); WK = T(AA.t[:].rearrange("p (a b) -> p a b", a=4), AA.res)
                    bk, rb = nb()
                    for g in range(4):
                        tr(bk[0:15, g * 128:(g + 1) * 128], ZB.t[:, g, H0 + Tn - 15:H0 + Tn], 128, [ZB.r(g)], rb)
                    cp(STP.t[0:15, :], bk[0:15, :], [rb], [STP.r()], "dve")
                    P.dma("sp", O["poolp"], STP.t[0:15, :], reads=[STP.r()], owner=STP.r(), final=True)
                    bk, rb = nb()
                    tr(bk[0:14, 0:128], ZB.t[:, 4:18, H0 + Tn - 1], 128, ZB.res, rb)
                    cp(STS.t[0:14, :], bk[0:14, 0:128], [rb], [STS.r()], "dve")
                    P.dma("sp", O["shiftp"], STS.t[0:14, :], reads=[STS.r()], owner=STS.r(), final=True)
                    for i in range(4):
                        bk, rb = nb()
                        tr(bk[:, 0:128], SBLK.t[:, i, :].bitcast(F32), 128, [SBLK.r(i)], rb)
                        cp(WK.t[:, i, :], bk[:, 0:128], [rb], [WK.r()], ev_eng())
                    wv = O["wkvp"].rearrange("(i j) v k -> j v i k", j=2)
                    for j in range(2):
                        js = slice(j * 64, (j + 1) * 64)
                        P.dma("sp", wv[j], WK.t[js, :, js], reads=[WK.r()], owner=WK.r(), final=True)

        def mixer_c(blk, sample, Tn, nch, CH):
            UU = av("uT", [128, 16, Tn], BF16, 16)
            VT = [av("vtok%d" % a, [128, 2048], F32R) for a in range(nch)]
            TMPS = [av("sptmp%d" % k, [128, Tn]) for k in range(2)]
            if not sample:
                CC = av("cc", [128, 16, 128]); BSB = av("bsb", [128, 512]); ONF = av("onf", [128, 128], F32R)
                P.dma("sp", BSB.t[:], I["bsb"], writes=[BSB.r()], owner=BSB.r())
                fill_r(ONF.t[:], 1.0, [ONF.r()])
                for g in range(4):
                    bk, rb = nb()
                    mm(bk[:, 0:128], ONF.t[:], WMT.t[:, g, :], True, True, [ONF.r(), WMT.r()], rb)
                    RS = av("rs%d" % g, [128, 128])
                    cp(RS.t[:], bk[:, 0:128], [rb], [RS.r()])
                    for q in range(4):
                        dcq = g * 4 + q
                        stt(CC.t[:, dcq, :], RS.t[:], prm("ln_b", dcq), BSB.t[:, g * 128:(g + 1) * 128], ALU.mult, ALU.add,
                            [RS.r(), PRM.r(), BSB.r()], [CC.r()])
            for cgp in range(4):
                s = wload(I["w_in_c"][:, 2048 + cgp * 512:2048 + (cgp + 1) * 512], 512)
                for a in range(nch):
                    bk, rb = nb()
                    for kc in range(8):
                        mm(bk[0:CH, :], CUR["HT"].t[:, kc, a * CH:(a + 1) * CH], s.t[:, kc, :], kc == 0, kc == 7, [s.r(), CUR["HT"].r(kc)], rb)
                    act(VT[a].t[0:CH, cgp * 512:(cgp + 1) * 512], bk[0:CH, :], AF.Gelu, [rb], [VT[a].r()])
            for a in range(nch):
                ST6 = av("st6_%d" % a, [128, 24]); MV = av("mvv%d" % a, [128, 2]); RSD = av("rsd%d" % a, [128, 1])
                vf = VT[a].t[0:CH, :].bitcast(F32)
                for q in range(4):
                    P.op("dve", lambda e, q=q, vf=vf, ST6=ST6: e.bn_stats(out=ST6.t[0:CH, q * 6:(q + 1) * 6], in_=vf[:, q * 512:(q + 1) * 512]),
                         reads=[VT[a].r()], writes=[ST6.r()])
                P.op("dve", lambda e, ST6=ST6, MV=MV: e.bn_aggr(out=MV.t[0:CH, :], in_=ST6.t[0:CH, :]), reads=[ST6.r()], writes=[MV.r()])
                act(RSD.t[0:CH, :], MV.t[0:CH, 1:2], AF.Ln, [MV.r()], [RSD.r()], bias=LN_EPS)
                act(RSD.t[0:CH, :], RSD.t[0:CH, :], AF.Exp, [RSD.r()], [RSD.r()], scale=-0.5)
                ts(VT[a].t[0:CH, :], vf, MV.t[0:CH, 0:1], RSD.t[0:CH, 0:1], ALU.subtract, ALU.mult, [VT[a].r(), MV.r(), RSD.r()], [VT[a].r()])
            if sample:
                LGB = av("lgb", [NS, 2048]); LBB = av("lbb", [NS, 2048]); SV = av("sv", [NS, 2048])
                P.dma("sp", LGB.t[:], I["lngb"], writes=[LGB.r()], owner=LGB.r())
                P.dma("sp", LBB.t[:], I["lnbb"], writes=[LBB.r()], owner=LBB.r())
                tt(SV.t[:], VT[0].t[0:NS, :].bitcast(F32), LGB.t[:], ALU.mult, [VT[0].r(), LGB.r()], [SV.r()])
                tt(SV.t[:], SV.t[:], LBB.t[:], ALU.add, [SV.r(), LBB.r()], [SV.r()])
                P.dma("sp", O["sguv"], SV.t[:], reads=[SV.r()], owner=SV.r(), final=True)

            def ev_u(m, bk, rb):
                act(UU.t[:, m, :], bk, AF.Gelu, [rb], [UU.r(m)])
            proj_fm(Tn, I["w_in_c"][:, 0:2048], 8, 2048, hrhs(Tn), ev_u)
            for dc in range(16):
                g = dc // 4
                bk, rb = nb()
                if sample:
                    mm(bk[:, 0:NS], VT[0].t[0:NS, dc * 128:(dc + 1) * 128], W00I.t[:, g, :], True, True, [VT[0].r(), W00I.r()], rb)
                else:
                    for a in range(nch):
                        mm(bk[:, a * 128:(a + 1) * 128], VT[a].t[:, dc * 128:(dc + 1) * 128], WMT.t[:, g, :], True, True, [VT[a].r(), WMT.r()], rb)
                t = TMPS[dc % 2]
                if sample:
                    ts(t.t[:], bk[:, 0:NS], prm("ln_g", dc), CS16.t[:, dc:dc + 1], ALU.mult, ALU.add, [rb, PRM.r(), CS16.r()], [t.r()])
                else:
                    stt(t.t[:].rearrange("p (a i) -> p a i", a=4), bk[:, :].rearrange("p (a i) -> p a i", a=4), prm("ln_g", dc),
                        CC.t[:, dc, :].unsqueeze(1).to_broadcast([128, 4, 128]), ALU.mult, ALU.add, [rb, PRM.r(), CC.r()], [t.r()])
                tt(UU.t[:, dc, :], t.t[:], UU.t[:, dc, :], ALU.mult, [t.r(), UU.r(dc)], [UU.r(dc)])
            proj_down(Tn, I["w_out_c"], 16, lambda k: (UU.t[:, k, :], UU.r(k)))

        def xattn_prompt(layer):
            QT = av("qT", [128, 8, 512], BF16, 8)

            def ev_q(m, bk, rb):
                cp(QT.t[:, m, :], bk, [rb], [QT.r(m)], ev_eng())
            proj_fm(512, I["w_xq"][layer], 8, 1024, hrhs(512), ev_q)
            PT = av("pT", [128, 4, 2, 512], BF16, 4)
            OT = av("oT", [128, 8, 512], BF16, 8)
            ASET = [dict(PF=av("pf%d" % k_, [128, 4, 256]), MX=av("mx%d" % k_, [128, 4]), NMX=av("nmx%d" % k_, [128, 4]),
                         SSUM=av("ssum%d" % k_, [128, 4]), RSUM=av("rsum%d" % k_, [128, 4])) for k_ in range(2)]

            def att_scores(c, S):
                banks = []
                for hp in range(2):
                    bk, rb = nb()
                    banks.append((bk, rb))
                    for hh in range(2):
                        hd = hp * 2 + hh
                        for dd in range(2):
                            mm(bk[:, hh * 256:(hh + 1) * 256], QT.t[:, 2 * hd + dd, c * 128:(c + 1) * 128], KT[layer].t[:, 2 * hd + dd, :],
                               dd == 0, dd == 1, [QT.r(2 * hd + dd), KT[layer].r()], rb)
                S["banks"] = banks

            def att_softmax(c, S):
                PF, MX, NMX, SSUM, RSUM = S["PF"], S["MX"], S["NMX"], S["SSUM"], S["RSUM"]
                for hp in range(2):
                    bk, rb = S["banks"][hp]
                    P.op("dve", lambda e, bk=bk, hp=hp, MX=MX: e.tensor_reduce(out=MX.t[:, hp * 2:hp * 2 + 2], in_=bk[:, :].rearrange("p (h m) -> p h m", h=2),
                                                                              axis=AX.X, op=ALU.max), reads=[rb], writes=[MX.r()])
                ts(NMX.t[:], MX.t[:], -1.0 / 16.0, None, ALU.mult, None, [MX.r()], [NMX.r()])
                for hd in range(4):
                    bk, rb = S["banks"][hd // 2]
                    act(PF.t[:, hd, :], bk[:, (hd % 2) * 256:(hd % 2 + 1) * 256], AF.Exp, [rb, NMX.r()], [PF.r(), SSUM.r()],
                        bias=NMX.t[:, hd:hd + 1], scale=1.0 / 16.0, accum=SSUM.t[:, hd:hd + 1])
                P.op("dve", lambda e, RSUM=RSUM, SSUM=SSUM: e.reciprocal(out=RSUM.t[:], in_=SSUM.t[:]), reads=[SSUM.r()], writes=[RSUM.r()])
                tt(PF.t[:], PF.t[:], RSUM.t[:].unsqueeze(2).to_broadcast([128, 4, 256]), ALU.mult, [PF.r(), RSUM.r()], [PF.r()])

            def att_transposes(c, S):
                PF = S["PF"]
                for hp in range(2):
                    bk, rb = nb()
                    for hh in range(2):
                        hd = hp * 2 + hh
                        for mc in range(2):
                            k = hh * 2 + mc
                            tr(bk[:, k * 128:(k + 1) * 128], PF.t[:, hd, mc * 128:(mc + 1) * 128], 128, [PF.r()], rb)
                    cp(PT.t[:, hp * 2:hp * 2 + 2, :, c * 128:(c + 1) * 128], bk[:, :].rearrange("p (h m t) -> p h m t", h=2, m=2),
                       [rb], [PT.r(hp * 2), PT.r(hp * 2 + 1)], ev_eng())

            att_scores(0, ASET[0])
            for c in range(4):
                if c + 1 < 4:
                    att_scores(c + 1, ASET[(c + 1) % 2])
                att_softmax(c, ASET[c % 2])
                att_transposes(c, ASET[c % 2])
            for j in range(8):
                hd = j // 2
                bk, rb = nb()
                for mc in range(2):
                    mm(bk[:, :], VM[layer].t[:, mc, j * 128:(j + 1) * 128], PT.t[:, hd, mc, :], mc == 0, mc == 1, [VM[layer].r(), PT.r(hd)], rb)
                cp(OT.t[:, j, :], bk[:, :], [rb], [OT.r(j)], ev_eng())
            proj_down(512, I["w_xo"][layer], 8, lambda k: (OT.t[:, k, :], OT.r(k)))

        def xattn_sample(layer):
            QTOK = av("qtok", [NS, 1024], BF16)
            SELA = av("sela", [16, 16, 128], BF16)
            DEL = av("del", [128, 16, 16])
            cp(DEL.t[:], ONES.t[:, 0:16].unsqueeze(2).to_broadcast([128, 16, 16]), [ONES.r()], [DEL.r()])
            P.op("pool", lambda e: e.affine_select(out=DEL.t[:], in_=DEL.t[:], pattern=[[1, 16], [-1, 16]], compare_op=ALU.is_equal,
                                                    fill=0.0, base=0, channel_multiplier=0), reads=[DEL.r()], writes=[DEL.r()])
            cp(SELA.t[:], IDN.t[0:16, 0:16].unsqueeze(2).to_broadcast([16, 16, 128]), [IDN.r()], [SELA.r()])
            for cg in range(2):
                s = wload(I["w_xq"][layer][:, cg * 512:(cg + 1) * 512], 512)
                bk, rb = nb()
                for kc in range(8):
                    mm(bk[0:NS, :], CUR["HT"].t[:, kc, 0:NS], s.t[:, kc, :], kc == 0, kc == 7, [s.r(), CUR["HT"].r(kc)], rb)
                cp(QTOK.t[:, cg * 512:(cg + 1) * 512], bk[0:NS, :], [rb], [QTOK.r()], ev_eng())
            KB_ = [av("kb%d" % k, [128, 2, 1024]) for k in range(3)]
            VB_ = [av("vb%d" % k, [128, 2, 1024], BF16) for k in range(4)]
            PRD = av("prd", [128, 1024])
            SC = av("sc", [128, 2, 64])
            for b in range(NS):
                kb = KB_[b % 3]
                P.dma("sp", kb.t[:], I["ck"][layer, b].rearrange("(a p) f -> p a f", p=128), writes=[kb.r()], owner=kb.r())
                bq = [nb(), nb()]
                for cg in range(2):
                    mm(bq[cg][0][:, :], SELA.t[:, b, :], QTOK.t[:, cg * 512:(cg + 1) * 512], True, True, [SELA.r(), QTOK.r()], bq[cg][1])
                for mc in range(2):
                    for cg in range(2):
                        tt(PRD.t[:, cg * 512:(cg + 1) * 512], kb.t[:, mc, cg * 512:(cg + 1) * 512], bq[cg][0][:, :], ALU.mult,
                           [kb.r(), bq[cg][1]], [PRD.r()])
                    P.op("dve", lambda e, mc=mc, b=b: e.tensor_reduce(out=SC.t[:, mc, b * 4:(b + 1) * 4], in_=PRD.t[:].rearrange("p (h d) -> p h d", h=4),
                                                                      axis=AX.X, op=ALU.add), reads=[PRD.r()], writes=[SC.r()])
            MX = av("mxs", [128, 1]); NMX = av("nmxs", [128, 1]); SSUM = av("ssums", [128, 1]); RSUM = av("rsums", [128, 1])
            PSM = av("psm", [128, 256]); PTS = av("pts", [128, 2, 64])
            PEX = av("pex", [128, 2, 16, 4, 16], BF16)
            bk, rb = nb()
            for mc in range(2):
                tr(bk[0:64, mc * 128:(mc + 1) * 128], SC.t[:, mc, :], 128, [SC.r()], rb)
            P.op("dve", lambda e: e.tensor_reduce(out=MX.t[0:64, :], in_=bk[0:64, 0:256], axis=AX.X, op=ALU.max), reads=[rb], writes=[MX.r()])
            ts(NMX.t[0:64, :], MX.t[0:64, :], -1.0 / 16.0, None, ALU.mult, None, [MX.r()], [NMX.r()])
            act(PSM.t[0:64, :], bk[0:64, 0:256], AF.Exp, [rb, NMX.r()], [PSM.r(), SSUM.r()], bias=NMX.t[0:64, 0:1], scale=1.0 / 16.0,
                accum=SSUM.t[0:64, 0:1])
            P.op("dve", lambda e: e.reciprocal(out=RSUM.t[0:64, :], in_=SSUM.t[0:64, :]), reads=[SSUM.r()], writes=[RSUM.r()])
            ts(PSM.t[0:64, :], PSM.t[0:64, :], RSUM.t[0:64, 0:1], None, ALU.mult, None, [PSM.r(), RSUM.r()], [PSM.r()])
            bk, rb = nb()
            for mc in range(2):
                tr(bk[:, mc * 64:(mc + 1) * 64], PSM.t[0:64, mc * 128:(mc + 1) * 128], 64, [PSM.r()], rb)
            cp(PTS.t[:].rearrange("p a b -> p (a b)"), bk[:, 0:128], [rb], [PTS.r()], "dve")
            for mc in range(2):
                tt(PEX.t[:, mc], PTS.t[:, mc, :].rearrange("p (b h) -> p b h", h=4).unsqueeze(3).to_broadcast([128, 16, 4, 16]),
                   DEL.t[:].unsqueeze(2).to_broadcast([128, 16, 4, 16]), ALU.mult, [PTS.r(), DEL.r()], [PEX.r()])
            bo = [nb() for _ in range(4)]
            for b in range(NS):
                vb = VB_[b % 4]
                P.dma("pool", vb.t[:], I["cv"][layer, b].rearrange("(a p) f -> p a f", p=128), writes=[vb.r()], owner=vb.r())
                for h in range(4):
                    for mc in range(2):
                        mm(bo[h][0][0:NS, 0:256], PEX.t[:, mc, b, h, :], vb.t[:, mc, h * 256:(h + 1) * 256], b == 0 and mc == 0,
                           b == NS - 1 and mc == 1, [PEX.r(), vb.r()], bo[h][1])
            OTOK = av("otok", [NS, 1024])
            for h in range(4):
                cp(OTOK.t[:, h * 256:(h + 1) * 256], bo[h][0][0:NS, 0:256], [bo[h][1]], [OTOK.r()], ev_eng())
            OTS = av("oTs", [128, 8, NS], BF16, 8)
            bk, rb = nb()
            for j in range(8):
                tr(bk[:, j * NS:(j + 1) * NS], OTOK.t[:, j * 128:(j + 1) * 128], NS, [OTOK.r()], rb)
            cp(OTS.t[:], bk[:, 0:8 * NS].rearrange("p (j b) -> p j b", j=8), [rb], OTS.res, "dve")
            proj_down(NS, I["w_xo"][layer], 8, lambda k: (OTS.t[:, k, :], OTS.r(k)))

        try:
            stage("memkv")
            for blk in range(3):
                run_block(blk, False)
                stage("b%d" % blk)
            run_joint()
        except StopBuild:
            pass
        P.finish("sp")
        P.emit()
    return nc


def _fm(v):
    v = np.asarray(v, np.float32).reshape(-1, 128)
    return np.ascontiguousarray(v.T)


_NC_CACHE = {}
STOP_AT = None


class StopBuild(Exception):
    pass


def stage(name):
    if STOP_AT is not None and name == STOP_AT:
        raise StopBuild()
_PREP_ONLY = False


def kernel(x_prompt, x_sample, mem_prompt, cache_mem_k, cache_mem_v, state_pool, state_shift, state_wkv,
           norm_mix_g, norm_xa_g, norm_mem_g, norm_ffn_g, norm_final_g,
           w_in_ab, w_out_ab, pool_w, pool_scale,
           rwkv_mu, rwkv_w0, rwkv_w2, rwkv_a0, rwkv_a2, rwkv_g2, rwkv_k_k, rwkv_k_a, rwkv_r_k,
           rwkv_gn_g, rwkv_gn_b,
           w_in_c, sgu_ln_g, sgu_ln_b, sgu_w_s, sgu_b_s, w_out_c,
           w_xq, w_xk, w_xv, w_xo, w_ff_up, w_ff_down):
    f = lambda a: np.ascontiguousarray(np.asarray(a, dtype=np.float32))
    cols = [_fm(norm_mix_g[0]), _fm(norm_xa_g[0]), _fm(norm_ffn_g[0]), _fm(norm_mix_g[1]), _fm(norm_xa_g[1]), _fm(norm_ffn_g[1]),
            _fm(norm_mem_g[0]), _fm(norm_mem_g[1]), _fm(norm_final_g), _fm(pool_scale[0]), _fm(rwkv_mu[0]), _fm(rwkv_w0[0]),
            _fm(rwkv_a0[0]), _fm(rwkv_k_k[0]), _fm(rwkv_k_a[0]), _fm(np.asarray(rwkv_r_k[0]).reshape(-1)), _fm(rwkv_gn_g[0]),
            _fm(rwkv_gn_b[0]), _fm(sgu_ln_g[0]), _fm(sgu_ln_b[0])]
    prm = np.ascontiguousarray(np.concatenate(cols, axis=1))
    assert prm.shape == (128, NPRM)
    ws = np.asarray(sgu_w_s[0], np.float32)
    bs = np.asarray(sgu_b_s[0], np.float32)
    shared = dict(
        prm=prm,
        bsb=np.ascontiguousarray(np.broadcast_to(bs.reshape(1, 512), (128, 512))),
        smp=np.ascontiguousarray(np.broadcast_to(np.concatenate([ws[:, 0, 0], bs[:, 0]]).reshape(1, 8), (128, 8))),
        lngb=np.ascontiguousarray(np.broadcast_to(np.asarray(sgu_ln_g[0], np.float32).reshape(1, 2048), (NS, 2048))),
        lnbb=np.ascontiguousarray(np.broadcast_to(np.asarray(sgu_ln_b[0], np.float32).reshape(1, 2048), (NS, 2048))),
        w_in_ab=f(w_in_ab[0]), w_out_ab=f(w_out_ab[0]),
        pool_w=np.ascontiguousarray(np.transpose(np.asarray(pool_w[0], np.float32), (1, 0, 2))),
        w2=f(rwkv_w2[0]), a2=f(rwkv_a2[0]), g2=f(rwkv_g2[0]),
        w_in_c=f(w_in_c[0]), wsT=np.ascontiguousarray(np.transpose(ws, (2, 0, 1))),
        w_out_c=f(w_out_c[0]),
        w_xq=f(w_xq), w_xk=f(w_xk), w_xv=f(w_xv), w_xo=f(w_xo), w_ff_up=f(w_ff_up), w_ff_down=f(w_ff_down),
    )
    xp = f(x_prompt); xs = f(x_sample); mp = f(mem_prompt); ck = f(cache_mem_k); cv = f(cache_mem_v)
    sp = f(state_pool); ssh = f(state_shift); swk = f(state_wkv)
    in_maps = []
    for c in range(NCORES):
        sl = slice(c * NS, (c + 1) * NS)
        m = dict(shared)
        m.update(
            xp=xp[c], xs=np.ascontiguousarray(xs[sl, 0, :]), memp=mp[c],
            ck=np.ascontiguousarray(ck[:, sl].reshape(2, NS, 256, D)), cv=np.ascontiguousarray(cv[:, sl].reshape(2, NS, 256, D)),
            spool=np.ascontiguousarray(sp[0, sl]), sshift=np.ascontiguousarray(ssh[0, sl]),
            swkv=np.ascontiguousarray(swk[0, sl].reshape(128, 4096)),
        )
        in_maps.append(m)
    if _PREP_ONLY:
        return in_maps
    if "nc" not in _NC_CACHE:
        _NC_CACHE["nc"] = build_nc()
    nc = _NC_CACHE["nc"]
    res = run_bass_kernel_spmd(nc, in_maps, core_ids=list(range(NCORES))).results
    return _gather(res)


def _gather(res):
    B = NCORES
    y_prompt = np.stack([res[c]["yp"] for c in range(B)]).astype(np.float32)
    y_sample = np.concatenate([res[c]["ys"] for c in range(B)]).reshape(128, 1, D).astype(np.float32)
    mem_k = np.stack([res[c]["mk"] for c in range(B)], axis=1).reshape(2, B, 256, 4, 256).astype(np.float32)
    mem_v = np.stack([res[c]["mv"] for c in range(B)], axis=1).reshape(2, B, 256, 4, 256).astype(np.float32)
    pool_p = np.stack([res[c]["poolp"] for c in range(B)])[None].astype(np.float32)
    pool_s = np.concatenate([res[c]["pools"] for c in range(B)])[None].astype(np.float32)
    shift_p = np.stack([res[c]["shiftp"].reshape(1792) for c in range(B)])[None].astype(np.float32)
    shift_s = np.concatenate([res[c]["shifts"] for c in range(B)])[None].astype(np.float32)
    wkv_p = np.stack([res[c]["wkvp"] for c in range(B)])[None].astype(np.float32)
    wkv_s = np.concatenate([res[c]["wkvs"].reshape(NS, 8, 64, 64) for c in range(B)])[None].astype(np.float32)
    sgu_v = np.concatenate([res[c]["sguv"] for c in range(B)]).reshape(1, 128, 1, 2048).astype(np.float32)
    return (y_prompt, y_sample, mem_k, mem_v, pool_p, pool_s, shift_p, shift_s, wkv_p, wkv_s, sgu_v)
```
